# Optimizing a Trainium2 kernel written in Bass

```python
import math
import jax, jax.numpy as jnp
from jax import lax
import numpy as np

D_MODEL = 2048
BATCH = 2
SEQ = 16384
DEPTH = 1
DEC_BATCH = 16
DEC_SEQ = 32
PAST_LEN = 2048

CHUNK = 64
Q_BLOCK = 128
HA = 8
HD_A = 64
VD_A = 2 * HD_A
HB = 8
HD_B = 128
D_FF = 4 * D_MODEL
ROPE_THETA = 10000.0
EPS = 1e-6
NEG = -1e30

QA_W = 2 * HA * HD_A
KA_W = 2 * HA * HD_A
VA_W = HA * VD_A
QB_W = HB * HD_B
KB_W = HB * HD_B
VB_W = HB * HD_B
IN_W = QA_W + KA_W + VA_W + QB_W + KB_W + VB_W
SPLITS = (QA_W, QA_W + KA_W, QA_W + KA_W + VA_W, QA_W + KA_W + VA_W + QB_W,
          QA_W + KA_W + VA_W + QB_W + KB_W)

kernel_name = 'hybrid_diffattn_stickbreak_stream_step'


def rmsnorm(x, g):
    xf = x.astype(jnp.float32)
    y = xf * lax.rsqrt(jnp.mean(xf * xf, axis=-1, keepdims=True) + EPS)
    return (y * g.astype(jnp.float32)).astype(x.dtype)


def rope(x, pos):
    half = x.shape[-1] // 2
    inv = ROPE_THETA ** (-jnp.arange(half, dtype=jnp.float32) / half)
    ang = pos.astype(jnp.float32)[:, None] * inv[None, :]
    cos = jnp.cos(ang)[None, :, None, :]
    sin = jnp.sin(ang)[None, :, None, :]
    xf = x.astype(jnp.float32)
    x1, x2 = xf[..., :half], xf[..., half:]
    return jnp.concatenate([x1 * cos - x2 * sin, x2 * cos + x1 * sin], axis=-1).astype(x.dtype)


def project(h, pos, w_in):
    b, t, _ = h.shape
    z = h @ w_in
    qa, ka, va, qb, kb, vb = jnp.split(z, SPLITS, axis=-1)
    qa = rope(qa.reshape(b, t, 2 * HA, HD_A), pos)
    ka = rope(ka.reshape(b, t, 2 * HA, HD_A), pos)
    va = va.reshape(b, t, HA, VD_A)
    qb = qb.reshape(b, t, HB, HD_B)
    kb = kb.reshape(b, t, HB, HD_B)
    vb = vb.reshape(b, t, HB, HD_B)
    return qa, ka, va, qb, kb, vb


def diff_attend(qa, ka, va, q_pos, k_pos, lam, g_head, lam_init):
    b, tq = qa.shape[0], qa.shape[1]
    s = jnp.einsum('bqhd,bkhd->bhqk', qa, ka, preferred_element_type=jnp.float32) * (HD_A ** -0.5)
    limit = (q_pos // CHUNK + 1) * CHUNK
    mask = k_pos[None, :] < limit[:, None]
    p = jax.nn.softmax(jnp.where(mask, s, NEG), axis=-1)
    w = p[:, :HA] - lam * p[:, HA:]
    o = jnp.einsum('bhqk,bkhe->bqhe', w, va, preferred_element_type=jnp.float32)
    o = o * lax.rsqrt(jnp.mean(o * o, axis=-1, keepdims=True) + EPS)
    o = o * g_head.astype(jnp.float32) * (1.0 - lam_init)
    return o.reshape(b, tq, HA * VD_A).astype(qa.dtype)


def sb_attend(qb, kb, vb, q_pos, k_pos):
    b, tq = qb.shape[0], qb.shape[1]
    z = jnp.einsum('bqhd,bkhd->bhqk', qb, kb, preferred_element_type=jnp.float32) * (HD_B ** -0.5)
    mask = k_pos[None, :] < q_pos[:, None]
    log_keep = jnp.where(mask, jax.nn.log_sigmoid(-z), 0.0)
    suffix = lax.cumsum(log_keep, axis=3, reverse=True) - log_keep
    a = jnp.where(mask, jnp.exp(jax.nn.log_sigmoid(z) + suffix), 0.0)
    o = jnp.einsum('bhqk,bkhe->bqhe', a, vb, preferred_element_type=jnp.float32)
    return o.reshape(b, tq, HB * HD_B).astype(qb.dtype)


def merge_and_mlp(x, h, oa, ob, w_gate, w_proj_a, w_proj_b, w_out, g_post_mix,
                  g_pre_mlp, w_up, w_down, g_post_mlp):
    g = jax.nn.sigmoid((h @ w_gate).astype(jnp.float32))
    ya = (oa @ w_proj_a).astype(jnp.float32)
    yb = (ob @ w_proj_b).astype(jnp.float32)
    merged = (g[..., :D_MODEL] * ya + g[..., D_MODEL:] * yb).astype(x.dtype)
    x = x + rmsnorm(merged @ w_out, g_post_mix)
    hm = rmsnorm(x, g_pre_mlp)
    u = jnp.square(jax.nn.relu(hm @ w_up))
    return x + rmsnorm(u @ w_down, g_post_mlp)


def setup_inputs(seed: int = 0) -> dict:
    key = jax.random.key(seed)
    ks = jax.random.split(key, 32)
    f32 = jnp.float32
    nrm = lambda k, shape, scale: jax.random.normal(k, shape, f32) * scale
    gain = lambda k, n: 1.0 + 0.05 * jax.random.normal(k, (DEPTH, n), f32)
    return {
        'x_prompt': nrm(ks[0], (BATCH, SEQ, D_MODEL), 1.0),
        'x_sample': nrm(ks[1], (DEC_BATCH, DEC_SEQ, D_MODEL), 1.0),
        'cache_diff_k': nrm(ks[2], (DEPTH, DEC_BATCH, PAST_LEN, 2 * HA, HD_A), 1.0),
        'cache_diff_v': nrm(ks[3], (DEPTH, DEC_BATCH, PAST_LEN, HA, VD_A), 1.0),
        'cache_sb_k': nrm(ks[4], (DEPTH, DEC_BATCH, PAST_LEN, HB, HD_B), 1.0),
        'cache_sb_v': nrm(ks[5], (DEPTH, DEC_BATCH, PAST_LEN, HB, HD_B), 1.0),
        'g_pre_mix': gain(ks[6], D_MODEL),
        'w_in': nrm(ks[7], (DEPTH, D_MODEL, IN_W), D_MODEL ** -0.5),
        'lambda_q1': nrm(ks[8], (DEPTH, HD_A), 0.1),
        'lambda_k1': nrm(ks[9], (DEPTH, HD_A), 0.1),
        'lambda_q2': nrm(ks[10], (DEPTH, HD_A), 0.1),
        'lambda_k2': nrm(ks[11], (DEPTH, HD_A), 0.1),
        'g_diff_head': gain(ks[12], VD_A),
        'w_gate': nrm(ks[13], (DEPTH, D_MODEL, 2 * D_MODEL), D_MODEL ** -0.5),
        'w_proj_a': nrm(ks[14], (DEPTH, VA_W, D_MODEL), VA_W ** -0.5),
        'w_proj_b': nrm(ks[15], (DEPTH, VB_W, D_MODEL), VB_W ** -0.5),
        'w_out': nrm(ks[16], (DEPTH, D_MODEL, D_MODEL), D_MODEL ** -0.5),
        'g_post_mix': gain(ks[17], D_MODEL),
        'g_pre_mlp': gain(ks[18], D_MODEL),
        'w_up': nrm(ks[19], (DEPTH, D_MODEL, D_FF), D_MODEL ** -0.5),
        'w_down': nrm(ks[20], (DEPTH, D_FF, D_MODEL), D_FF ** -0.5),
        'g_post_mlp': gain(ks[21], D_MODEL),
    }


def reference(x_prompt, x_sample, cache_diff_k, cache_diff_v, cache_sb_k, cache_sb_v,
              g_pre_mix, w_in, lambda_q1, lambda_k1, lambda_q2, lambda_k2, g_diff_head,
              w_gate, w_proj_a, w_proj_b, w_out, g_post_mix, g_pre_mlp, w_up, w_down,
              g_post_mlp):
    bp, tp, _ = x_prompt.shape
    bs, ts, _ = x_sample.shape
    past = cache_diff_k.shape[2]
    nblk = tp // Q_BLOCK
    pos_p = jnp.arange(tp, dtype=jnp.int32)
    pos_s = past + jnp.arange(ts, dtype=jnp.int32)
    kpos_s = jnp.arange(past + ts, dtype=jnp.int32)

    xp, xs = x_prompt, x_sample
    dk_p, dv_p, sk_p, sv_p = [], [], [], []
    dk_s, dv_s, sk_s, sv_s = [], [], [], []
    for l in range(DEPTH):
        lam_init = 0.8 - 0.6 * math.exp(-0.3 * l)
        lam = (jnp.exp(jnp.sum(lambda_q1[l].astype(jnp.float32) * lambda_k1[l].astype(jnp.float32)))
               - jnp.exp(jnp.sum(lambda_q2[l].astype(jnp.float32) * lambda_k2[l].astype(jnp.float32)))
               + lam_init)

        hp = rmsnorm(xp, g_pre_mix[l])
        qa, ka, va, qb, kb, vb = project(hp, pos_p, w_in[l])
        qa_blk = qa.reshape(bp, nblk, Q_BLOCK, 2 * HA, HD_A).swapaxes(0, 1)
        qb_blk = qb.reshape(bp, nblk, Q_BLOCK, HB, HD_B).swapaxes(0, 1)
        qpos_blk = pos_p.reshape(nblk, Q_BLOCK)

        def block(args, ka=ka, va=va, kb=kb, vb=vb, lam=lam, l=l, lam_init=lam_init):
            qa_b, qb_b, qp = args
            return (diff_attend(qa_b, ka, va, qp, pos_p, lam, g_diff_head[l], lam_init),
                    sb_attend(qb_b, kb, vb, qp, pos_p))

        oa, ob = lax.map(block, (qa_blk, qb_blk, qpos_blk))
        oa = oa.swapaxes(0, 1).reshape(bp, tp, VA_W)
        ob = ob.swapaxes(0, 1).reshape(bp, tp, VB_W)
        xp = merge_and_mlp(xp, hp, oa, ob, w_gate[l], w_proj_a[l], w_proj_b[l], w_out[l],
                           g_post_mix[l], g_pre_mlp[l], w_up[l], w_down[l], g_post_mlp[l])
        dk_p.append(ka); dv_p.append(va); sk_p.append(kb); sv_p.append(vb)

        hs = rmsnorm(xs, g_pre_mix[l])
        qa2, ka2, va2, qb2, kb2, vb2 = project(hs, pos_s, w_in[l])
        ka_all = jnp.concatenate([cache_diff_k[l], ka2], axis=1)
        va_all = jnp.concatenate([cache_diff_v[l], va2], axis=1)
        kb_all = jnp.concatenate([cache_sb_k[l], kb2], axis=1)
        vb_all = jnp.concatenate([cache_sb_v[l], vb2], axis=1)
        oa2 = diff_attend(qa2, ka_all, va_all, pos_s, kpos_s, lam, g_diff_head[l], lam_init)
        ob2 = sb_attend(qb2, kb_all, vb_all, pos_s, kpos_s)
        xs = merge_and_mlp(xs, hs, oa2, ob2, w_gate[l], w_proj_a[l], w_proj_b[l], w_out[l],
                           g_post_mix[l], g_pre_mlp[l], w_up[l], w_down[l], g_post_mlp[l])
        dk_s.append(ka2); dv_s.append(va2); sk_s.append(kb2); sv_s.append(vb2)

    new_diff_k_prompt = jnp.stack(dk_p)
    new_diff_v_prompt = jnp.stack(dv_p)
    new_sb_k_prompt = jnp.stack(sk_p)
    new_sb_v_prompt = jnp.stack(sv_p)
    new_diff_k_sample = jnp.stack(dk_s)
    new_diff_v_sample = jnp.stack(dv_s)
    new_sb_k_sample = jnp.stack(sk_s)
    new_sb_v_sample = jnp.stack(sv_s)
    return (xp, xs, new_diff_k_prompt, new_diff_v_prompt, new_sb_k_prompt, new_sb_v_prompt,
            new_diff_k_sample, new_diff_v_sample, new_sb_k_sample, new_sb_v_sample)
```

```python
import contextlib
import numpy as np
import ml_dtypes
import concourse.bass as bass
import concourse.mybir as mybir
from concourse.bass_utils import run_bass_kernel_spmd

F32 = mybir.dt.float32
BF16 = mybir.dt.bfloat16
AF = mybir.ActivationFunctionType
ALU = mybir.AluOpType
NPBF = ml_dtypes.bfloat16

D = 2048
KC = 16
EPS = 1e-6
THETA = 10000.0
LAM_INIT = 0.8 - 0.6 * 1.0


class Res:
    __slots__ = ("lw", "rd")

    def __init__(self):
        self.lw = None
        self.rd = {}


class Prog:
    ENGS = ("pe", "act", "dve", "pool", "sp")

    def __init__(self, nc, stack):
        self.nc = nc
        self.stack = stack
        self.ops = {e: [] for e in self.ENGS}
        self.cnt = {}
        self.seen = {e: {} for e in self.ENGS}
        self.sems = {}
        self.noself = {"pe"}

    def sem(self, k):
        if k not in self.sems:
            self.sems[k] = self.stack.enter_context(self.nc.semaphore("s_" + k))
        return self.sems[k]

    def _deps(self, eng, reads, writes):
        need = {}

        def add(k, v):
            if need.get(k, 0) < v:
                need[k] = v
        for r in reads:
            if r.lw is not None:
                add(*r.lw)
        for w in writes:
            if w.lw is not None:
                add(*w.lw)
            for k, v in w.rd.items():
                add(k, v)
        seen = self.seen[eng]
        out = []
        for k, v in need.items():
            if k == eng and eng in self.noself:
                continue
            if seen.get(k, 0) < v:
                seen[k] = v
                out.append((k, v))
        return out

    def _fin(self, key, val, reads, writes):
        for r in reads:
            if r.rd.get(key, 0) < val:
                r.rd[key] = val
        for w in writes:
            w.lw = (key, val)
            w.rd = {}

    def op(self, eng, fn, reads=(), writes=()):
        waits = self._deps(eng, reads, writes)
        self.cnt[eng] = self.cnt.get(eng, 0) + 1
        self.ops[eng].append((waits, fn, eng, 1))
        self._fin(eng, self.cnt[eng], reads, writes)

    def dma(self, q, chan, fn, reads=(), writes=()):
        waits = self._deps(q, reads, writes)
        self.cnt[chan] = self.cnt.get(chan, 0) + 16
        self.ops[q].append((waits, fn, chan, 16))
        self._fin(chan, self.cnt[chan], reads, writes)

    def wait_all(self, eng, keys):
        waits = []
        for k in keys:
            v = self.cnt.get(k, 0)
            if v and self.seen[eng].get(k, 0) < v:
                self.seen[eng][k] = v
                waits.append((k, v))
        self.ops[eng].append((waits, None, None, 0))

    def barrier(self):
        keys = list(self.cnt.keys())
        for e in self.ENGS:
            self.wait_all(e, keys)

    def emit(self):
        for k in list(self.cnt.keys()):
            self.sem(k)
        sems = self.sems
        ops = self.ops
        self.ops = {e: [] for e in self.ENGS}

        def run(name):
            def body(e):
                for waits, fn, key, inc in ops[name]:
                    for k, v in waits:
                        e.wait_ge(sems[k], v)
                    if fn is not None:
                        fn(e).then_inc(sems[key], inc)
            return body
        with self.nc.Block() as block:
            block.tensor(run("pe"))
            block.scalar(run("act"))
            block.vector(run("dve"))
            block.gpsimd(run("pool"))
            block.sync(run("sp"))


class Cfg:
    def __init__(self, T=16384, PAST=2048, ST=32, SBC=2):
        self.T = T
        self.TQ = 512
        self.NT = T // 512
        self.NJ = self.NT // 4
        self.TO = self.NJ * 512
        self.ST = ST
        self.SBC = SBC
        self.NS = ST * SBC
        self.NSP = 256
        self.TOT = self.TO + self.NSP
        self.PAST = PAST
        self.NPB = PAST // 128


def build(cfg):
    nc = bass.Bass("TRN2", target_bir_lowering=False)
    T, TO, TOT, NS, NJ, PAST, NPB, ST, SBC = (cfg.T, cfg.TO, cfg.TOT, cfg.NS, cfg.NJ, cfg.PAST,
                                              cfg.NPB, cfg.ST, cfg.SBC)

    def din(name, shape, dt=F32):
        return nc.dram_tensor(name, list(shape), dt, kind="ExternalInput").ap()

    def dout(name, shape, dt=F32):
        return nc.dram_tensor(name, list(shape), dt, kind="ExternalOutput").ap()

    def dscr(name, shape, dt):
        return nc.dram_tensor(name, list(shape), dt, kind="Internal").ap()

    xT_b = din("xT_b", [128, KC, T])
    xT_o = din("xT_o", [128, KC, TOT])
    ropeC_b = din("ropeC_b", [128, T])
    ropeS_b = din("ropeS_b", [128, T])
    ropeC_o = din("ropeC_o", [128, TOT])
    ropeS_o = din("ropeS_o", [128, TOT])
    mdiff_d = din("mdiff", [128, 16, 512], BF16)
    msb_d = din("msb", [128, 16, 512], BF16)
    msamp_d = din("msamp", [128, 256], BF16)
    tri_d = din("tri", [128, 128], BF16)
    onesb_d = din("onesb", [128, 128], BF16)
    onesf_d = din("onesf", [128, 128])
    identf_d = din("identf", [128, 128])
    gains_d = din("gains", [128, 4, KC])
    ghead_d = din("ghead", [128, 1])
    lams_d = din("lams", [128, 4, 64])
    win_r = din("win_r", [64, 128, 2048])
    wgate_r = din("wgate_r", [32, 128, 2048])
    wpa_r = din("wpa_r", [16, 128, 1024])
    wpb_r = din("wpb_r", [16, 128, 1024])
    wout_r = din("wout_r", [16, 128, 2048])
    wup_r = din("wup_r", [64, 128, 2048])
    wdown_r = din("wdown_r", [16, 128, 8192])
    cdk = din("cdk", [SBC, 8, 128, PAST])
    cdv = din("cdv", [SBC, 8, 128, NPB, 128])
    csk = din("csk", [SBC, 8, 128, PAST])
    csv = din("csv", [SBC, 8, 128, NPB, 128])

    yT = dout("yT", [128, KC, TOT])
    kaT_o = dout("kaT_o", [8, 128, TOT])
    vaT_o = dout("vaT_o", [8, 128, TOT])
    kbT_o = dout("kbT_o", [8, 128, TOT])
    vbT_o = dout("vbT_o", [8, 128, TOT])

    hT_b = dscr("hT_b", [128, KC, T], BF16)
    hT_o = dscr("hT_o", [128, KC, TOT], BF16)
    win_s = dscr("win_s", [64, 128, 2048], BF16)
    wgate_s = dscr("wgate_s", [32, 128, 2048], BF16)
    wpa_s = dscr("wpa_s", [16, 128, 1024], BF16)
    wpb_s = dscr("wpb_s", [16, 128, 1024], BF16)
    wout_s = dscr("wout_s", [16, 128, 2048], BF16)
    wup_s = dscr("wup_s", [64, 128, 2048], BF16)
    wdown_s = dscr("wdown_s", [16, 128, 8192], BF16)
    oaT_s = dscr("oaT_s", [128, 8, TOT], BF16)
    obT_s = dscr("obT_s", [128, 8, TOT], BF16)

    out_chans = set()

    with contextlib.ExitStack() as top:
        P = Prog(nc, top)

        def sb(name, shape, dt, st=top):
            return st.enter_context(nc.sbuf_tensor("sb_" + name, list(shape), dt))

        tri = sb("tri", [128, 128], BF16)
        onesb = sb("onesb", [128, 128], BF16)
        onesf = sb("onesf", [128, 128], F32)
        identf = sb("identf", [128, 128], F32)
        gains = sb("gains", [128, 4, KC], F32)
        ghead = sb("ghead", [128, 1], F32)
        lams = sb("lams", [128, 4, 64], F32)
        lamt = sb("lamt", [128, 4, 64], F32)
        lamv = sb("lamv", [128, 4], F32)
        neglam = sb("neglam", [128, 1], F32)
        msamp = sb("msamp", [128, 256], BF16)
        rC = Res()
        for t_, d_ in ((tri, tri_d), (onesb, onesb_d), (onesf, onesf_d), (identf, identf_d),
                       (gains, gains_d), (ghead, ghead_d), (lams, lams_d), (msamp, msamp_d)):
            P.dma("sp", "const", lambda e, t_=t_, d_=d_: e.dma_start(out=t_[:], in_=d_), writes=[rC])
        P.op("dve", lambda e: e.tensor_tensor(out=lamt[:, 0, :], in0=lams[:, 0, :], in1=lams[:, 1, :], op=ALU.mult), reads=[rC], writes=[rC])
        P.op("dve", lambda e: e.tensor_tensor(out=lamt[:, 1, :], in0=lams[:, 2, :], in1=lams[:, 3, :], op=ALU.mult), reads=[rC], writes=[rC])
        P.op("dve", lambda e: e.tensor_reduce(out=lamv[:, 0:2], in_=lamt[:, 0:2, :], axis=mybir.AxisListType.X, op=ALU.add), reads=[rC], writes=[rC])
        P.op("act", lambda e: e.activation(out=lamv[:, 2:4], in_=lamv[:, 0:2], func=AF.Exp), reads=[rC], writes=[rC])
        P.op("dve", lambda e: e.scalar_tensor_tensor(out=neglam[:], in0=lamv[:, 3:4], scalar=-LAM_INIT, in1=lamv[:, 2:3], op0=ALU.add, op1=ALU.subtract), reads=[rC], writes=[rC])
        P.op("dve", lambda e: e.tensor_scalar(out=ghead[:], in0=ghead[:], scalar1=1.0 - LAM_INIT, scalar2=None, op0=ALU.mult), reads=[rC], writes=[rC])

        pall = top.enter_context(nc.psum_tensor("pall", [128, 8, 512], F32))
        rP = [Res() for _ in range(8)]

        def rstd_from(ps_ap, dst_ap, n_feat, eng_reads, eng_writes):
            P.op("dve", lambda e: e.tensor_scalar(out=dst_ap, in0=ps_ap, scalar1=1.0 / n_feat, scalar2=EPS, op0=ALU.mult, op1=ALU.add), reads=eng_reads, writes=eng_writes)
            P.op("act", lambda e: e.activation(out=dst_ap, in_=dst_ap, func=AF.Ln), reads=eng_writes, writes=eng_writes)
            P.op("act", lambda e: e.activation(out=dst_ap, in_=dst_ap, func=AF.Exp, scale=-0.5), reads=eng_writes, writes=eng_writes)

        with contextlib.ExitStack() as s0:
            NSL = 3
            wf = [sb(f"wf{i}", [128, 2048], F32, s0) for i in range(NSL)]
            wb = [sb(f"wb{i}", [128, 2048], BF16, s0) for i in range(NSL)]
            rwf = [Res() for _ in range(NSL)]
            rwb = [Res() for _ in range(NSL)]
            cnt = 0
            for src, dst, nb, w in ((win_r, win_s, 64, 2048), (wgate_r, wgate_s, 32, 2048), (wpa_r, wpa_s, 16, 1024),
                                    (wpb_r, wpb_s, 16, 1024), (wout_r, wout_s, 16, 2048), (wup_r, wup_s, 64, 2048),
                                    (wdown_r, wdown_s, 16, 8192)):
                for b_ in range(nb):
                    for c0 in range(0, w, 2048):
                        cw = min(2048, w - c0)
                        s = cnt % NSL
                        P.dma("sp", f"wl{s}", lambda e, s=s, src=src, b_=b_, c0=c0, cw=cw: e.dma_start(out=wf[s][:, 0:cw], in_=src[b_, :, c0:c0 + cw]), writes=[rwf[s]])
                        ce = ("dve", "pool", "act")[cnt % 3]
                        if ce == "act":
                            P.op("act", lambda e, s=s, cw=cw: e.copy(out=wb[s][:, 0:cw], in_=wf[s][:, 0:cw]), reads=[rwf[s]], writes=[rwb[s]])
                        else:
                            P.op(ce, lambda e, s=s, cw=cw: e.tensor_copy(out=wb[s][:, 0:cw], in_=wf[s][:, 0:cw]), reads=[rwf[s]], writes=[rwb[s]])
                        P.dma("pool", f"ws{s}", lambda e, s=s, dst=dst, b_=b_, c0=c0, cw=cw: e.dma_start(out=dst[b_, :, c0:c0 + cw], in_=wb[s][:, 0:cw]), reads=[rwb[s]], writes=[])
                        cnt += 1
            wstore_keys = [f"ws{s}" for s in range(NSL)]

            NH = 256
            xs = [sb(f"xs{i}", [128, KC, NH], F32, s0) for i in range(2)]
            sq = sb("sq0", [128, KC, NH], F32, s0)
            hb = [sb(f"hb{i}", [128, KC, NH], BF16, s0) for i in range(2)]
            rs0 = sb("rs0", [128, NH], F32, s0)
            rxs = [Res(), Res()]
            rsq = Res()
            rhb = [Res(), Res()]
            rrs = Res()
            rHB = Res()
            rHO = Res()
            it = 0
            for src, dst, ntok, rdst in ((xT_b, hT_b, T, rHB), (xT_o, hT_o, TOT, rHO)):
                for c0 in range(0, ntok, NH):
                    n = min(NH, ntok - c0)
                    s = it % 2
                    P.dma("sp", f"xl{s}", lambda e, s=s, src=src, c0=c0, n=n: e.dma_start(out=xs[s][:, :, 0:n], in_=src[:, :, c0:c0 + n]), writes=[rxs[s]])
                    P.op("act", lambda e, s=s, n=n: e.activation(out=sq[:, :, 0:n], in_=xs[s][:, :, 0:n], func=AF.Square), reads=[rxs[s]], writes=[rsq])
                    for kc in range(KC):
                        P.op("pe", lambda e, kc=kc, n=n: e.matmul(out=pall[:, 0, 0:n], lhsT=onesf[:], rhs=sq[:, kc, 0:n], start=(kc == 0), stop=(kc == KC - 1)),
                             reads=[rsq, rC], writes=[rP[0]])
                    rstd_from(pall[:, 0, 0:n], rs0[:, 0:n], D, [rP[0]], [rrs])
                    for kc in range(KC):
                        P.op("dve", lambda e, s=s, kc=kc, n=n: e.scalar_tensor_tensor(out=hb[s][:, kc, 0:n], in0=xs[s][:, kc, 0:n], scalar=gains[:, 0, kc:kc + 1], in1=rs0[:, 0:n], op0=ALU.mult, op1=ALU.mult),
                             reads=[rxs[s], rrs, rC], writes=[rhb[s]])
                    P.dma("pool", f"hs{s}", lambda e, s=s, dst=dst, c0=c0, n=n: e.dma_start(out=dst[:, :, c0:c0 + n], in_=hb[s][:, :, 0:n]), reads=[rhb[s]], writes=[rdst])
                    it += 1
            hstore_keys = ["hs0", "hs1"]
            P.barrier()

        rOA = Res()
        rOB = Res()
        ostore_keys = []
        import os
        KSTOP = int(os.environ.get('KSTOP', '9'))
        KATT = int(os.environ.get('KATT', '1'))
        KP = int(os.environ.get('KP', '9'))
        KSAMP = int(os.environ.get('KSAMP', '1'))
        KTYP = [int(c) for c in os.environ.get('KTYP', '01')]
        with contextlib.ExitStack() as s1:
            NK = 256
            KT = sb("KT", [128, T], BF16, s1)
            V = sb("V", [128, T // 128, 128], BF16, s1)
            QT = sb("QT", [128, TOT], BF16, s1)
            wh = [sb(f"wh{i}", [128, KC, 128], BF16, s1) for i in range(5)]
            ht = [sb(f"ht{i}", [128, KC, NK], BF16, s1) for i in range(2)]
            tC = [sb(f"tC{i}", [128, NK], F32, s1) for i in range(2)]
            tS = [sb(f"tS{i}", [128, NK], F32, s1) for i in range(2)]
            mt = sb("mt", [128, 16, 512], BF16, s1)
            tA = sb("tA", [128, NK], F32, s1)
            tB = sb("tB", [128, NK], F32, s1)
            kst = [sb(f"kst{i}", [128, NK], F32, s1) for i in range(2)]
            vst = [sb(f"vst{i}", [128, NK], F32, s1) for i in range(2)]
            KTs = sb("KTs", [128, 256], BF16, s1)
            ostage = sb("ostage", [128, 256], BF16, s1)
            rost = Res()
            Vs = sb("Vs", [128, SBC, 128], BF16, s1)
            NSS = 4
            Pt = [sb(f"Pt{i}", [128, 2, 512], BF16, s1) for i in range(NSS)]
            Et = [sb(f"Et{i}", [128, 512], F32, s1) for i in range(NSS)]
            Lt = [sb(f"Lt{i}", [128, 512], BF16, s1) for i in range(NSS)]
            Tt = [sb(f"Tt{i}", [128, 512], F32, s1) for i in range(NSS)]
            At = [sb(f"At{i}", [128, 512], BF16, s1) for i in range(NSS)]
            carry = sb("carry", [128, 512], F32, s1)
            acc = sb("acc", [128, 2, 512], F32, s1)
            racc = Res()
            f1 = sb("f1", [128, 512], F32, s1)
            f2 = sb("f2", [128, 512], F32, s1)
            f3 = sb("f3", [128, 512], F32, s1)
            f4 = sb("f4", [128, 512], F32, s1)
            ob16 = [sb(f"ob16{i}", [128, 512], BF16, s1) for i in range(2)]
            ckf = sb("ckf", [128, PAST // 2], F32, s1)
            ckb = sb("ckb", [128, PAST], BF16, s1)
            cvf = sb("cvf", [128, NPB // 2, 128], F32, s1)
            cvb = sb("cvb", [128, NPB, 128], BF16, s1)

            rKT, rV, rQT, rmt, rtA, rtB, rKTs, rVs, rcarry = [Res() for _ in range(9)]
            rwh = [Res() for _ in range(5)]
            rht = [Res(), Res()]
            rtab = [Res(), Res()]
            rkst = [Res(), Res()]
            rvst = [Res(), Res()]
            rPt, rEt, rLt, rTt, rAt = [[Res() for _ in range(NSS)] for _ in range(5)]
            rf = [Res() for _ in range(4)]
            rob16 = [Res(), Res()]
            rckf, rckb, rcvf, rcvb = [Res() for _ in range(4)]
            state = {"ht": 0, "st": 0, "sl": 0, "ob": 0, "s4": 0}
            P.op("pool", lambda e: e.memset(ostage[:], 0.0), writes=[rost])

            def project_tile(typ, head, src, c0, n, own):
                s = state["ht"] % 2
                state["ht"] += 1
                rsrc = rHO if own else rHB
                P.dma("sp", f"htl{s}", lambda e: e.dma_start(out=ht[s][:, :, 0:n], in_=src[:, :, c0:c0 + n]), reads=[rsrc], writes=[rht[s]])
                if typ == 0:
                    cs, ss_ = (ropeC_o, ropeS_o) if own else (ropeC_b, ropeS_b)
                    P.dma("sp", f"tbl{s}", lambda e: e.dma_start(out=tC[s][:, 0:n], in_=cs[:, c0:c0 + n]), writes=[rtab[s]])
                    P.dma("sp", f"tbl{s}", lambda e: e.dma_start(out=tS[s][:, 0:n], in_=ss_[:, c0:c0 + n]), writes=[rtab[s]])

                def chain(widx, bank):
                    for kc in range(KC):
                        P.op("pe", lambda e, kc=kc: e.matmul(out=pall[:, bank, 0:n], lhsT=wh[widx][:, kc, :], rhs=ht[s][:, kc, 0:n], start=(kc == 0), stop=(kc == KC - 1)),
                             reads=[rwh[widx], rht[s]], writes=[rP[bank]])

                def roped(w0, dst_fn, dst_res, extra=None):
                    chain(w0, 4)
                    chain(w0 + 1, 5)
                    P.op("dve", lambda e: e.tensor_tensor(out=tA[:, 0:n], in0=pall[:, 4, 0:n], in1=tC[s][:, 0:n], op=ALU.mult), reads=[rP[4], rtab[s]], writes=[rtA])
                    P.op("dve", lambda e: e.tensor_tensor(out=tB[:, 0:n], in0=pall[:, 5, 0:n], in1=tS[s][:, 0:n], op=ALU.mult), reads=[rP[5], rtab[s]], writes=[rtB])
                    P.op("pool", lambda e: e.tensor_tensor(out=dst_fn(), in0=tA[:, 0:n], in1=tB[:, 0:n], op=ALU.add), reads=[rtA, rtB], writes=[dst_res])
                    if extra is not None:
                        P.op("pool", lambda e: e.tensor_tensor(out=extra[0](), in0=tA[:, 0:n], in1=tB[:, 0:n], op=ALU.add), reads=[rtA, rtB], writes=[extra[1]])

                def plain(widx, dst_fn, dst_res, extra=None):
                    chain(widx, 4)
                    P.op("act", lambda e: e.copy(out=dst_fn(), in_=pall[:, 4, 0:n]), reads=[rP[4]], writes=[dst_res])
                    if extra is not None:
                        P.op("pool", lambda e: e.tensor_copy(out=extra[0](), in_=dst_fn()), reads=[dst_res], writes=[extra[1]])

                samp = own and c0 >= TO
                if KP < 1 or (own and KP < 4) or (samp and KP < 6):
                    return
                if typ == 0:
                    iq, ik, iv = 0, 2, 4
                else:
                    iq, ik, iv = 0, 1, 2
                so = state["st"] % 2
                if own:
                    state["st"] += 1
                    if typ == 0:
                        roped(iq, lambda: QT[:, c0:c0 + n], rQT)
                    else:
                        plain(iq, lambda: QT[:, c0:c0 + n], rQT)
                    ex = (lambda: KTs[:, 0:n], rKTs) if samp else None
                    if typ == 0:
                        roped(ik, lambda: kst[so][:, 0:n], rkst[so], ex)
                    else:
                        plain(ik, lambda: kst[so][:, 0:n], rkst[so], ex)
                    kout = kaT_o if typ == 0 else kbT_o
                    P.dma("pool", f"kso{so}", lambda e: e.dma_start(out=kout[head, :, c0:c0 + n], in_=kst[so][:, 0:n]), reads=[rkst[so]])
                    out_chans.add(f"kso{so}")
                else:
                    if typ == 0:
                        roped(ik, lambda: KT[:, c0:c0 + n], rKT)
                    else:
                        plain(ik, lambda: KT[:, c0:c0 + n], rKT)
                if KP < 2:
                    return
                chain(iv, 6)
                P.op("act", lambda e: e.copy(out=vst[so][:, 0:n], in_=pall[:, 6, 0:n]), reads=[rP[6]], writes=[rvst[so]])
                if own:
                    vout = vaT_o if typ == 0 else vbT_o
                    P.dma("pool", f"vso{so}", lambda e: e.dma_start(out=vout[head, :, c0:c0 + n], in_=vst[so][:, 0:n]), reads=[rvst[so]])
                    out_chans.add(f"vso{so}")
                    if samp and KP >= 7:
                        for sbi in range(SBC):
                            P.op("pe", lambda e, sbi=sbi: e.transpose(out=pall[0:ST, 7, sbi * 128:(sbi + 1) * 128], in_=vst[so][:, sbi * ST:(sbi + 1) * ST], identity=identf[:]),
                                 reads=[rvst[so], rC], writes=[rP[7]])
                            P.op("dve", lambda e, sbi=sbi: e.tensor_copy(out=Vs[0:ST, sbi, :], in_=pall[0:ST, 7, sbi * 128:(sbi + 1) * 128]), reads=[rP[7]], writes=[rVs])
                elif KP >= 3:
                    nb_ = n // 128
                    for i in range(nb_):
                        P.op("pe", lambda e, i=i: e.transpose(out=pall[:, 7, i * 128:(i + 1) * 128], in_=vst[so][:, i * 128:(i + 1) * 128], identity=identf[:]),
                             reads=[rvst[so], rC], writes=[rP[7]])
                    kb0 = c0 // 128
                    for i in range(nb_):
                        P.op("dve", lambda e, i=i: e.tensor_copy(out=V[:, kb0 + i, :], in_=pall[:, 7, i * 128:(i + 1) * 128]), reads=[rP[7]], writes=[rV])
                    state["st"] += 1

            def attn_diff(head, q0, n, blocks):
                nblk = len(blocks)

                P.op("pool", lambda e: e.memset(acc[:, :, 0:n], 0.0), writes=[racc])

                def step(bi, kT, vap, nk, mask, rds):
                    b0 = 2 + 2 * (state["sl"] % 3)
                    state["sl"] += 1
                    sl = state["s4"] % NSS
                    state["s4"] += 1
                    for m in range(2):
                        P.op("pe", lambda e, m=m: e.matmul(out=pall[0:nk, b0 + m, 0:n], lhsT=kT(m), rhs=QT[m * 64:(m + 1) * 64, q0:q0 + n], start=True, stop=True),
                             reads=rds + [rQT], writes=[rP[b0 + m]])
                    P.op("act", lambda e: e.activation(out=Pt[sl][0:nk, :, 0:n], in_=pall[0:nk, b0:b0 + 2, 0:n], func=AF.Exp, scale=0.125),
                         reads=[rP[b0], rP[b0 + 1]], writes=[rPt[sl]])
                    if mask is not None:
                        for m in range(2):
                            P.op("pool", lambda e, m=m: e.tensor_tensor(out=Pt[sl][0:nk, m, 0:n], in0=Pt[sl][0:nk, m, 0:n], in1=mask, op=ALU.mult),
                                 reads=[rPt[sl], rmt], writes=[rPt[sl]])
                    st_, sp_ = (bi == 0), (bi == nblk - 1)
                    for m in range(2):
                        P.op("pe", lambda e, m=m: e.matmul(out=pall[:, m, 0:n], lhsT=vap, rhs=Pt[sl][0:nk, m, 0:n], start=st_, stop=sp_),
                             reads=rds + [rPt[sl]], writes=[rP[m]])
                    P.op("dve", lambda e: e.tensor_tensor(out=acc[0:nk, :, 0:n], in0=acc[0:nk, :, 0:n], in1=Pt[sl][0:nk, :, 0:n], op=ALU.add), reads=[racc, rPt[sl]], writes=[racc])
                for bi, blk in enumerate(blocks):
                    step(bi, *blk)
                a, b, o, q_ = f1[:, 0:n], f2[:, 0:n], f3[:, 0:n], f4[:, 0:n]
                for m in range(2):
                    P.op("pe", lambda e, m=m: e.matmul(out=pall[:, 2 + m, 0:n], lhsT=onesf[:], rhs=acc[:, m, 0:n], start=True, stop=True), reads=[rC, racc], writes=[rP[2 + m]])
                P.op("dve", lambda e: e.reciprocal(out=q_, in_=pall[:, 2, 0:n]), reads=[rP[2]], writes=[rf[3]])
                P.op("dve", lambda e: e.tensor_tensor(out=a, in0=pall[:, 0, 0:n], in1=q_, op=ALU.mult), reads=[rP[0], rf[3]], writes=[rf[0]])
                P.op("dve", lambda e: e.reciprocal(out=q_, in_=pall[:, 3, 0:n]), reads=[rP[3]], writes=[rf[3]])
                P.op("dve", lambda e: e.tensor_tensor(out=b, in0=pall[:, 1, 0:n], in1=q_, op=ALU.mult), reads=[rP[1], rf[3]], writes=[rf[1]])
                P.op("dve", lambda e: e.scalar_tensor_tensor(out=o, in0=b, scalar=neglam[:, 0:1], in1=a, op0=ALU.mult, op1=ALU.add), reads=[rf[0], rf[1], rC], writes=[rf[2]])
                P.op("act", lambda e: e.activation(out=a, in_=o, func=AF.Square), reads=[rf[2]], writes=[rf[0]])
                P.op("pe", lambda e: e.matmul(out=pall[:, 4, 0:n], lhsT=onesf[:], rhs=a, start=True, stop=True), reads=[rC, rf[0]], writes=[rP[4]])
                rstd_from(pall[:, 4, 0:n], b, 128, [rP[4]], [rf[1]])
                so = state["ob"] % 2
                state["ob"] += 1
                P.op("dve", lambda e: e.scalar_tensor_tensor(out=ob16[so][:, 0:n], in0=o, scalar=ghead[:, 0:1], in1=b, op0=ALU.mult, op1=ALU.mult), reads=[rf[2], rf[1], rC], writes=[rob16[so]])
                if q0 >= TO:
                    P.op("pool", lambda e: e.tensor_copy(out=ostage[:, q0 - TO:q0 - TO + n], in_=ob16[so][:, 0:n]), reads=[rob16[so]], writes=[rost])
                    if q0 + n == TO + NS:
                        P.dma("pool", "osst", lambda e: e.dma_start(out=oaT_s[:, head, TO:TO + 256], in_=ostage[:]), reads=[rost], writes=[])
                else:
                    P.dma("pool", f"oas{so}", lambda e: e.dma_start(out=oaT_s[:, head, q0:q0 + n], in_=ob16[so][:, 0:n]), reads=[rob16[so]], writes=[])

            def attn_sb(head, q0, n, blocks):
                nblk = len(blocks)
                P.op("pool", lambda e: e.memset(carry[:, 0:n], 0.0), writes=[rcarry])
                sc = 128.0 ** -0.5

                def step(bi, kT, vap, nk, mask, rds):
                    slp = state["sl"] % 2
                    state["sl"] += 1
                    bz, bc, bs_ = 1 + slp, 3 + slp, 5 + slp
                    sl = state["s4"] % NSS
                    state["s4"] += 1
                    P.op("pe", lambda e: e.matmul(out=pall[0:nk, bz, 0:n], lhsT=kT(0), rhs=QT[:, q0:q0 + n], start=True, stop=True),
                         reads=rds + [rQT], writes=[rP[bz]])
                    P.op("act", lambda e: e.activation(out=Et[sl][0:nk, 0:n], in_=pall[0:nk, bz, 0:n], func=AF.Exp, scale=sc), reads=[rP[bz]], writes=[rEt[sl]])
                    P.op("act", lambda e: e.activation(out=Lt[sl][0:nk, 0:n], in_=Et[sl][0:nk, 0:n], func=AF.Ln, bias=1.0, scale=1.0), reads=[rEt[sl]], writes=[rLt[sl]])
                    if mask is not None:
                        P.op("pool", lambda e: e.tensor_tensor(out=Lt[sl][0:nk, 0:n], in0=Lt[sl][0:nk, 0:n], in1=mask, op=ALU.mult), reads=[rLt[sl], rmt], writes=[rLt[sl]])
                    P.op("pe", lambda e: e.matmul(out=pall[0:nk, bc, 0:n], lhsT=tri[0:nk, 0:nk], rhs=Lt[sl][0:nk, 0:n], start=True, stop=True), reads=[rC, rLt[sl]], writes=[rP[bc]])
                    P.op("pe", lambda e: e.matmul(out=pall[:, bs_, 0:n], lhsT=onesb[0:nk, :], rhs=Lt[sl][0:nk, 0:n], start=True, stop=True), reads=[rC, rLt[sl]], writes=[rP[bs_]])
                    P.op("dve", lambda e: e.tensor_tensor(out=Tt[sl][0:nk, 0:n], in0=pall[0:nk, bc, 0:n], in1=carry[0:nk, 0:n], op=ALU.add), reads=[rP[bc], rcarry], writes=[rTt[sl]])
                    P.op("act", lambda e: e.activation(out=Tt[sl][0:nk, 0:n], in_=Tt[sl][0:nk, 0:n], func=AF.Exp, scale=-1.0), reads=[rTt[sl]], writes=[rTt[sl]])
                    P.op("pool", lambda e: e.tensor_tensor(out=At[sl][0:nk, 0:n], in0=Et[sl][0:nk, 0:n], in1=Tt[sl][0:nk, 0:n], op=ALU.mult), reads=[rEt[sl], rTt[sl]], writes=[rAt[sl]])
                    if mask is not None:
                        P.op("pool", lambda e: e.tensor_tensor(out=At[sl][0:nk, 0:n], in0=At[sl][0:nk, 0:n], in1=mask, op=ALU.mult), reads=[rAt[sl], rmt], writes=[rAt[sl]])
                    P.op("dve", lambda e: e.tensor_tensor(out=carry[:, 0:n], in0=pall[:, bs_, 0:n], in1=carry[:, 0:n], op=ALU.add), reads=[rP[bs_], rcarry], writes=[rcarry])
                    P.op("pe", lambda e: e.matmul(out=pall[:, 0, 0:n], lhsT=vap, rhs=At[sl][0:nk, 0:n], start=(bi == 0), stop=(bi == nblk - 1)),
                         reads=rds + [rAt[sl]], writes=[rP[0]])
                for bi, blk in enumerate(blocks):
                    step(bi, *blk)
                so = state["ob"] % 2
                state["ob"] += 1
                P.op("act", lambda e: e.copy(out=ob16[so][:, 0:n], in_=pall[:, 0, 0:n]), reads=[rP[0]], writes=[rob16[so]])
                if q0 >= TO:
                    P.op("pool", lambda e: e.tensor_copy(out=ostage[:, q0 - TO:q0 - TO + n], in_=ob16[so][:, 0:n]), reads=[rob16[so]], writes=[rost])
                    if q0 + n == TO + NS:
                        P.dma("pool", "osst", lambda e: e.dma_start(out=obT_s[:, head, TO:TO + 256], in_=ostage[:]), reads=[rost], writes=[])
                else:
                    P.dma("pool", f"obs{so}", lambda e: e.dma_start(out=obT_s[:, head, q0:q0 + n], in_=ob16[so][:, 0:n]), reads=[rob16[so]], writes=[])

            for typ in (KTYP if KSTOP >= 2 else []):
                md = mdiff_d if typ == 0 else msb_d
                P.dma("sp", "mtl", lambda e, md=md: e.dma_start(out=mt[:], in_=md), writes=[rmt])
                for head in range(8):
                    nw = 5 if typ == 0 else 3
                    base = 5 * head if typ == 0 else 40 + 3 * head
                    for i in range(nw):
                        P.dma("sp", f"whl{i}", lambda e, i=i, base=base: e.dma_start(out=wh[i][:], in_=win_s[base + i, :, :].rearrange("p (a b) -> p a b", b=128)), writes=[rwh[i]])
                    for c0 in range(0, T, NK):
                        project_tile(typ, head, hT_b, c0, NK, False)
                    for c0 in range(0, TOT, NK):
                        project_tile(typ, head, hT_o, c0, min(NK, TOT - c0), True)
                    afn = attn_diff if typ == 0 else attn_sb
                    for j in (range(NJ) if KATT else []):
                        blocks = []
                        for kb in range(16 * j + 16):
                            if typ == 0:
                                kT = (lambda m, kb=kb: KT[m * 64:(m + 1) * 64, kb * 128:(kb + 1) * 128])
                            else:
                                kT = (lambda m, kb=kb: KT[:, kb * 128:(kb + 1) * 128])
                            mask = mt[:, kb - 16 * j, :] if kb >= 16 * j else None
                            blocks.append((kT, V[:, kb, :], 128, mask, [rKT, rV]))
                        if typ == 1:
                            blocks = blocks[::-1]
                        afn(head, j * 512, 512, blocks)
                    ck_d, cv_d = (cdk, cdv) if typ == 0 else (csk, csv)
                    for sbi in (range(SBC) if KSAMP else []):
                        for hf in range(2):
                            P.dma("sp", "ckl", lambda e, sbi=sbi, ck_d=ck_d, head=head, hf=hf: e.dma_start(out=ckf[:], in_=ck_d[sbi, head, :, hf * (PAST // 2):(hf + 1) * (PAST // 2)]), writes=[rckf])
                            P.dma("sp", "cvl", lambda e, sbi=sbi, cv_d=cv_d, head=head, hf=hf: e.dma_start(out=cvf[:], in_=cv_d[sbi, head, :, hf * (NPB // 2):(hf + 1) * (NPB // 2), :]), writes=[rcvf])
                            P.op("dve", lambda e, hf=hf: e.tensor_copy(out=ckb[:, hf * (PAST // 2):(hf + 1) * (PAST // 2)], in_=ckf[:]), reads=[rckf], writes=[rckb])
                            P.op("pool", lambda e, hf=hf: e.tensor_copy(out=cvb[:, hf * (NPB // 2):(hf + 1) * (NPB // 2), :], in_=cvf[:]), reads=[rcvf], writes=[rcvb])
                        blocks = []
                        for kb in range(NPB):
                            if typ == 0:
                                kT = (lambda m, kb=kb: ckb[m * 64:(m + 1) * 64, kb * 128:(kb + 1) * 128])
                            else:
                                kT = (lambda m, kb=kb: ckb[:, kb * 128:(kb + 1) * 128])
                            blocks.append((kT, cvb[:, kb, :], 128, None, [rckb, rcvb]))
                        if typ == 0:
                            kT = (lambda m, sbi=sbi: KTs[m * 64:(m + 1) * 64, sbi * ST:(sbi + 1) * ST])
                            blocks.append((kT, Vs[0:ST, sbi, :], ST, None, [rKTs, rVs]))
                        else:
                            kT = (lambda m, sbi=sbi: KTs[:, sbi * ST:(sbi + 1) * ST])
                            blocks.append((kT, Vs[0:ST, sbi, :], ST, msamp[0:ST, 0:ST], [rKTs, rVs, rC]))
                            blocks = blocks[::-1]
                        afn(head, TO + sbi * ST, ST, blocks)
            P.barrier()

        with contextlib.ExitStack() as s3:
            NQ = 512
            A = sb("A3", [128, KC, NQ], F32, s3)
            B = sb("B3", [128, KC, NQ], F32, s3)
            C = sb("C3", [128, KC, NQ], BF16, s3)
            Fb = sb("F3", [128, 64, NQ], BF16, s3)
            NR = 8
            ring = [sb(f"ring{i}", [128, KC, 128], BF16, s3) for i in range(NR)]
            sga = sb("sga", [128, NQ], F32, s3)
            sgb = sb("sgb", [128, NQ], F32, s3)
            m1 = sb("m1", [128, NQ], F32, s3)
            m2 = sb("m2", [128, NQ], F32, s3)
            sqt = [sb(f"sqt{i}", [128, NQ], F32, s3) for i in range(2)]
            rsd = sb("rsd", [128, NQ], F32, s3)
            tmp3 = [sb(f"tmp3{i}", [128, NQ], F32, s3) for i in range(2)]
            rA, rB, rCC, rFlo, rFhi, rsga, rsgb, rm1, rm2, rrsd = [Res() for _ in range(10)]
            rring = [Res() for _ in range(NR)]
            rsqt = [Res(), Res()]
            rtmp3 = [Res(), Res()]
            st3 = {"r": 0, "q": 0, "t": 0, "pb": 0}

            def wload(src_ap, ncols):
                i = st3["r"] % NR
                st3["r"] += 1
                nch = ncols // 128
                P.dma("sp", f"rg{i}", lambda e: e.dma_start(out=ring[i][:, 0:nch, :], in_=src_ap.rearrange("p (a b) -> p a b", b=128)), writes=[rring[i]])
                return i

            def rF(f):
                return rFlo if f < 32 else rFhi

            def sumsq_accum(src_ap, src_res, n, f, nf):
                q = st3["q"] % 2
                st3["q"] += 1
                P.op("act", lambda e: e.activation(out=sqt[q][:, 0:n], in_=src_ap, func=AF.Square), reads=src_res, writes=[rsqt[q]])
                P.op("pe", lambda e: e.matmul(out=pall[:, 4, 0:n], lhsT=onesf[:], rhs=sqt[q][:, 0:n], start=(f == 0), stop=(f == nf - 1)), reads=[rC, rsqt[q]], writes=[rP[4]])

            def resid_add(n, gi):
                for f in range(KC):
                    t = st3["t"] % 2
                    st3["t"] += 1
                    P.op("dve", lambda e, f=f, t=t: e.scalar_tensor_tensor(out=tmp3[t][:, 0:n], in0=B[:, f, 0:n], scalar=gains[:, gi, f:f + 1], in1=rsd[:, 0:n], op0=ALU.mult, op1=ALU.mult),
                         reads=[rB, rrsd, rC], writes=[rtmp3[t]])
                    P.op("pool", lambda e, f=f, t=t: e.tensor_tensor(out=A[:, f, 0:n], in0=A[:, f, 0:n], in1=tmp3[t][:, 0:n], op=ALU.add), reads=[rA, rtmp3[t]], writes=[rA])

            def tile3(c0, n):
                P.dma("sp", "xa", lambda e, c0=c0, n=n: e.dma_start(out=A[:, :, 0:n], in_=xT_o[:, :, c0:c0 + n]), writes=[rA])
                P.dma("sp", "hc", lambda e, c0=c0, n=n: e.dma_start(out=C[:, :, 0:n], in_=hT_o[:, :, c0:c0 + n]), reads=[rHO], writes=[rCC])
                P.dma("sp", "oab", lambda e, c0=c0, n=n: e.dma_start(out=Fb[:, 0:8, 0:n], in_=oaT_s[:, :, c0:c0 + n]), reads=[rOA], writes=[rFlo])
                P.dma("sp", "oab", lambda e, c0=c0, n=n: e.dma_start(out=Fb[:, 8:16, 0:n], in_=obT_s[:, :, c0:c0 + n]), reads=[rOB], writes=[rFlo])
                for f in range(KC):
                    ia = wload(wgate_s[f, :, :], 2048)
                    ib = wload(wgate_s[16 + f, :, :], 2048)
                    ipa = wload(wpa_s[f, :, :], 1024)
                    ipb = wload(wpb_s[f, :, :], 1024)
                    for kc in range(KC):
                        P.op("pe", lambda e, kc=kc, ia=ia: e.matmul(out=pall[:, 0, 0:n], lhsT=ring[ia][:, kc, :], rhs=C[:, kc, 0:n], start=(kc == 0), stop=(kc == KC - 1)), reads=[rring[ia], rCC], writes=[rP[0]])
                    for kc in range(KC):
                        P.op("pe", lambda e, kc=kc, ib=ib: e.matmul(out=pall[:, 1, 0:n], lhsT=ring[ib][:, kc, :], rhs=C[:, kc, 0:n], start=(kc == 0), stop=(kc == KC - 1)), reads=[rring[ib], rCC], writes=[rP[1]])
                    for h in range(8):
                        P.op("pe", lambda e, h=h, ipa=ipa: e.matmul(out=pall[:, 2, 0:n], lhsT=ring[ipa][:, h, :], rhs=Fb[:, h, 0:n], start=(h == 0), stop=(h == 7)), reads=[rring[ipa], rFlo], writes=[rP[2]])
                    for h in range(8):
                        P.op("pe", lambda e, h=h, ipb=ipb: e.matmul(out=pall[:, 3, 0:n], lhsT=ring[ipb][:, h, :], rhs=Fb[:, 8 + h, 0:n], start=(h == 0), stop=(h == 7)), reads=[rring[ipb], rFlo], writes=[rP[3]])
                    P.op("act", lambda e: e.activation(out=sga[:, 0:n], in_=pall[:, 0, 0:n], func=AF.Sigmoid), reads=[rP[0]], writes=[rsga])
                    P.op("act", lambda e: e.activation(out=sgb[:, 0:n], in_=pall[:, 1, 0:n], func=AF.Sigmoid), reads=[rP[1]], writes=[rsgb])
                    P.op("dve", lambda e: e.tensor_tensor(out=m1[:, 0:n], in0=pall[:, 2, 0:n], in1=sga[:, 0:n], op=ALU.mult), reads=[rP[2], rsga], writes=[rm1])
                    P.op("dve", lambda e: e.tensor_tensor(out=m2[:, 0:n], in0=pall[:, 3, 0:n], in1=sgb[:, 0:n], op=ALU.mult), reads=[rP[3], rsgb], writes=[rm2])
                    P.op("pool", lambda e, f=f: e.tensor_tensor(out=Fb[:, 16 + f, 0:n], in0=m1[:, 0:n], in1=m2[:, 0:n], op=ALU.add), reads=[rm1, rm2], writes=[rFlo])
                for f in range(KC):
                    iw = wload(wout_s[f, :, :], 2048)
                    pb = 5 + st3["pb"] % 2
                    st3["pb"] += 1
                    for kc in range(KC):
                        P.op("pe", lambda e, kc=kc, iw=iw, pb=pb: e.matmul(out=pall[:, pb, 0:n], lhsT=ring[iw][:, kc, :], rhs=Fb[:, 16 + kc, 0:n], start=(kc == 0), stop=(kc == KC - 1)), reads=[rring[iw], rFlo], writes=[rP[pb]])
                    P.op("dve", lambda e, f=f, pb=pb: e.tensor_copy(out=B[:, f, 0:n], in_=pall[:, pb, 0:n]), reads=[rP[pb]], writes=[rB])
                    sumsq_accum(B[:, f, 0:n], [rB], n, f, KC)
                rstd_from(pall[:, 4, 0:n], rsd[:, 0:n], D, [rP[4]], [rrsd])
                resid_add(n, 1)
                for f in range(KC):
                    sumsq_accum(A[:, f, 0:n], [rA], n, f, KC)
                rstd_from(pall[:, 4, 0:n], rsd[:, 0:n], D, [rP[4]], [rrsd])
                for f in range(KC):
                    P.op("dve", lambda e, f=f: e.scalar_tensor_tensor(out=C[:, f, 0:n], in0=A[:, f, 0:n], scalar=gains[:, 2, f:f + 1], in1=rsd[:, 0:n], op0=ALU.mult, op1=ALU.mult), reads=[rA, rrsd, rC], writes=[rCC])
                for f in range(64):
                    iw = wload(wup_s[f, :, :], 2048)
                    pb = 5 + st3["pb"] % 2
                    st3["pb"] += 1
                    for kc in range(KC):
                        P.op("pe", lambda e, kc=kc, iw=iw, pb=pb: e.matmul(out=pall[:, pb, 0:n], lhsT=ring[iw][:, kc, :], rhs=C[:, kc, 0:n], start=(kc == 0), stop=(kc == KC - 1)), reads=[rring[iw], rCC], writes=[rP[pb]])
                    t = st3["t"] % 2
                    st3["t"] += 1
                    P.op("act", lambda e, pb=pb, t=t: e.activation(out=tmp3[t][:, 0:n], in_=pall[:, pb, 0:n], func=AF.Relu), reads=[rP[pb]], writes=[rtmp3[t]])
                    P.op("pool", lambda e, f=f, t=t: e.tensor_tensor(out=Fb[:, f, 0:n], in0=tmp3[t][:, 0:n], in1=tmp3[t][:, 0:n], op=ALU.mult), reads=[rtmp3[t]], writes=[rF(f)])
                for f in range(KC):
                    pb = 5 + st3["pb"] % 2
                    st3["pb"] += 1
                    for part in range(4):
                        iw = wload(wdown_s[f, :, part * 2048:(part + 1) * 2048], 2048)
                        for kc in range(KC):
                            kk = part * 16 + kc
                            P.op("pe", lambda e, kc=kc, kk=kk, iw=iw, pb=pb: e.matmul(out=pall[:, pb, 0:n], lhsT=ring[iw][:, kc, :], rhs=Fb[:, kk, 0:n], start=(kk == 0), stop=(kk == 63)), reads=[rring[iw], rF(kk)], writes=[rP[pb]])
                    P.op("dve", lambda e, f=f, pb=pb: e.tensor_copy(out=B[:, f, 0:n], in_=pall[:, pb, 0:n]), reads=[rP[pb]], writes=[rB])
                    sumsq_accum(B[:, f, 0:n], [rB], n, f, KC)
                rstd_from(pall[:, 4, 0:n], rsd[:, 0:n], D, [rP[4]], [rrsd])
                resid_add(n, 3)
                P.dma("pool", "yst", lambda e, c0=c0, n=n: e.dma_start(out=yT[:, :, c0:c0 + n], in_=A[:, :, 0:n]), reads=[rA])
                out_chans.add("yst")
            for c0 in (range(0, TOT, NQ) if KSTOP >= 3 else []):
                tile3(c0, min(NQ, TOT - c0))
        P.wait_all("sp", sorted(out_chans))
        P.barrier()
        P.emit()
    return nc


def fblocks(W):
    K, N = W.shape
    return np.ascontiguousarray(W.reshape(K // 128, 128, N // 128, 128).transpose(2, 1, 0, 3).reshape(N // 128, 128, K))


def featmajor(x2d):
    n = x2d.shape[0]
    return np.ascontiguousarray(x2d.T.reshape(KC, 128, n).transpose(1, 0, 2))


def rope_tables(pos):
    inv = (np.float32(THETA) ** (-np.arange(32, dtype=np.float32) / np.float32(32))).astype(np.float32)
    ang = pos.astype(np.float32)[:, None] * inv[None, :]
    cos = np.cos(ang).astype(np.float32)
    sin = np.sin(ang).astype(np.float32)
    p = np.arange(128)
    Cc = cos[:, p % 32].T
    Ss = sin[:, p % 32].T * np.where((p % 64) < 32, -1.0, 1.0).astype(np.float32)[:, None]
    return np.ascontiguousarray(Cc, dtype=np.float32), np.ascontiguousarray(Ss, dtype=np.float32)


_CACHE = {}


def kernel(x_prompt, x_sample, cache_diff_k, cache_diff_v, cache_sb_k, cache_sb_v,
           g_pre_mix, w_in, lambda_q1, lambda_k1, lambda_q2, lambda_k2, g_diff_head,
           w_gate, w_proj_a, w_proj_b, w_out, g_post_mix, g_pre_mlp, w_up, w_down, g_post_mlp):
    f32 = np.float32
    x_prompt = np.asarray(x_prompt, f32)
    x_sample = np.asarray(x_sample, f32)
    B_, T, _ = x_prompt.shape
    SBT, ST, _ = x_sample.shape
    PAST = cache_diff_k.shape[2]
    assert B_ == 2 and SBT == 16
    cfg = Cfg(T=T, PAST=PAST, ST=ST, SBC=2)
    key = (T, PAST, ST)
    if key not in _CACHE:
        _CACHE[key] = build(cfg)
    nc = _CACHE[key]
    NJ, TO, TOT, NS, NPB = cfg.NJ, cfg.TO, cfg.TOT, cfg.NS, cfg.NPB

    w_in0 = np.asarray(w_in, f32)[0]
    perm = (np.arange(64) + 32) % 64
    slices = []
    for h in range(8):
        q = np.concatenate([h * 64 + np.arange(64), (8 + h) * 64 + np.arange(64)])
        qp = np.concatenate([h * 64 + perm, (8 + h) * 64 + perm])
        slices += [q, qp, 1024 + q, 1024 + qp, 2048 + h * 128 + np.arange(128)]
    for h in range(8):
        r = h * 128 + np.arange(128)
        slices += [3072 + r, 4096 + r, 5120 + r]
    win_r = np.stack([fblocks(w_in0[:, c])[0] for c in slices])
    wgate_r = fblocks(np.asarray(w_gate, f32)[0])
    wpa_r = fblocks(np.asarray(w_proj_a, f32)[0])
    wpb_r = fblocks(np.asarray(w_proj_b, f32)[0])
    wout_r = fblocks(np.asarray(w_out, f32)[0])
    wup_r = fblocks(np.asarray(w_up, f32)[0])
    wdown_r = fblocks(np.asarray(w_down, f32)[0])

    def gvec(g):
        return np.asarray(g, f32)[0].reshape(KC, 128).T
    gains = np.ascontiguousarray(np.stack([gvec(g_pre_mix), gvec(g_post_mix), gvec(g_pre_mlp), gvec(g_post_mlp)], axis=1))
    ghead = np.ascontiguousarray(np.asarray(g_diff_head, f32)[0].reshape(128, 1))
    lams = np.ascontiguousarray(np.broadcast_to(np.stack([np.asarray(v, f32)[0] for v in (lambda_q1, lambda_k1, lambda_q2, lambda_k2)])[None], (128, 4, 64)))
    tri = (np.arange(128)[:, None] >= np.arange(128)[None, :]).astype(NPBF)
    onesb = np.ones((128, 128), NPBF)
    onesf = np.ones((128, 128), f32)
    identf = np.eye(128, dtype=f32)
    msamp = np.zeros((128, 256), NPBF)
    msamp[:ST, :ST] = (np.arange(ST)[:, None] < np.arange(ST)[None, :]).astype(NPBF)
    ropeC_b, ropeS_b = rope_tables(np.arange(T))
    xTb = [featmajor(x_prompt[b]) for b in range(2)]
    cdk_all = np.asarray(cache_diff_k, f32)[0]
    cdv_all = np.asarray(cache_diff_v, f32)[0]
    csk_all = np.asarray(cache_sb_k, f32)[0]
    csv_all = np.asarray(cache_sb_v, f32)[0]

    in_maps = []
    own_tok = []
    for c in range(8):
        b, qtr = c // 4, c % 4
        toks = np.concatenate([np.arange((4 * j + qtr) * 512, (4 * j + qtr + 1) * 512) for j in range(NJ)])
        own_tok.append(toks)
        xo = np.concatenate([x_prompt[b][toks], x_sample[2 * c], x_sample[2 * c + 1], np.zeros((cfg.NSP - NS, D), f32)], axis=0)
        pos_o = np.concatenate([toks, PAST + np.arange(ST), PAST + np.arange(ST), np.zeros(cfg.NSP - NS, np.int64)])
        rC_o, rS_o = rope_tables(pos_o)
        kpos = (np.arange(4)[:, None, None] * 512 + np.arange(4)[None, :, None] * 128 + np.arange(128)[None, None, :])
        qpos = qtr * 512 + np.arange(512)
        md = (kpos[..., None] < ((qpos // 64 + 1) * 64)[None, None, None, :])
        ms = (kpos[..., None] < qpos[None, None, None, :])
        mdiff = np.ascontiguousarray(md.reshape(16, 128, 512).transpose(1, 0, 2)).astype(NPBF)
        msb = np.ascontiguousarray(ms.reshape(16, 128, 512).transpose(1, 0, 2)).astype(NPBF)
        sb_ids = [2 * c, 2 * c + 1]
        dk = cdk_all[sb_ids]
        cdk_c = np.concatenate([dk[:, :, 0:8, :], dk[:, :, 8:16, :]], axis=-1)
        cdk_c = np.ascontiguousarray(cdk_c.transpose(0, 2, 3, 1))
        csk_c = np.ascontiguousarray(csk_all[sb_ids].transpose(0, 2, 3, 1))
        def vlay(v):
            return np.ascontiguousarray(v.reshape(2, NPB, 128, 8, 128).transpose(0, 3, 2, 1, 4))
        in_maps.append(dict(
            xT_b=xTb[b], xT_o=featmajor(xo), ropeC_b=ropeC_b, ropeS_b=ropeS_b, ropeC_o=rC_o, ropeS_o=rS_o,
            mdiff=mdiff, msb=msb, msamp=msamp, tri=tri, onesb=onesb, onesf=onesf, identf=identf,
            gains=gains, ghead=ghead, lams=lams, win_r=win_r, wgate_r=wgate_r, wpa_r=wpa_r, wpb_r=wpb_r,
            wout_r=wout_r, wup_r=wup_r, wdown_r=wdown_r,
            cdk=cdk_c, cdv=vlay(cdv_all[sb_ids]), csk=csk_c, csv=vlay(csv_all[sb_ids])))

    res = run_bass_kernel_spmd(nc, in_maps, core_ids=list(range(8)))

    y_p = np.zeros((2, T, D), f32)
    y_s = np.zeros((16, ST, D), f32)
    dk_p = np.zeros((1, 2, T, 16, 64), f32)
    dv_p = np.zeros((1, 2, T, 8, 128), f32)
    sk_p = np.zeros((1, 2, T, 8, 128), f32)
    sv_p = np.zeros((1, 2, T, 8, 128), f32)
    dk_s = np.zeros((1, 16, ST, 16, 64), f32)
    dv_s = np.zeros((1, 16, ST, 8, 128), f32)
    sk_s = np.zeros((1, 16, ST, 8, 128), f32)
    sv_s = np.zeros((1, 16, ST, 8, 128), f32)
    for c in range(8):
        r = res.results[c]
        b = c // 4
        toks = own_tok[c]
        y = np.asarray(r["yT"]).transpose(1, 0, 2).reshape(D, TOT).T
        y_p[b, toks] = y[:TO]
        ka = np.asarray(r["kaT_o"]).transpose(2, 0, 1)
        va = np.asarray(r["vaT_o"]).transpose(2, 0, 1)
        kb = np.asarray(r["kbT_o"]).transpose(2, 0, 1)
        vb = np.asarray(r["vbT_o"]).transpose(2, 0, 1)
        ka16 = np.concatenate([ka[:, :, 0:64], ka[:, :, 64:128]], axis=1)
        dk_p[0, b, toks] = ka16[:TO]
        dv_p[0, b, toks] = va[:TO]
        sk_p[0, b, toks] = kb[:TO]
        sv_p[0, b, toks] = vb[:TO]
        for i in range(2):
            sl = slice(TO + i * ST, TO + (i + 1) * ST)
            y_s[2 * c + i] = y[sl]
            dk_s[0, 2 * c + i] = ka16[sl]
            dv_s[0, 2 * c + i] = va[sl]
            sk_s[0, 2 * c + i] = kb[sl]
            sv_s[0, 2 * c + i] = vb[sl]
    return (y_p, y_s, dk_p, dv_p, sk_p, sv_p, dk_s, dv_s, sk_s, sv_s)
```

```python
import contextlib
import numpy as np
import ml_dtypes
import concourse.bass as bass
import concourse.mybir as mybir
from concourse.bass_utils import run_bass_kernel_spmd

F32 = mybir.dt.float32
BF16 = mybir.dt.bfloat16
AF = mybir.ActivationFunctionType
ALU = mybir.AluOpType
NPBF = ml_dtypes.bfloat16

D = 2048
KC = 16
EPS = 1e-6
THETA = 10000.0
LAM_INIT = 0.8 - 0.6 * 1.0


class Res:
    __slots__ = ("lw", "rd")

    def __init__(self):
        self.lw = None
        self.rd = {}


class Prog:
    ENGS = ("pe", "act", "dve", "pool", "sp")

    def __init__(self, nc, stack):
        self.nc = nc
        self.stack = stack
        self.ops = {e: [] for e in self.ENGS}
        self.cnt = {}
        self.seen = {e: {} for e in self.ENGS}
        self.sems = {}
        self.noself = {"pe"}

    def sem(self, k):
        if k not in self.sems:
            self.sems[k] = self.stack.enter_context(self.nc.semaphore("s_" + k))
        return self.sems[k]

    def _deps(self, eng, reads, writes):
        need = {}

        def add(k, v):
            if need.get(k, 0) < v:
                need[k] = v
        for r in reads:
            if r.lw is not None:
                add(*r.lw)
        for w in writes:
            if w.lw is not None:
                add(*w.lw)
            for k, v in w.rd.items():
                add(k, v)
        seen = self.seen[eng]
        out = []
        for k, v in need.items():
            if k == eng and eng in self.noself:
                continue
            if seen.get(k, 0) < v:
                seen[k] = v
                out.append((k, v))
        return out

    def _fin(self, key, val, reads, writes):
        for r in reads:
            if r.rd.get(key, 0) < val:
                r.rd[key] = val
        for w in writes:
            w.lw = (key, val)
            w.rd = {}

    def op(self, eng, fn, reads=(), writes=()):
        waits = self._deps(eng, reads, writes)
        self.cnt[eng] = self.cnt.get(eng, 0) + 1
        self.ops[eng].append((waits, fn, eng, 1))
        self._fin(eng, self.cnt[eng], reads, writes)

    def dma(self, q, chan, fn, reads=(), writes=()):
        waits = self._deps(q, reads, writes)
        self.cnt[chan] = self.cnt.get(chan, 0) + 16
        self.ops[q].append((waits, fn, chan, 16))
        self._fin(chan, self.cnt[chan], reads, writes)

    def wait_all(self, eng, keys):
        waits = []
        for k in keys:
            v = self.cnt.get(k, 0)
            if v and self.seen[eng].get(k, 0) < v:
                self.seen[eng][k] = v
                waits.append((k, v))
        self.ops[eng].append((waits, None, None, 0))

    def barrier(self):
        keys = list(self.cnt.keys())
        for e in self.ENGS:
            self.wait_all(e, keys)

    def emit(self):
        for k in list(self.cnt.keys()):
            self.sem(k)
        sems = self.sems
        ops = self.ops
        self.ops = {e: [] for e in self.ENGS}

        def run(name):
            def body(e):
                for waits, fn, key, inc in ops[name]:
                    for k, v in waits:
                        e.wait_ge(sems[k], v)
                    if fn is not None:
                        fn(e).then_inc(sems[key], inc)
            return body
        with self.nc.Block() as block:
            block.tensor(run("pe"))
            block.scalar(run("act"))
            block.vector(run("dve"))
            block.gpsimd(run("pool"))
            block.sync(run("sp"))


class Cfg:
    def __init__(self, T=16384, PAST=2048, ST=32, SBC=2):
        self.T = T
        self.TQ = 512
        self.NT = T // 512
        self.NJ = self.NT // 4
        self.TO = self.NJ * 512
        self.ST = ST
        self.SBC = SBC
        self.NS = ST * SBC
        self.NSP = 256
        self.TOT = self.TO + self.NSP
        self.PAST = PAST
        self.NPB = PAST // 128


def build(cfg):
    nc = bass.Bass("TRN2", target_bir_lowering=False)
    T, TO, TOT, NS, NJ, PAST, NPB, ST, SBC = (cfg.T, cfg.TO, cfg.TOT, cfg.NS, cfg.NJ, cfg.PAST,
                                              cfg.NPB, cfg.ST, cfg.SBC)

    def din(name, shape, dt=F32):
        return nc.dram_tensor(name, list(shape), dt, kind="ExternalInput").ap()

    def dout(name, shape, dt=F32):
        return nc.dram_tensor(name, list(shape), dt, kind="ExternalOutput").ap()

    def dscr(name, shape, dt):
        return nc.dram_tensor(name, list(shape), dt, kind="Internal").ap()

    xT_b = din("xT_b", [128, KC, T])
    xT_o = din("xT_o", [128, KC, TOT])
    ropeC_b = din("ropeC_b", [128, T])
    ropeS_b = din("ropeS_b", [128, T])
    ropeC_o = din("ropeC_o", [128, TOT])
    ropeS_o = din("ropeS_o", [128, TOT])
    mdiff_d = din("mdiff", [128, 16, 512], BF16)
    msb_d = din("msb", [128, 16, 512], BF16)
    msamp_d = din("msamp", [128, 256], BF16)
    tri_d = din("tri", [128, 128], BF16)
    onesb_d = din("onesb", [128, 128], BF16)
    onesf_d = din("onesf", [128, 128])
    identf_d = din("identf", [128, 128])
    gains_d = din("gains", [128, 4, KC])
    ghead_d = din("ghead", [128, 1])
    lams_d = din("lams", [128, 4, 64])
    win_r = din("win_r", [64, 128, 2048])
    wgate_r = din("wgate_r", [32, 128, 2048])
    wpa_r = din("wpa_r", [16, 128, 1024])
    wpb_r = din("wpb_r", [16, 128, 1024])
    wout_r = din("wout_r", [16, 128, 2048])
    wup_r = din("wup_r", [64, 128, 2048])
    wdown_r = din("wdown_r", [16, 128, 8192])
    cdk = din("cdk", [SBC, 8, 128, PAST])
    cdv = din("cdv", [SBC, 8, 128, NPB, 128])
    csk = din("csk", [SBC, 8, 128, PAST])
    csv = din("csv", [SBC, 8, 128, NPB, 128])

    yT = dout("yT", [128, KC, TOT])
    kaT_o = dout("kaT_o", [8, 128, TOT])
    vaT_o = dout("vaT_o", [8, 128, TOT])
    kbT_o = dout("kbT_o", [8, 128, TOT])
    vbT_o = dout("vbT_o", [8, 128, TOT])

    hT_b = dscr("hT_b", [128, KC, T], BF16)
    hT_o = dscr("hT_o", [128, KC, TOT], BF16)
    win_s = dscr("win_s", [64, 128, 2048], BF16)
    wgate_s = dscr("wgate_s", [32, 128, 2048], BF16)
    wpa_s = dscr("wpa_s", [16, 128, 1024], BF16)
    wpb_s = dscr("wpb_s", [16, 128, 1024], BF16)
    wout_s = dscr("wout_s", [16, 128, 2048], BF16)
    wup_s = dscr("wup_s", [64, 128, 2048], BF16)
    wdown_s = dscr("wdown_s", [16, 128, 8192], BF16)
    oaT_s = dscr("oaT_s", [128, 8, TOT], BF16)
    obT_s = dscr("obT_s", [128, 8, TOT], BF16)

    out_chans = set()

    with contextlib.ExitStack() as top:
        P = Prog(nc, top)

        def sb(name, shape, dt, st=top):
            return st.enter_context(nc.sbuf_tensor("sb_" + name, list(shape), dt))

        tri = sb("tri", [128, 128], BF16)
        onesb = sb("onesb", [128, 128], BF16)
        onesf = sb("onesf", [128, 128], F32)
        identf = sb("identf", [128, 128], F32)
        gains = sb("gains", [128, 4, KC], F32)
        ghead = sb("ghead", [128, 1], F32)
        lams = sb("lams", [128, 4, 64], F32)
        lamt = sb("lamt", [128, 4, 64], F32)
        lamv = sb("lamv", [128, 4], F32)
        neglam = sb("neglam", [128, 1], F32)
        msamp = sb("msamp", [128, 256], BF16)
        rC = Res()
        for t_, d_ in ((tri, tri_d), (onesb, onesb_d), (onesf, onesf_d), (identf, identf_d),
                       (gains, gains_d), (ghead, ghead_d), (lams, lams_d), (msamp, msamp_d)):
            P.dma("sp", "const", lambda e, t_=t_, d_=d_: e.dma_start(out=t_[:], in_=d_), writes=[rC])
        P.op("dve", lambda e: e.tensor_tensor(out=lamt[:, 0, :], in0=lams[:, 0, :], in1=lams[:, 1, :], op=ALU.mult), reads=[rC], writes=[rC])
        P.op("dve", lambda e: e.tensor_tensor(out=lamt[:, 1, :], in0=lams[:, 2, :], in1=lams[:, 3, :], op=ALU.mult), reads=[rC], writes=[rC])
        P.op("dve", lambda e: e.tensor_reduce(out=lamv[:, 0:2], in_=lamt[:, 0:2, :], axis=mybir.AxisListType.X, op=ALU.add), reads=[rC], writes=[rC])
        P.op("act", lambda e: e.activation(out=lamv[:, 2:4], in_=lamv[:, 0:2], func=AF.Exp), reads=[rC], writes=[rC])
        P.op("dve", lambda e: e.scalar_tensor_tensor(out=neglam[:], in0=lamv[:, 3:4], scalar=-LAM_INIT, in1=lamv[:, 2:3], op0=ALU.add, op1=ALU.subtract), reads=[rC], writes=[rC])
        P.op("dve", lambda e: e.tensor_scalar(out=ghead[:], in0=ghead[:], scalar1=1.0 - LAM_INIT, scalar2=None, op0=ALU.mult), reads=[rC], writes=[rC])

        pall = top.enter_context(nc.psum_tensor("pall", [128, 8, 512], F32))
        rP = [Res() for _ in range(8)]

        def rstd_from(ps_ap, dst_ap, n_feat, eng_reads, eng_writes):
            P.op("dve", lambda e: e.tensor_scalar(out=dst_ap, in0=ps_ap, scalar1=1.0 / n_feat, scalar2=EPS, op0=ALU.mult, op1=ALU.add), reads=eng_reads, writes=eng_writes)
            P.op("act", lambda e: e.activation(out=dst_ap, in_=dst_ap, func=AF.Ln), reads=eng_writes, writes=eng_writes)
            P.op("act", lambda e: e.activation(out=dst_ap, in_=dst_ap, func=AF.Exp, scale=-0.5), reads=eng_writes, writes=eng_writes)

        with contextlib.ExitStack() as s0:
            NSL = 3
            wf = [sb(f"wf{i}", [128, 2048], F32, s0) for i in range(NSL)]
            wb = [sb(f"wb{i}", [128, 2048], BF16, s0) for i in range(NSL)]
            rwf = [Res() for _ in range(NSL)]
            rwb = [Res() for _ in range(NSL)]
            cnt = 0
            for src, dst, nb, w in ((win_r, win_s, 64, 2048), (wgate_r, wgate_s, 32, 2048), (wpa_r, wpa_s, 16, 1024),
                                    (wpb_r, wpb_s, 16, 1024), (wout_r, wout_s, 16, 2048), (wup_r, wup_s, 64, 2048),
                                    (wdown_r, wdown_s, 16, 8192)):
                for b_ in range(nb):
                    for c0 in range(0, w, 2048):
                        cw = min(2048, w - c0)
                        s = cnt % NSL
                        P.dma("sp", f"wl{s}", lambda e, s=s, src=src, b_=b_, c0=c0, cw=cw: e.dma_start(out=wf[s][:, 0:cw], in_=src[b_, :, c0:c0 + cw]), writes=[rwf[s]])
                        ce = ("dve", "pool", "act")[cnt % 3]
                        if ce == "act":
                            P.op("act", lambda e, s=s, cw=cw: e.copy(out=wb[s][:, 0:cw], in_=wf[s][:, 0:cw]), reads=[rwf[s]], writes=[rwb[s]])
                        else:
                            P.op(ce, lambda e, s=s, cw=cw: e.tensor_copy(out=wb[s][:, 0:cw], in_=wf[s][:, 0:cw]), reads=[rwf[s]], writes=[rwb[s]])
                        P.dma("pool", f"ws{s}", lambda e, s=s, dst=dst, b_=b_, c0=c0, cw=cw: e.dma_start(out=dst[b_, :, c0:c0 + cw], in_=wb[s][:, 0:cw]), reads=[rwb[s]], writes=[])
                        cnt += 1
            wstore_keys = [f"ws{s}" for s in range(NSL)]

            NH = 256
            xs = [sb(f"xs{i}", [128, KC, NH], F32, s0) for i in range(2)]
            sq = sb("sq0", [128, KC, NH], F32, s0)
            hb = [sb(f"hb{i}", [128, KC, NH], BF16, s0) for i in range(2)]
            rs0 = sb("rs0", [128, NH], F32, s0)
            rxs = [Res(), Res()]
            rsq = Res()
            rhb = [Res(), Res()]
            rrs = Res()
            rHB = Res()
            rHO = Res()
            it = 0
            for src, dst, ntok, rdst in ((xT_b, hT_b, T, rHB), (xT_o, hT_o, TOT, rHO)):
                for c0 in range(0, ntok, NH):
                    n = min(NH, ntok - c0)
                    s = it % 2
                    P.dma("sp", f"xl{s}", lambda e, s=s, src=src, c0=c0, n=n: e.dma_start(out=xs[s][:, :, 0:n], in_=src[:, :, c0:c0 + n]), writes=[rxs[s]])
                    P.op("act", lambda e, s=s, n=n: e.activation(out=sq[:, :, 0:n], in_=xs[s][:, :, 0:n], func=AF.Square), reads=[rxs[s]], writes=[rsq])
                    for kc in range(KC):
                        P.op("pe", lambda e, kc=kc, n=n: e.matmul(out=pall[:, 0, 0:n], lhsT=onesf[:], rhs=sq[:, kc, 0:n], start=(kc == 0), stop=(kc == KC - 1)),
                             reads=[rsq, rC], writes=[rP[0]])
                    rstd_from(pall[:, 0, 0:n], rs0[:, 0:n], D, [rP[0]], [rrs])
                    for kc in range(KC):
                        P.op("dve", lambda e, s=s, kc=kc, n=n: e.scalar_tensor_tensor(out=hb[s][:, kc, 0:n], in0=xs[s][:, kc, 0:n], scalar=gains[:, 0, kc:kc + 1], in1=rs0[:, 0:n], op0=ALU.mult, op1=ALU.mult),
                             reads=[rxs[s], rrs, rC], writes=[rhb[s]])
                    P.dma("pool", f"hs{s}", lambda e, s=s, dst=dst, c0=c0, n=n: e.dma_start(out=dst[:, :, c0:c0 + n], in_=hb[s][:, :, 0:n]), reads=[rhb[s]], writes=[rdst])
                    it += 1
            hstore_keys = ["hs0", "hs1"]
            P.barrier()

        rOA = Res()
        rOB = Res()
        ostore_keys = []
        import os
        KSTOP = int(os.environ.get('KSTOP', '9'))
        KATT = int(os.environ.get('KATT', '1'))
        KP = int(os.environ.get('KP', '9'))
        KSAMP = int(os.environ.get('KSAMP', '1'))
        KTYP = [int(c) for c in os.environ.get('KTYP', '01')]
        with contextlib.ExitStack() as s1:
            NK = 256
            KT = sb("KT", [128, T], BF16, s1)
            V = sb("V", [128, T // 128, 128], BF16, s1)
            QT = sb("QT", [128, TOT], BF16, s1)
            wh = [sb(f"wh{i}", [128, KC, 128], BF16, s1) for i in range(5)]
            ht = [sb(f"ht{i}", [128, KC, NK], BF16, s1) for i in range(2)]
            tC = [sb(f"tC{i}", [128, NK], F32, s1) for i in range(2)]
            tS = [sb(f"tS{i}", [128, NK], F32, s1) for i in range(2)]
            mt = sb("mt", [128, 16, 512], BF16, s1)
            tA = sb("tA", [128, NK], F32, s1)
            tB = sb("tB", [128, NK], F32, s1)
            kst = [sb(f"kst{i}", [128, NK], F32, s1) for i in range(2)]
            vst = [sb(f"vst{i}", [128, NK], F32, s1) for i in range(2)]
            KTs = sb("KTs", [128, 256], BF16, s1)
            ostage = sb("ostage", [128, 256], BF16, s1)
            rost = Res()
            Vs = sb("Vs", [128, SBC, 128], BF16, s1)
            NSS = 4
            Pt = [sb(f"Pt{i}", [128, 2, 512], BF16, s1) for i in range(NSS)]
            Et = [sb(f"Et{i}", [128, 512], F32, s1) for i in range(NSS)]
            Lt = [sb(f"Lt{i}", [128, 512], BF16, s1) for i in range(NSS)]
            Tt = [sb(f"Tt{i}", [128, 512], F32, s1) for i in range(NSS)]
            At = [sb(f"At{i}", [128, 512], BF16, s1) for i in range(NSS)]
            carry = sb("carry", [128, 512], F32, s1)
            acc = sb("acc", [128, 2, 512], F32, s1)
            racc = Res()
            f1 = sb("f1", [128, 512], F32, s1)
            f2 = sb("f2", [128, 512], F32, s1)
            f3 = sb("f3", [128, 512], F32, s1)
            f4 = sb("f4", [128, 512], F32, s1)
            ob16 = [sb(f"ob16{i}", [128, 512], BF16, s1) for i in range(2)]
            ckf = sb("ckf", [128, PAST // 2], F32, s1)
            ckb = sb("ckb", [128, PAST], BF16, s1)
            cvf = sb("cvf", [128, NPB // 2, 128], F32, s1)
            cvb = sb("cvb", [128, NPB, 128], BF16, s1)

            rKT, rV, rQT, rmt, rtA, rtB, rKTs, rVs, rcarry = [Res() for _ in range(9)]
            rwh = [Res() for _ in range(5)]
            rht = [Res(), Res()]
            rtab = [Res(), Res()]
            rkst = [Res(), Res()]
            rvst = [Res(), Res()]
            rPt, rEt, rLt, rTt, rAt = [[Res() for _ in range(NSS)] for _ in range(5)]
            rf = [Res() for _ in range(4)]
            rob16 = [Res(), Res()]
            rckf, rckb, rcvf, rcvb = [Res() for _ in range(4)]
            state = {"ht": 0, "st": 0, "sl": 0, "ob": 0, "s4": 0, "sp2": 0}
            P.op("pool", lambda e: e.memset(ostage[:], 0.0), writes=[rost])

            def project_tile(typ, head, src, c0, n, own):
                s = state["ht"] % 2
                state["ht"] += 1
                rsrc = rHO if own else rHB
                P.dma("sp", f"htl{s}", lambda e: e.dma_start(out=ht[s][:, :, 0:n], in_=src[:, :, c0:c0 + n]), reads=[rsrc], writes=[rht[s]])
                if typ == 0:
                    cs, ss_ = (ropeC_o, ropeS_o) if own else (ropeC_b, ropeS_b)
                    P.dma("sp", f"tbl{s}", lambda e: e.dma_start(out=tC[s][:, 0:n], in_=cs[:, c0:c0 + n]), writes=[rtab[s]])
                    P.dma("sp", f"tbl{s}", lambda e: e.dma_start(out=tS[s][:, 0:n], in_=ss_[:, c0:c0 + n]), writes=[rtab[s]])

                def chain(widx, bank):
                    for kc in range(KC):
                        P.op("pe", lambda e, kc=kc: e.matmul(out=pall[:, bank, 0:n], lhsT=wh[widx][:, kc, :], rhs=ht[s][:, kc, 0:n], start=(kc == 0), stop=(kc == KC - 1)),
                             reads=[rwh[widx], rht[s]], writes=[rP[bank]])

                def roped(w0, dst_fn, dst_res, extra=None):
                    chain(w0, 4)
                    chain(w0 + 1, 5)
                    P.op("dve", lambda e: e.tensor_tensor(out=tA[:, 0:n], in0=pall[:, 4, 0:n], in1=tC[s][:, 0:n], op=ALU.mult), reads=[rP[4], rtab[s]], writes=[rtA])
                    P.op("dve", lambda e: e.tensor_tensor(out=tB[:, 0:n], in0=pall[:, 5, 0:n], in1=tS[s][:, 0:n], op=ALU.mult), reads=[rP[5], rtab[s]], writes=[rtB])
                    P.op("pool", lambda e: e.tensor_tensor(out=dst_fn(), in0=tA[:, 0:n], in1=tB[:, 0:n], op=ALU.add), reads=[rtA, rtB], writes=[dst_res])
                    if extra is not None:
                        P.op("pool", lambda e: e.tensor_tensor(out=extra[0](), in0=tA[:, 0:n], in1=tB[:, 0:n], op=ALU.add), reads=[rtA, rtB], writes=[extra[1]])

                def plain(widx, dst_fn, dst_res, extra=None):
                    chain(widx, 4)
                    P.op("act", lambda e: e.copy(out=dst_fn(), in_=pall[:, 4, 0:n]), reads=[rP[4]], writes=[dst_res])
                    if extra is not None:
                        P.op("pool", lambda e: e.tensor_copy(out=extra[0](), in_=dst_fn()), reads=[dst_res], writes=[extra[1]])

                samp = own and c0 >= TO
                if KP < 1 or (own and KP < 4) or (samp and KP < 6):
                    return
                if typ == 0:
                    iq, ik, iv = 0, 2, 4
                else:
                    iq, ik, iv = 0, 1, 2
                so = state["st"] % 2
                if own:
                    state["st"] += 1
                    if typ == 0:
                        roped(iq, lambda: QT[:, c0:c0 + n], rQT)
                    else:
                        plain(iq, lambda: QT[:, c0:c0 + n], rQT)
                    ex = (lambda: KTs[:, 0:n], rKTs) if samp else None
                    if typ == 0:
                        roped(ik, lambda: kst[so][:, 0:n], rkst[so], ex)
                    else:
                        plain(ik, lambda: kst[so][:, 0:n], rkst[so], ex)
                    kout = kaT_o if typ == 0 else kbT_o
                    P.dma("pool", f"kso{so}", lambda e: e.dma_start(out=kout[head, :, c0:c0 + n], in_=kst[so][:, 0:n]), reads=[rkst[so]])
                    out_chans.add(f"kso{so}")
                else:
                    if typ == 0:
                        roped(ik, lambda: KT[:, c0:c0 + n], rKT)
                    else:
                        plain(ik, lambda: KT[:, c0:c0 + n], rKT)
                if KP < 2:
                    return
                chain(iv, 6)
                P.op("act", lambda e: e.copy(out=vst[so][:, 0:n], in_=pall[:, 6, 0:n]), reads=[rP[6]], writes=[rvst[so]])
                if own:
                    vout = vaT_o if typ == 0 else vbT_o
                    P.dma("pool", f"vso{so}", lambda e: e.dma_start(out=vout[head, :, c0:c0 + n], in_=vst[so][:, 0:n]), reads=[rvst[so]])
                    out_chans.add(f"vso{so}")
                    if samp and KP >= 7:
                        for sbi in range(SBC):
                            P.op("pe", lambda e, sbi=sbi: e.transpose(out=pall[0:ST, 7, sbi * 128:(sbi + 1) * 128], in_=vst[so][:, sbi * ST:(sbi + 1) * ST], identity=identf[:]),
                                 reads=[rvst[so], rC], writes=[rP[7]])
                            P.op("dve", lambda e, sbi=sbi: e.tensor_copy(out=Vs[0:ST, sbi, :], in_=pall[0:ST, 7, sbi * 128:(sbi + 1) * 128]), reads=[rP[7]], writes=[rVs])
                elif KP >= 3:
                    state["st"] += 1

                    def deferred():
                        nb_ = n // 128
                        for i in range(nb_):
                            P.op("pe", lambda e, i=i: e.transpose(out=pall[:, 7, i * 128:(i + 1) * 128], in_=vst[so][:, i * 128:(i + 1) * 128], identity=identf[:]),
                                 reads=[rvst[so], rC], writes=[rP[7]])
                        kb0 = c0 // 128
                        for i in range(nb_):
                            P.op("dve", lambda e, i=i: e.tensor_copy(out=V[:, kb0 + i, :], in_=pall[:, 7, i * 128:(i + 1) * 128]), reads=[rP[7]], writes=[rV])
                    return deferred
                return None

            def attn_diff(head, q0, n, blocks):
                nblk = len(blocks)

                P.op("pool", lambda e: e.memset(acc[:, :, 0:n], 0.0), writes=[racc])

                info = {}

                def stageA(bi, kT, vap, nk, mask, rds):
                    b0 = 2 + 2 * (state["sl"] % 3)
                    state["sl"] += 1
                    sl = state["s4"] % NSS
                    state["s4"] += 1
                    info[bi] = sl
                    for m in range(2):
                        P.op("pe", lambda e, m=m: e.matmul(out=pall[0:nk, b0 + m, 0:n], lhsT=kT(m), rhs=QT[m * 64:(m + 1) * 64, q0:q0 + n], start=True, stop=True),
                             reads=rds + [rQT], writes=[rP[b0 + m]])
                    P.op("act", lambda e: e.activation(out=Pt[sl][0:nk, :, 0:n], in_=pall[0:nk, b0:b0 + 2, 0:n], func=AF.Exp, scale=0.125),
                         reads=[rP[b0], rP[b0 + 1]], writes=[rPt[sl]])
                    if mask is not None:
                        for m in range(2):
                            P.op(("pool", "dve")[m], lambda e, m=m: e.tensor_tensor(out=Pt[sl][0:nk, m, 0:n], in0=Pt[sl][0:nk, m, 0:n], in1=mask, op=ALU.mult),
                                 reads=[rPt[sl], rmt], writes=[rPt[sl]])

                def stageB(bi, kT, vap, nk, mask, rds):
                    sl = info[bi]
                    st_, sp_ = (bi == 0), (bi == nblk - 1)
                    for m in range(2):
                        P.op("pe", lambda e, m=m: e.matmul(out=pall[:, m, 0:n], lhsT=vap, rhs=Pt[sl][0:nk, m, 0:n], start=st_, stop=sp_),
                             reads=rds + [rPt[sl]], writes=[rP[m]])
                    P.op("dve", lambda e: e.tensor_tensor(out=acc[0:nk, :, 0:n], in0=acc[0:nk, :, 0:n], in1=Pt[sl][0:nk, :, 0:n], op=ALU.add), reads=[racc, rPt[sl]], writes=[racc])
                SK = 2
                for t in range(nblk + SK):
                    if t < nblk:
                        stageA(t, *blocks[t])
                    if t - SK >= 0:
                        stageB(t - SK, *blocks[t - SK])
                a, b, o, q_ = f1[:, 0:n], f2[:, 0:n], f3[:, 0:n], f4[:, 0:n]
                for m in range(2):
                    P.op("pe", lambda e, m=m: e.matmul(out=pall[:, 2 + m, 0:n], lhsT=onesf[:], rhs=acc[:, m, 0:n], start=True, stop=True), reads=[rC, racc], writes=[rP[2 + m]])
                P.op("dve", lambda e: e.reciprocal(out=q_, in_=pall[:, 2, 0:n]), reads=[rP[2]], writes=[rf[3]])
                P.op("dve", lambda e: e.tensor_tensor(out=a, in0=pall[:, 0, 0:n], in1=q_, op=ALU.mult), reads=[rP[0], rf[3]], writes=[rf[0]])
                P.op("dve", lambda e: e.reciprocal(out=q_, in_=pall[:, 3, 0:n]), reads=[rP[3]], writes=[rf[3]])
                P.op("dve", lambda e: e.tensor_tensor(out=b, in0=pall[:, 1, 0:n], in1=q_, op=ALU.mult), reads=[rP[1], rf[3]], writes=[rf[1]])
                P.op("dve", lambda e: e.scalar_tensor_tensor(out=o, in0=b, scalar=neglam[:, 0:1], in1=a, op0=ALU.mult, op1=ALU.add), reads=[rf[0], rf[1], rC], writes=[rf[2]])
                P.op("act", lambda e: e.activation(out=a, in_=o, func=AF.Square), reads=[rf[2]], writes=[rf[0]])
                P.op("pe", lambda e: e.matmul(out=pall[:, 4, 0:n], lhsT=onesf[:], rhs=a, start=True, stop=True), reads=[rC, rf[0]], writes=[rP[4]])
                rstd_from(pall[:, 4, 0:n], b, 128, [rP[4]], [rf[1]])
                so = state["ob"] % 2
                state["ob"] += 1
                P.op("dve", lambda e: e.scalar_tensor_tensor(out=ob16[so][:, 0:n], in0=o, scalar=ghead[:, 0:1], in1=b, op0=ALU.mult, op1=ALU.mult), reads=[rf[2], rf[1], rC], writes=[rob16[so]])
                if q0 >= TO:
                    P.op("pool", lambda e: e.tensor_copy(out=ostage[:, q0 - TO:q0 - TO + n], in_=ob16[so][:, 0:n]), reads=[rob16[so]], writes=[rost])
                    if q0 + n == TO + NS:
                        P.dma("pool", "osst", lambda e: e.dma_start(out=oaT_s[:, head, TO:TO + 256], in_=ostage[:]), reads=[rost], writes=[])
                else:
                    P.dma("pool", f"oas{so}", lambda e: e.dma_start(out=oaT_s[:, head, q0:q0 + n], in_=ob16[so][:, 0:n]), reads=[rob16[so]], writes=[])

            def attn_sb(head, q0, n, blocks):
                nblk = len(blocks)
                P.op("pool", lambda e: e.memset(carry[:, 0:n], 0.0), writes=[rcarry])
                sc = 128.0 ** -0.5

                info = {}

                def stageA(bi, kT, vap, nk, mask, rds):
                    bz = 1 + state["sl"] % 2
                    state["sl"] += 1
                    sl = state["s4"] % NSS
                    state["s4"] += 1
                    info[bi] = sl
                    P.op("pe", lambda e: e.matmul(out=pall[0:nk, bz, 0:n], lhsT=kT(0), rhs=QT[:, q0:q0 + n], start=True, stop=True),
                         reads=rds + [rQT], writes=[rP[bz]])
                    P.op("act", lambda e: e.activation(out=Et[sl][0:nk, 0:n], in_=pall[0:nk, bz, 0:n], func=AF.Exp, scale=sc), reads=[rP[bz]], writes=[rEt[sl]])
                    P.op("act", lambda e: e.activation(out=Lt[sl][0:nk, 0:n], in_=Et[sl][0:nk, 0:n], func=AF.Ln, bias=1.0, scale=1.0), reads=[rEt[sl]], writes=[rLt[sl]])
                    if mask is not None:
                        P.op("pool", lambda e: e.tensor_tensor(out=Lt[sl][0:nk, 0:n], in0=Lt[sl][0:nk, 0:n], in1=mask, op=ALU.mult), reads=[rLt[sl], rmt], writes=[rLt[sl]])

                def stageB(bi, kT, vap, nk, mask, rds):
                    sl = info[bi]
                    slp = state["sp2"] % 2
                    state["sp2"] += 1
                    bc, bs_ = 3 + slp, 5 + slp
                    P.op("pe", lambda e: e.matmul(out=pall[0:nk, bc, 0:n], lhsT=tri[0:nk, 0:nk], rhs=Lt[sl][0:nk, 0:n], start=True, stop=True), reads=[rC, rLt[sl]], writes=[rP[bc]])
                    P.op("pe", lambda e: e.matmul(out=pall[:, bs_, 0:n], lhsT=onesb[0:nk, :], rhs=Lt[sl][0:nk, 0:n], start=True, stop=True), reads=[rC, rLt[sl]], writes=[rP[bs_]])
                    P.op("dve", lambda e: e.tensor_tensor(out=Tt[sl][0:nk, 0:n], in0=pall[0:nk, bc, 0:n], in1=carry[0:nk, 0:n], op=ALU.add), reads=[rP[bc], rcarry], writes=[rTt[sl]])
                    P.op("dve", lambda e: e.tensor_tensor(out=carry[:, 0:n], in0=pall[:, bs_, 0:n], in1=carry[:, 0:n], op=ALU.add), reads=[rP[bs_], rcarry], writes=[rcarry])
                    P.op("act", lambda e: e.activation(out=Tt[sl][0:nk, 0:n], in_=Tt[sl][0:nk, 0:n], func=AF.Exp, scale=-1.0), reads=[rTt[sl]], writes=[rTt[sl]])
                    P.op("pool", lambda e: e.tensor_tensor(out=At[sl][0:nk, 0:n], in0=Et[sl][0:nk, 0:n], in1=Tt[sl][0:nk, 0:n], op=ALU.mult), reads=[rEt[sl], rTt[sl]], writes=[rAt[sl]])
                    if mask is not None:
                        P.op("pool", lambda e: e.tensor_tensor(out=At[sl][0:nk, 0:n], in0=At[sl][0:nk, 0:n], in1=mask, op=ALU.mult), reads=[rAt[sl], rmt], writes=[rAt[sl]])

                def stageC(bi, kT, vap, nk, mask, rds):
                    sl = info[bi]
                    P.op("pe", lambda e: e.matmul(out=pall[:, 0, 0:n], lhsT=vap, rhs=At[sl][0:nk, 0:n], start=(bi == 0), stop=(bi == nblk - 1)),
                         reads=rds + [rAt[sl]], writes=[rP[0]])
                for t in range(nblk + 2):
                    if t < nblk:
                        stageA(t, *blocks[t])
                    if 0 <= t - 1 < nblk:
                        stageB(t - 1, *blocks[t - 1])
                    if 0 <= t - 2 < nblk:
                        stageC(t - 2, *blocks[t - 2])
                so = state["ob"] % 2
                state["ob"] += 1
                P.op("act", lambda e: e.copy(out=ob16[so][:, 0:n], in_=pall[:, 0, 0:n]), reads=[rP[0]], writes=[rob16[so]])
                if q0 >= TO:
                    P.op("pool", lambda e: e.tensor_copy(out=ostage[:, q0 - TO:q0 - TO + n], in_=ob16[so][:, 0:n]), reads=[rob16[so]], writes=[rost])
                    if q0 + n == TO + NS:
                        P.dma("pool", "osst", lambda e: e.dma_start(out=obT_s[:, head, TO:TO + 256], in_=ostage[:]), reads=[rost], writes=[])
                else:
                    P.dma("pool", f"obs{so}", lambda e: e.dma_start(out=obT_s[:, head, q0:q0 + n], in_=ob16[so][:, 0:n]), reads=[rob16[so]], writes=[])

            for typ in (KTYP if KSTOP >= 2 else []):
                md = mdiff_d if typ == 0 else msb_d
                P.dma("sp", "mtl", lambda e, md=md: e.dma_start(out=mt[:], in_=md), writes=[rmt])
                for head in range(8):
                    nw = 5 if typ == 0 else 3
                    base = 5 * head if typ == 0 else 40 + 3 * head
                    for i in range(nw):
                        P.dma("sp", f"whl{i}", lambda e, i=i, base=base: e.dma_start(out=wh[i][:], in_=win_s[base + i, :, :].rearrange("p (a b) -> p a b", b=128)), writes=[rwh[i]])
                    pend = None
                    for c0 in range(0, T, NK):
                        d_ = project_tile(typ, head, hT_b, c0, NK, False)
                        if pend is not None:
                            pend()
                        pend = d_
                    if pend is not None:
                        pend()
                    for c0 in range(0, TOT, NK):
                        project_tile(typ, head, hT_o, c0, min(NK, TOT - c0), True)
                    afn = attn_diff if typ == 0 else attn_sb
                    for j in (range(NJ) if KATT else []):
                        blocks = []
                        for kb in range(16 * j + 16):
                            if typ == 0:
                                kT = (lambda m, kb=kb: KT[m * 64:(m + 1) * 64, kb * 128:(kb + 1) * 128])
                            else:
                                kT = (lambda m, kb=kb: KT[:, kb * 128:(kb + 1) * 128])
                            mask = mt[:, kb - 16 * j, :] if kb >= 16 * j else None
                            blocks.append((kT, V[:, kb, :], 128, mask, [rKT, rV]))
                        if typ == 1:
                            blocks = blocks[::-1]
                        afn(head, j * 512, 512, blocks)
                    ck_d, cv_d = (cdk, cdv) if typ == 0 else (csk, csv)
                    for sbi in (range(SBC) if KSAMP else []):
                        for hf in range(2):
                            P.dma("sp", "ckl", lambda e, sbi=sbi, ck_d=ck_d, head=head, hf=hf: e.dma_start(out=ckf[:], in_=ck_d[sbi, head, :, hf * (PAST // 2):(hf + 1) * (PAST // 2)]), writes=[rckf])
                            P.dma("sp", "cvl", lambda e, sbi=sbi, cv_d=cv_d, head=head, hf=hf: e.dma_start(out=cvf[:], in_=cv_d[sbi, head, :, hf * (NPB // 2):(hf + 1) * (NPB // 2), :]), writes=[rcvf])
                            P.op("dve", lambda e, hf=hf: e.tensor_copy(out=ckb[:, hf * (PAST // 2):(hf + 1) * (PAST // 2)], in_=ckf[:]), reads=[rckf], writes=[rckb])
                            P.op("pool", lambda e, hf=hf: e.tensor_copy(out=cvb[:, hf * (NPB // 2):(hf + 1) * (NPB // 2), :], in_=cvf[:]), reads=[rcvf], writes=[rcvb])
                        blocks = []
                        for kb in range(NPB):
                            if typ == 0:
                                kT = (lambda m, kb=kb: ckb[m * 64:(m + 1) * 64, kb * 128:(kb + 1) * 128])
                            else:
                                kT = (lambda m, kb=kb: ckb[:, kb * 128:(kb + 1) * 128])
                            blocks.append((kT, cvb[:, kb, :], 128, None, [rckb, rcvb]))
                        if typ == 0:
                            kT = (lambda m, sbi=sbi: KTs[m * 64:(m + 1) * 64, sbi * ST:(sbi + 1) * ST])
                            blocks.append((kT, Vs[0:ST, sbi, :], ST, None, [rKTs, rVs]))
                        else:
                            kT = (lambda m, sbi=sbi: KTs[:, sbi * ST:(sbi + 1) * ST])
                            blocks.append((kT, Vs[0:ST, sbi, :], ST, msamp[0:ST, 0:ST], [rKTs, rVs, rC]))
                            blocks = blocks[::-1]
                        afn(head, TO + sbi * ST, ST, blocks)
            P.barrier()

        with contextlib.ExitStack() as s3:
            NQ = 512
            A = sb("A3", [128, KC, NQ], F32, s3)
            B = sb("B3", [128, KC, NQ], F32, s3)
            C = sb("C3", [128, KC, NQ], BF16, s3)
            Fb = sb("F3", [128, 64, NQ], BF16, s3)
            NR = 8
            ring = [sb(f"ring{i}", [128, KC, 128], BF16, s3) for i in range(NR)]
            sga = sb("sga", [128, NQ], F32, s3)
            sgb = sb("sgb", [128, NQ], F32, s3)
            m1 = sb("m1", [128, NQ], F32, s3)
            m2 = sb("m2", [128, NQ], F32, s3)
            sqt = [sb(f"sqt{i}", [128, NQ], F32, s3) for i in range(2)]
            rsd = sb("rsd", [128, NQ], F32, s3)
            tmp3 = [sb(f"tmp3{i}", [128, NQ], F32, s3) for i in range(2)]
            rA, rB, rCC, rFlo, rFhi, rsga, rsgb, rm1, rm2, rrsd = [Res() for _ in range(10)]
            rring = [Res() for _ in range(NR)]
            rsqt = [Res(), Res()]
            rtmp3 = [Res(), Res()]
            st3 = {"r": 0, "q": 0, "t": 0, "pb": 0}

            def wload(src_ap, ncols):
                i = st3["r"] % NR
                st3["r"] += 1
                nch = ncols // 128
                P.dma("sp", f"rg{i}", lambda e: e.dma_start(out=ring[i][:, 0:nch, :], in_=src_ap.rearrange("p (a b) -> p a b", b=128)), writes=[rring[i]])
                return i

            def rF(f):
                return rFlo if f < 32 else rFhi

            def sumsq_accum(src_ap, src_res, n, f, nf):
                q = st3["q"] % 2
                st3["q"] += 1
                P.op("act", lambda e: e.activation(out=sqt[q][:, 0:n], in_=src_ap, func=AF.Square), reads=src_res, writes=[rsqt[q]])
                P.op("pe", lambda e: e.matmul(out=pall[:, 4, 0:n], lhsT=onesf[:], rhs=sqt[q][:, 0:n], start=(f == 0), stop=(f == nf - 1)), reads=[rC, rsqt[q]], writes=[rP[4]])

            def resid_add(n, gi):
                for f in range(KC):
                    t = st3["t"] % 2
                    st3["t"] += 1
                    P.op("dve", lambda e, f=f, t=t: e.scalar_tensor_tensor(out=tmp3[t][:, 0:n], in0=B[:, f, 0:n], scalar=gains[:, gi, f:f + 1], in1=rsd[:, 0:n], op0=ALU.mult, op1=ALU.mult),
                         reads=[rB, rrsd, rC], writes=[rtmp3[t]])
                    P.op("pool", lambda e, f=f, t=t: e.tensor_tensor(out=A[:, f, 0:n], in0=A[:, f, 0:n], in1=tmp3[t][:, 0:n], op=ALU.add), reads=[rA, rtmp3[t]], writes=[rA])

            def tile3(c0, n):
                P.dma("sp", "xa", lambda e, c0=c0, n=n: e.dma_start(out=A[:, :, 0:n], in_=xT_o[:, :, c0:c0 + n]), writes=[rA])
                P.dma("sp", "hc", lambda e, c0=c0, n=n: e.dma_start(out=C[:, :, 0:n], in_=hT_o[:, :, c0:c0 + n]), reads=[rHO], writes=[rCC])
                P.dma("sp", "oab", lambda e, c0=c0, n=n: e.dma_start(out=Fb[:, 0:8, 0:n], in_=oaT_s[:, :, c0:c0 + n]), reads=[rOA], writes=[rFlo])
                P.dma("sp", "oab", lambda e, c0=c0, n=n: e.dma_start(out=Fb[:, 8:16, 0:n], in_=obT_s[:, :, c0:c0 + n]), reads=[rOB], writes=[rFlo])
                for f in range(KC):
                    ia = wload(wgate_s[f, :, :], 2048)
                    ib = wload(wgate_s[16 + f, :, :], 2048)
                    ipa = wload(wpa_s[f, :, :], 1024)
                    ipb = wload(wpb_s[f, :, :], 1024)
                    for kc in range(KC):
                        P.op("pe", lambda e, kc=kc, ia=ia: e.matmul(out=pall[:, 0, 0:n], lhsT=ring[ia][:, kc, :], rhs=C[:, kc, 0:n], start=(kc == 0), stop=(kc == KC - 1)), reads=[rring[ia], rCC], writes=[rP[0]])
                    for kc in range(KC):
                        P.op("pe", lambda e, kc=kc, ib=ib: e.matmul(out=pall[:, 1, 0:n], lhsT=ring[ib][:, kc, :], rhs=C[:, kc, 0:n], start=(kc == 0), stop=(kc == KC - 1)), reads=[rring[ib], rCC], writes=[rP[1]])
                    for h in range(8):
                        P.op("pe", lambda e, h=h, ipa=ipa: e.matmul(out=pall[:, 2, 0:n], lhsT=ring[ipa][:, h, :], rhs=Fb[:, h, 0:n], start=(h == 0), stop=(h == 7)), reads=[rring[ipa], rFlo], writes=[rP[2]])
                    for h in range(8):
                        P.op("pe", lambda e, h=h, ipb=ipb: e.matmul(out=pall[:, 3, 0:n], lhsT=ring[ipb][:, h, :], rhs=Fb[:, 8 + h, 0:n], start=(h == 0), stop=(h == 7)), reads=[rring[ipb], rFlo], writes=[rP[3]])
                    P.op("act", lambda e: e.activation(out=sga[:, 0:n], in_=pall[:, 0, 0:n], func=AF.Sigmoid), reads=[rP[0]], writes=[rsga])
                    P.op("act", lambda e: e.activation(out=sgb[:, 0:n], in_=pall[:, 1, 0:n], func=AF.Sigmoid), reads=[rP[1]], writes=[rsgb])
                    P.op("dve", lambda e: e.tensor_tensor(out=m1[:, 0:n], in0=pall[:, 2, 0:n], in1=sga[:, 0:n], op=ALU.mult), reads=[rP[2], rsga], writes=[rm1])
                    P.op("dve", lambda e: e.tensor_tensor(out=m2[:, 0:n], in0=pall[:, 3, 0:n], in1=sgb[:, 0:n], op=ALU.mult), reads=[rP[3], rsgb], writes=[rm2])
                    P.op("pool", lambda e, f=f: e.tensor_tensor(out=Fb[:, 16 + f, 0:n], in0=m1[:, 0:n], in1=m2[:, 0:n], op=ALU.add), reads=[rm1, rm2], writes=[rFlo])
                for f in range(KC):
                    iw = wload(wout_s[f, :, :], 2048)
                    pb = 5 + st3["pb"] % 2
                    st3["pb"] += 1
                    for kc in range(KC):
                        P.op("pe", lambda e, kc=kc, iw=iw, pb=pb: e.matmul(out=pall[:, pb, 0:n], lhsT=ring[iw][:, kc, :], rhs=Fb[:, 16 + kc, 0:n], start=(kc == 0), stop=(kc == KC - 1)), reads=[rring[iw], rFlo], writes=[rP[pb]])
                    P.op("dve", lambda e, f=f, pb=pb: e.tensor_copy(out=B[:, f, 0:n], in_=pall[:, pb, 0:n]), reads=[rP[pb]], writes=[rB])
                    sumsq_accum(B[:, f, 0:n], [rB], n, f, KC)
                rstd_from(pall[:, 4, 0:n], rsd[:, 0:n], D, [rP[4]], [rrsd])
                resid_add(n, 1)
                for f in range(KC):
                    sumsq_accum(A[:, f, 0:n], [rA], n, f, KC)
                rstd_from(pall[:, 4, 0:n], rsd[:, 0:n], D, [rP[4]], [rrsd])
                for f in range(KC):
                    P.op("dve", lambda e, f=f: e.scalar_tensor_tensor(out=C[:, f, 0:n], in0=A[:, f, 0:n], scalar=gains[:, 2, f:f + 1], in1=rsd[:, 0:n], op0=ALU.mult, op1=ALU.mult), reads=[rA, rrsd, rC], writes=[rCC])
                for f in range(64):
                    iw = wload(wup_s[f, :, :], 2048)
                    pb = 5 + st3["pb"] % 2
                    st3["pb"] += 1
                    for kc in range(KC):
                        P.op("pe", lambda e, kc=kc, iw=iw, pb=pb: e.matmul(out=pall[:, pb, 0:n], lhsT=ring[iw][:, kc, :], rhs=C[:, kc, 0:n], start=(kc == 0), stop=(kc == KC - 1)), reads=[rring[iw], rCC], writes=[rP[pb]])
                    t = st3["t"] % 2
                    st3["t"] += 1
                    P.op("act", lambda e, pb=pb, t=t: e.activation(out=tmp3[t][:, 0:n], in_=pall[:, pb, 0:n], func=AF.Relu), reads=[rP[pb]], writes=[rtmp3[t]])
                    P.op("pool", lambda e, f=f, t=t: e.tensor_tensor(out=Fb[:, f, 0:n], in0=tmp3[t][:, 0:n], in1=tmp3[t][:, 0:n], op=ALU.mult), reads=[rtmp3[t]], writes=[rF(f)])
                for f in range(KC):
                    pb = 5 + st3["pb"] % 2
                    st3["pb"] += 1
                    for part in range(4):
                        iw = wload(wdown_s[f, :, part * 2048:(part + 1) * 2048], 2048)
                        for kc in range(KC):
                            kk = part * 16 + kc
                            P.op("pe", lambda e, kc=kc, kk=kk, iw=iw, pb=pb: e.matmul(out=pall[:, pb, 0:n], lhsT=ring[iw][:, kc, :], rhs=Fb[:, kk, 0:n], start=(kk == 0), stop=(kk == 63)), reads=[rring[iw], rF(kk)], writes=[rP[pb]])
                    P.op("dve", lambda e, f=f, pb=pb: e.tensor_copy(out=B[:, f, 0:n], in_=pall[:, pb, 0:n]), reads=[rP[pb]], writes=[rB])
                    sumsq_accum(B[:, f, 0:n], [rB], n, f, KC)
                rstd_from(pall[:, 4, 0:n], rsd[:, 0:n], D, [rP[4]], [rrsd])
                resid_add(n, 3)
                P.dma("pool", "yst", lambda e, c0=c0, n=n: e.dma_start(out=yT[:, :, c0:c0 + n], in_=A[:, :, 0:n]), reads=[rA])
                out_chans.add("yst")
            for c0 in (range(0, TOT, NQ) if KSTOP >= 3 else []):
                tile3(c0, min(NQ, TOT - c0))
        P.wait_all("sp", sorted(out_chans))
        P.barrier()
        P.emit()
    return nc


def fblocks(W):
    K, N = W.shape
    return np.ascontiguousarray(W.reshape(K // 128, 128, N // 128, 128).transpose(2, 1, 0, 3).reshape(N // 128, 128, K))


def featmajor(x2d):
    n = x2d.shape[0]
    return np.ascontiguousarray(x2d.T.reshape(KC, 128, n).transpose(1, 0, 2))


def rope_tables(pos):
    inv = (np.float32(THETA) ** (-np.arange(32, dtype=np.float32) / np.float32(32))).astype(np.float32)
    ang = pos.astype(np.float32)[:, None] * inv[None, :]
    cos = np.cos(ang).astype(np.float32)
    sin = np.sin(ang).astype(np.float32)
    p = np.arange(128)
    Cc = cos[:, p % 32].T
    Ss = sin[:, p % 32].T * np.where((p % 64) < 32, -1.0, 1.0).astype(np.float32)[:, None]
    return np.ascontiguousarray(Cc, dtype=np.float32), np.ascontiguousarray(Ss, dtype=np.float32)


_CACHE = {}


def kernel(x_prompt, x_sample, cache_diff_k, cache_diff_v, cache_sb_k, cache_sb_v,
           g_pre_mix, w_in, lambda_q1, lambda_k1, lambda_q2, lambda_k2, g_diff_head,
           w_gate, w_proj_a, w_proj_b, w_out, g_post_mix, g_pre_mlp, w_up, w_down, g_post_mlp):
    f32 = np.float32
    x_prompt = np.asarray(x_prompt, f32)
    x_sample = np.asarray(x_sample, f32)
    B_, T, _ = x_prompt.shape
    SBT, ST, _ = x_sample.shape
    PAST = cache_diff_k.shape[2]
    assert B_ == 2 and SBT == 16
    cfg = Cfg(T=T, PAST=PAST, ST=ST, SBC=2)
    key = (T, PAST, ST)
    if key not in _CACHE:
        _CACHE[key] = build(cfg)
    nc = _CACHE[key]
    NJ, TO, TOT, NS, NPB = cfg.NJ, cfg.TO, cfg.TOT, cfg.NS, cfg.NPB

    w_in0 = np.asarray(w_in, f32)[0]
    perm = (np.arange(64) + 32) % 64
    slices = []
    for h in range(8):
        q = np.concatenate([h * 64 + np.arange(64), (8 + h) * 64 + np.arange(64)])
        qp = np.concatenate([h * 64 + perm, (8 + h) * 64 + perm])
        slices += [q, qp, 1024 + q, 1024 + qp, 2048 + h * 128 + np.arange(128)]
    for h in range(8):
        r = h * 128 + np.arange(128)
        slices += [3072 + r, 4096 + r, 5120 + r]
    win_r = np.stack([fblocks(w_in0[:, c])[0] for c in slices])
    wgate_r = fblocks(np.asarray(w_gate, f32)[0])
    wpa_r = fblocks(np.asarray(w_proj_a, f32)[0])
    wpb_r = fblocks(np.asarray(w_proj_b, f32)[0])
    wout_r = fblocks(np.asarray(w_out, f32)[0])
    wup_r = fblocks(np.asarray(w_up, f32)[0])
    wdown_r = fblocks(np.asarray(w_down, f32)[0])

    def gvec(g):
        return np.asarray(g, f32)[0].reshape(KC, 128).T
    gains = np.ascontiguousarray(np.stack([gvec(g_pre_mix), gvec(g_post_mix), gvec(g_pre_mlp), gvec(g_post_mlp)], axis=1))
    ghead = np.ascontiguousarray(np.asarray(g_diff_head, f32)[0].reshape(128, 1))
    lams = np.ascontiguousarray(np.broadcast_to(np.stack([np.asarray(v, f32)[0] for v in (lambda_q1, lambda_k1, lambda_q2, lambda_k2)])[None], (128, 4, 64)))
    tri = (np.arange(128)[:, None] >= np.arange(128)[None, :]).astype(NPBF)
    onesb = np.ones((128, 128), NPBF)
    onesf = np.ones((128, 128), f32)
    identf = np.eye(128, dtype=f32)
    msamp = np.zeros((128, 256), NPBF)
    msamp[:ST, :ST] = (np.arange(ST)[:, None] < np.arange(ST)[None, :]).astype(NPBF)
    ropeC_b, ropeS_b = rope_tables(np.arange(T))
    xTb = [featmajor(x_prompt[b]) for b in range(2)]
    cdk_all = np.asarray(cache_diff_k, f32)[0]
    cdv_all = np.asarray(cache_diff_v, f32)[0]
    csk_all = np.asarray(cache_sb_k, f32)[0]
    csv_all = np.asarray(cache_sb_v, f32)[0]

    in_maps = []
    own_tok = []
    for c in range(8):
        b, qtr = c // 4, c % 4
        toks = np.concatenate([np.arange((4 * j + qtr) * 512, (4 * j + qtr + 1) * 512) for j in range(NJ)])
        own_tok.append(toks)
        xo = np.concatenate([x_prompt[b][toks], x_sample[2 * c], x_sample[2 * c + 1], np.zeros((cfg.NSP - NS, D), f32)], axis=0)
        pos_o = np.concatenate([toks, PAST + np.arange(ST), PAST + np.arange(ST), np.zeros(cfg.NSP - NS, np.int64)])
        rC_o, rS_o = rope_tables(pos_o)
        kpos = (np.arange(4)[:, None, None] * 512 + np.arange(4)[None, :, None] * 128 + np.arange(128)[None, None, :])
        qpos = qtr * 512 + np.arange(512)
        md = (kpos[..., None] < ((qpos // 64 + 1) * 64)[None, None, None, :])
        ms = (kpos[..., None] < qpos[None, None, None, :])
        mdiff = np.ascontiguousarray(md.reshape(16, 128, 512).transpose(1, 0, 2)).astype(NPBF)
        msb = np.ascontiguousarray(ms.reshape(16, 128, 512).transpose(1, 0, 2)).astype(NPBF)
        sb_ids = [2 * c, 2 * c + 1]
        dk = cdk_all[sb_ids]
        cdk_c = np.concatenate([dk[:, :, 0:8, :], dk[:, :, 8:16, :]], axis=-1)
        cdk_c = np.ascontiguousarray(cdk_c.transpose(0, 2, 3, 1))
        csk_c = np.ascontiguousarray(csk_all[sb_ids].transpose(0, 2, 3, 1))
        def vlay(v):
            return np.ascontiguousarray(v.reshape(2, NPB, 128, 8, 128).transpose(0, 3, 2, 1, 4))
        in_maps.append(dict(
            xT_b=xTb[b], xT_o=featmajor(xo), ropeC_b=ropeC_b, ropeS_b=ropeS_b, ropeC_o=rC_o, ropeS_o=rS_o,
            mdiff=mdiff, msb=msb, msamp=msamp, tri=tri, onesb=onesb, onesf=onesf, identf=identf,
            gains=gains, ghead=ghead, lams=lams, win_r=win_r, wgate_r=wgate_r, wpa_r=wpa_r, wpb_r=wpb_r,
            wout_r=wout_r, wup_r=wup_r, wdown_r=wdown_r,
            cdk=cdk_c, cdv=vlay(cdv_all[sb_ids]), csk=csk_c, csv=vlay(csv_all[sb_ids])))

    res = run_bass_kernel_spmd(nc, in_maps, core_ids=list(range(8)))

    y_p = np.zeros((2, T, D), f32)
    y_s = np.zeros((16, ST, D), f32)
    dk_p = np.zeros((1, 2, T, 16, 64), f32)
    dv_p = np.zeros((1, 2, T, 8, 128), f32)
    sk_p = np.zeros((1, 2, T, 8, 128), f32)
    sv_p = np.zeros((1, 2, T, 8, 128), f32)
    dk_s = np.zeros((1, 16, ST, 16, 64), f32)
    dv_s = np.zeros((1, 16, ST, 8, 128), f32)
    sk_s = np.zeros((1, 16, ST, 8, 128), f32)
    sv_s = np.zeros((1, 16, ST, 8, 128), f32)
    for c in range(8):
        r = res.results[c]
        b = c // 4
        toks = own_tok[c]
        y = np.asarray(r["yT"]).transpose(1, 0, 2).reshape(D, TOT).T
        y_p[b, toks] = y[:TO]
        ka = np.asarray(r["kaT_o"]).transpose(2, 0, 1)
        va = np.asarray(r["vaT_o"]).transpose(2, 0, 1)
        kb = np.asarray(r["kbT_o"]).transpose(2, 0, 1)
        vb = np.asarray(r["vbT_o"]).transpose(2, 0, 1)
        ka16 = np.concatenate([ka[:, :, 0:64], ka[:, :, 64:128]], axis=1)
        dk_p[0, b, toks] = ka16[:TO]
        dv_p[0, b, toks] = va[:TO]
        sk_p[0, b, toks] = kb[:TO]
        sv_p[0, b, toks] = vb[:TO]
        for i in range(2):
            sl = slice(TO + i * ST, TO + (i + 1) * ST)
            y_s[2 * c + i] = y[sl]
            dk_s[0, 2 * c + i] = ka16[sl]
            dv_s[0, 2 * c + i] = va[sl]
            sk_s[0, 2 * c + i] = kb[sl]
            sv_s[0, 2 * c + i] = vb[sl]
    return (y_p, y_s, dk_p, dv_p, sk_p, sv_p, dk_s, dv_s, sk_s, sv_s)
```

```python
import contextlib
import numpy as np
import ml_dtypes
import concourse.bass as bass
import concourse.mybir as mybir
from concourse.bass_utils import run_bass_kernel_spmd

F32 = mybir.dt.float32
BF16 = mybir.dt.bfloat16
AF = mybir.ActivationFunctionType
ALU = mybir.AluOpType
NPBF = ml_dtypes.bfloat16

D = 2048
KC = 16
EPS = 1e-6
THETA = 10000.0
LAM_INIT = 0.8 - 0.6 * 1.0


class Res:
    __slots__ = ("lw", "rd")

    def __init__(self):
        self.lw = None
        self.rd = {}


class Prog:
    ENGS = ("pe", "act", "dve", "pool", "sp")

    def __init__(self, nc, stack):
        self.nc = nc
        self.stack = stack
        self.ops = {e: [] for e in self.ENGS}
        self.cnt = {}
        self.seen = {e: {} for e in self.ENGS}
        self.sems = {}
        self.noself = {"pe"}

    def sem(self, k):
        if k not in self.sems:
            self.sems[k] = self.stack.enter_context(self.nc.semaphore("s_" + k))
        return self.sems[k]

    def _deps(self, eng, reads, writes):
        need = {}

        def add(k, v):
            if need.get(k, 0) < v:
                need[k] = v
        for r in reads:
            if r.lw is not None:
                add(*r.lw)
        for w in writes:
            if w.lw is not None:
                add(*w.lw)
            for k, v in w.rd.items():
                add(k, v)
        seen = self.seen[eng]
        out = []
        for k, v in need.items():
            if k == eng and eng in self.noself:
                continue
            if seen.get(k, 0) < v:
                seen[k] = v
                out.append((k, v))
        return out

    def _fin(self, key, val, reads, writes):
        for r in reads:
            if r.rd.get(key, 0) < val:
                r.rd[key] = val
        for w in writes:
            w.lw = (key, val)
            w.rd = {}

    def op(self, eng, fn, reads=(), writes=()):
        waits = self._deps(eng, reads, writes)
        self.cnt[eng] = self.cnt.get(eng, 0) + 1
        self.ops[eng].append((waits, fn, eng, 1))
        self._fin(eng, self.cnt[eng], reads, writes)

    def dma(self, q, chan, fn, reads=(), writes=()):
        waits = self._deps(q, reads, writes)
        self.cnt[chan] = self.cnt.get(chan, 0) + 16
        self.ops[q].append((waits, fn, chan, 16))
        self._fin(chan, self.cnt[chan], reads, writes)

    def wait_all(self, eng, keys):
        waits = []
        for k in keys:
            v = self.cnt.get(k, 0)
            if v and self.seen[eng].get(k, 0) < v:
                self.seen[eng][k] = v
                waits.append((k, v))
        self.ops[eng].append((waits, None, None, 0))

    def barrier(self):
        keys = list(self.cnt.keys())
        for e in self.ENGS:
            self.wait_all(e, keys)

    def emit(self):
        for k in list(self.cnt.keys()):
            self.sem(k)
        sems = self.sems
        ops = self.ops
        self.ops = {e: [] for e in self.ENGS}

        def run(name):
            def body(e):
                for waits, fn, key, inc in ops[name]:
                    for k, v in waits:
                        e.wait_ge(sems[k], v)
                    if fn is not None:
                        fn(e).then_inc(sems[key], inc)
            return body
        with self.nc.Block() as block:
            block.tensor(run("pe"))
            block.scalar(run("act"))
            block.vector(run("dve"))
            block.gpsimd(run("pool"))
            block.sync(run("sp"))


class Cfg:
    def __init__(self, T=16384, PAST=2048, ST=32, SBC=2):
        self.T = T
        self.TQ = 512
        self.NT = T // 512
        self.NJ = self.NT // 4
        self.TO = self.NJ * 512
        self.ST = ST
        self.SBC = SBC
        self.NS = ST * SBC
        self.NSP = 256
        self.TOT = self.TO + self.NSP
        self.PAST = PAST
        self.NPB = PAST // 128


def build(cfg):
    nc = bass.Bass("TRN2", target_bir_lowering=False)
    T, TO, TOT, NS, NJ, PAST, NPB, ST, SBC = (cfg.T, cfg.TO, cfg.TOT, cfg.NS, cfg.NJ, cfg.PAST,
                                              cfg.NPB, cfg.ST, cfg.SBC)

    def din(name, shape, dt=F32):
        return nc.dram_tensor(name, list(shape), dt, kind="ExternalInput").ap()

    def dout(name, shape, dt=F32):
        return nc.dram_tensor(name, list(shape), dt, kind="ExternalOutput").ap()

    def dscr(name, shape, dt):
        return nc.dram_tensor(name, list(shape), dt, kind="Internal").ap()

    xT_b = din("xT_b", [128, KC, T])
    xT_o = din("xT_o", [128, KC, TOT])
    ropeC_b = din("ropeC_b", [128, T])
    ropeS_b = din("ropeS_b", [128, T])
    ropeC_o = din("ropeC_o", [128, TOT])
    ropeS_o = din("ropeS_o", [128, TOT])
    mdiff_d = din("mdiff", [128, 16, 512], BF16)
    msb_d = din("msb", [128, 16, 512], BF16)
    msamp_d = din("msamp", [128, 256], BF16)
    tri_d = din("tri", [128, 128], BF16)
    onesb_d = din("onesb", [128, 128], BF16)
    onesf_d = din("onesf", [128, 128])
    identf_d = din("identf", [128, 128])
    gains_d = din("gains", [128, 4, KC])
    ghead_d = din("ghead", [128, 1])
    lams_d = din("lams", [128, 4, 64])
    win_r = din("win_r", [64, 128, 2048])
    wgate_r = din("wgate_r", [32, 128, 2048])
    wpa_r = din("wpa_r", [16, 128, 1024])
    wpb_r = din("wpb_r", [16, 128, 1024])
    wout_r = din("wout_r", [16, 128, 2048])
    wup_r = din("wup_r", [64, 128, 2048])
    wdown_r = din("wdown_r", [16, 128, 8192])
    cdk = din("cdk", [SBC, 8, 128, PAST])
    cdv = din("cdv", [SBC, 8, 128, NPB, 128])
    csk = din("csk", [SBC, 8, 128, PAST])
    csv = din("csv", [SBC, 8, 128, NPB, 128])

    yT = dout("yT", [128, KC, TOT])
    kaT_o = dout("kaT_o", [8, 128, TOT])
    vaT_o = dout("vaT_o", [8, 128, TOT])
    kbT_o = dout("kbT_o", [8, 128, TOT])
    vbT_o = dout("vbT_o", [8, 128, TOT])

    hT_b = dscr("hT_b", [128, KC, T], BF16)
    hT_o = dscr("hT_o", [128, KC, TOT], BF16)
    win_s = dscr("win_s", [64, 128, 2048], BF16)
    wgate_s = dscr("wgate_s", [32, 128, 2048], BF16)
    wpa_s = dscr("wpa_s", [16, 128, 1024], BF16)
    wpb_s = dscr("wpb_s", [16, 128, 1024], BF16)
    wout_s = dscr("wout_s", [16, 128, 2048], BF16)
    wup_s = dscr("wup_s", [64, 128, 2048], BF16)
    wdown_s = dscr("wdown_s", [16, 128, 8192], BF16)
    oaT_s = dscr("oaT_s", [128, 8, TOT], BF16)
    obT_s = dscr("obT_s", [128, 8, TOT], BF16)

    out_chans = set()

    with contextlib.ExitStack() as top:
        P = Prog(nc, top)

        def sb(name, shape, dt, st=top):
            return st.enter_context(nc.sbuf_tensor("sb_" + name, list(shape), dt))

        tri = sb("tri", [128, 128], BF16)
        onesb = sb("onesb", [128, 128], BF16)
        onesf = sb("onesf", [128, 128], F32)
        identf = sb("identf", [128, 128], F32)
        gains = sb("gains", [128, 4, KC], F32)
        ghead = sb("ghead", [128, 1], F32)
        lams = sb("lams", [128, 4, 64], F32)
        lamt = sb("lamt", [128, 4, 64], F32)
        lamv = sb("lamv", [128, 4], F32)
        neglam = sb("neglam", [128, 1], F32)
        msamp = sb("msamp", [128, 256], BF16)
        rC = Res()
        for t_, d_ in ((tri, tri_d), (onesb, onesb_d), (onesf, onesf_d), (identf, identf_d),
                       (gains, gains_d), (ghead, ghead_d), (lams, lams_d), (msamp, msamp_d)):
            P.dma("sp", "const", lambda e, t_=t_, d_=d_: e.dma_start(out=t_[:], in_=d_), writes=[rC])
        P.op("dve", lambda e: e.tensor_tensor(out=lamt[:, 0, :], in0=lams[:, 0, :], in1=lams[:, 1, :], op=ALU.mult), reads=[rC], writes=[rC])
        P.op("dve", lambda e: e.tensor_tensor(out=lamt[:, 1, :], in0=lams[:, 2, :], in1=lams[:, 3, :], op=ALU.mult), reads=[rC], writes=[rC])
        P.op("dve", lambda e: e.tensor_reduce(out=lamv[:, 0:2], in_=lamt[:, 0:2, :], axis=mybir.AxisListType.X, op=ALU.add), reads=[rC], writes=[rC])
        P.op("act", lambda e: e.activation(out=lamv[:, 2:4], in_=lamv[:, 0:2], func=AF.Exp), reads=[rC], writes=[rC])
        P.op("dve", lambda e: e.scalar_tensor_tensor(out=neglam[:], in0=lamv[:, 3:4], scalar=-LAM_INIT, in1=lamv[:, 2:3], op0=ALU.add, op1=ALU.subtract), reads=[rC], writes=[rC])
        P.op("dve", lambda e: e.tensor_scalar(out=ghead[:], in0=ghead[:], scalar1=1.0 - LAM_INIT, scalar2=None, op0=ALU.mult), reads=[rC], writes=[rC])

        pall = top.enter_context(nc.psum_tensor("pall", [128, 8, 512], F32))
        rP = [Res() for _ in range(8)]

        def rstd_from(ps_ap, dst_ap, n_feat, eng_reads, eng_writes):
            P.op("dve", lambda e: e.tensor_scalar(out=dst_ap, in0=ps_ap, scalar1=1.0 / n_feat, scalar2=EPS, op0=ALU.mult, op1=ALU.add), reads=eng_reads, writes=eng_writes)
            P.op("act", lambda e: e.activation(out=dst_ap, in_=dst_ap, func=AF.Ln), reads=eng_writes, writes=eng_writes)
            P.op("act", lambda e: e.activation(out=dst_ap, in_=dst_ap, func=AF.Exp, scale=-0.5), reads=eng_writes, writes=eng_writes)

        with contextlib.ExitStack() as s0:
            NSL = 3
            wf = [sb(f"wf{i}", [128, 2048], F32, s0) for i in range(NSL)]
            wb = [sb(f"wb{i}", [128, 2048], BF16, s0) for i in range(NSL)]
            rwf = [Res() for _ in range(NSL)]
            rwb = [Res() for _ in range(NSL)]
            cnt = 0
            for src, dst, nb, w in ((win_r, win_s, 64, 2048), (wgate_r, wgate_s, 32, 2048), (wpa_r, wpa_s, 16, 1024),
                                    (wpb_r, wpb_s, 16, 1024), (wout_r, wout_s, 16, 2048), (wup_r, wup_s, 64, 2048),
                                    (wdown_r, wdown_s, 16, 8192)):
                for b_ in range(nb):
                    for c0 in range(0, w, 2048):
                        cw = min(2048, w - c0)
                        s = cnt % NSL
                        P.dma("sp", f"wl{s}", lambda e, s=s, src=src, b_=b_, c0=c0, cw=cw: e.dma_start(out=wf[s][:, 0:cw], in_=src[b_, :, c0:c0 + cw]), writes=[rwf[s]])
                        ce = ("dve", "pool", "act")[cnt % 3]
                        if ce == "act":
                            P.op("act", lambda e, s=s, cw=cw: e.copy(out=wb[s][:, 0:cw], in_=wf[s][:, 0:cw]), reads=[rwf[s]], writes=[rwb[s]])
                        else:
                            P.op(ce, lambda e, s=s, cw=cw: e.tensor_copy(out=wb[s][:, 0:cw], in_=wf[s][:, 0:cw]), reads=[rwf[s]], writes=[rwb[s]])
                        P.dma("pool", f"ws{s}", lambda e, s=s, dst=dst, b_=b_, c0=c0, cw=cw: e.dma_start(out=dst[b_, :, c0:c0 + cw], in_=wb[s][:, 0:cw]), reads=[rwb[s]], writes=[])
                        cnt += 1
            wstore_keys = [f"ws{s}" for s in range(NSL)]

            NH = 256
            xs = [sb(f"xs{i}", [128, KC, NH], F32, s0) for i in range(2)]
            sq = sb("sq0", [128, KC, NH], F32, s0)
            hb = [sb(f"hb{i}", [128, KC, NH], BF16, s0) for i in range(2)]
            rs0 = sb("rs0", [128, NH], F32, s0)
            rxs = [Res(), Res()]
            rsq = Res()
            rhb = [Res(), Res()]
            rrs = Res()
            rHB = Res()
            rHO = Res()
            it = 0
            for src, dst, ntok, rdst in ((xT_b, hT_b, T, rHB), (xT_o, hT_o, TOT, rHO)):
                for c0 in range(0, ntok, NH):
                    n = min(NH, ntok - c0)
                    s = it % 2
                    P.dma("sp", f"xl{s}", lambda e, s=s, src=src, c0=c0, n=n: e.dma_start(out=xs[s][:, :, 0:n], in_=src[:, :, c0:c0 + n]), writes=[rxs[s]])
                    P.op("act", lambda e, s=s, n=n: e.activation(out=sq[:, :, 0:n], in_=xs[s][:, :, 0:n], func=AF.Square), reads=[rxs[s]], writes=[rsq])
                    for kc in range(KC):
                        P.op("pe", lambda e, kc=kc, n=n: e.matmul(out=pall[:, 0, 0:n], lhsT=onesf[:], rhs=sq[:, kc, 0:n], start=(kc == 0), stop=(kc == KC - 1)),
                             reads=[rsq, rC], writes=[rP[0]])
                    rstd_from(pall[:, 0, 0:n], rs0[:, 0:n], D, [rP[0]], [rrs])
                    for kc in range(KC):
                        P.op("dve", lambda e, s=s, kc=kc, n=n: e.scalar_tensor_tensor(out=hb[s][:, kc, 0:n], in0=xs[s][:, kc, 0:n], scalar=gains[:, 0, kc:kc + 1], in1=rs0[:, 0:n], op0=ALU.mult, op1=ALU.mult),
                             reads=[rxs[s], rrs, rC], writes=[rhb[s]])
                    P.dma("pool", f"hs{s}", lambda e, s=s, dst=dst, c0=c0, n=n: e.dma_start(out=dst[:, :, c0:c0 + n], in_=hb[s][:, :, 0:n]), reads=[rhb[s]], writes=[rdst])
                    it += 1
            hstore_keys = ["hs0", "hs1"]
            P.barrier()

        rOA = Res()
        rOB = Res()
        ostore_keys = []
        import os
        KSTOP = int(os.environ.get('KSTOP', '9'))
        KATT = int(os.environ.get('KATT', '1'))
        KP = int(os.environ.get('KP', '9'))
        KSAMP = int(os.environ.get('KSAMP', '1'))
        KTYP = [int(c) for c in os.environ.get('KTYP', '01')]
        with contextlib.ExitStack() as s1:
            NK = 256
            KT = sb("KT", [128, T], BF16, s1)
            V = sb("V", [128, T // 128, 128], BF16, s1)
            QT = sb("QT", [128, TOT], BF16, s1)
            wh = [sb(f"wh{i}", [128, KC, 128], BF16, s1) for i in range(5)]
            ht = [sb(f"ht{i}", [128, KC, NK], BF16, s1) for i in range(2)]
            tC = [sb(f"tC{i}", [128, NK], F32, s1) for i in range(2)]
            tS = [sb(f"tS{i}", [128, NK], F32, s1) for i in range(2)]
            mt = sb("mt", [128, 16, 512], BF16, s1)
            tA = sb("tA", [128, NK], F32, s1)
            tB = sb("tB", [128, NK], F32, s1)
            kst = [sb(f"kst{i}", [128, NK], F32, s1) for i in range(2)]
            vst = [sb(f"vst{i}", [128, NK], F32, s1) for i in range(2)]
            KTs = sb("KTs", [128, 256], BF16, s1)
            ostage = sb("ostage", [128, 256], BF16, s1)
            rost = Res()
            Vs = sb("Vs", [128, SBC, 128], BF16, s1)
            NSS = 4
            Pt = [sb(f"Pt{i}", [128, 2, 512], BF16, s1) for i in range(NSS)]
            Et = [sb(f"Et{i}", [128, 512], F32, s1) for i in range(NSS)]
            Lt = [sb(f"Lt{i}", [128, 512], BF16, s1) for i in range(NSS)]
            Tt = [sb(f"Tt{i}", [128, 512], F32, s1) for i in range(NSS)]
            At = [sb(f"At{i}", [128, 512], BF16, s1) for i in range(NSS)]
            carry = sb("carry", [128, 512], F32, s1)
            acc = sb("acc", [128, 2, 512], F32, s1)
            racc = Res()
            f1 = sb("f1", [128, 512], F32, s1)
            f2 = sb("f2", [128, 512], F32, s1)
            f3 = sb("f3", [128, 512], F32, s1)
            f4 = sb("f4", [128, 512], F32, s1)
            ob16 = [sb(f"ob16{i}", [128, 512], BF16, s1) for i in range(2)]
            ckf = sb("ckf", [128, PAST // 2], F32, s1)
            ckb = sb("ckb", [128, PAST], BF16, s1)
            cvf = sb("cvf", [128, NPB // 2, 128], F32, s1)
            cvb = sb("cvb", [128, NPB, 128], BF16, s1)

            rKT, rV, rQT, rmt, rtA, rtB, rKTs, rVs, rcarry = [Res() for _ in range(9)]
            rwh = [Res() for _ in range(5)]
            rht = [Res(), Res()]
            rtab = [Res(), Res()]
            rkst = [Res(), Res()]
            rvst = [Res(), Res()]
            rPt, rEt, rLt, rTt, rAt = [[Res() for _ in range(NSS)] for _ in range(5)]
            rf = [Res() for _ in range(4)]
            rob16 = [Res(), Res()]
            rckf, rckb, rcvf, rcvb = [Res() for _ in range(4)]
            state = {"ht": 0, "st": 0, "sl": 0, "ob": 0, "s4": 0, "sp2": 0}
            P.op("pool", lambda e: e.memset(ostage[:], 0.0), writes=[rost])

            def project_tile(typ, head, src, c0, n, own):
                s = state["ht"] % 2
                state["ht"] += 1
                rsrc = rHO if own else rHB
                P.dma("sp", f"htl{s}", lambda e: e.dma_start(out=ht[s][:, :, 0:n], in_=src[:, :, c0:c0 + n]), reads=[rsrc], writes=[rht[s]])
                if typ == 0:
                    cs, ss_ = (ropeC_o, ropeS_o) if own else (ropeC_b, ropeS_b)
                    P.dma("sp", f"tbl{s}", lambda e: e.dma_start(out=tC[s][:, 0:n], in_=cs[:, c0:c0 + n]), writes=[rtab[s]])
                    P.dma("sp", f"tbl{s}", lambda e: e.dma_start(out=tS[s][:, 0:n], in_=ss_[:, c0:c0 + n]), writes=[rtab[s]])

                def chain(widx, bank):
                    for kc in range(KC):
                        P.op("pe", lambda e, kc=kc: e.matmul(out=pall[:, bank, 0:n], lhsT=wh[widx][:, kc, :], rhs=ht[s][:, kc, 0:n], start=(kc == 0), stop=(kc == KC - 1)),
                             reads=[rwh[widx], rht[s]], writes=[rP[bank]])

                def roped(w0, dst_fn, dst_res, extra=None):
                    chain(w0, 4)
                    chain(w0 + 1, 5)
                    P.op("dve", lambda e: e.tensor_tensor(out=tA[:, 0:n], in0=pall[:, 4, 0:n], in1=tC[s][:, 0:n], op=ALU.mult), reads=[rP[4], rtab[s]], writes=[rtA])
                    P.op("dve", lambda e: e.tensor_tensor(out=tB[:, 0:n], in0=pall[:, 5, 0:n], in1=tS[s][:, 0:n], op=ALU.mult), reads=[rP[5], rtab[s]], writes=[rtB])
                    P.op("pool", lambda e: e.tensor_tensor(out=dst_fn(), in0=tA[:, 0:n], in1=tB[:, 0:n], op=ALU.add), reads=[rtA, rtB], writes=[dst_res])
                    if extra is not None:
                        P.op("pool", lambda e: e.tensor_tensor(out=extra[0](), in0=tA[:, 0:n], in1=tB[:, 0:n], op=ALU.add), reads=[rtA, rtB], writes=[extra[1]])

                def plain(widx, dst_fn, dst_res, extra=None):
                    chain(widx, 4)
                    P.op("act", lambda e: e.copy(out=dst_fn(), in_=pall[:, 4, 0:n]), reads=[rP[4]], writes=[dst_res])
                    if extra is not None:
                        P.op("pool", lambda e: e.tensor_copy(out=extra[0](), in_=dst_fn()), reads=[dst_res], writes=[extra[1]])

                samp = own and c0 >= TO
                if KP < 1 or (own and KP < 4) or (samp and KP < 6):
                    return
                if typ == 0:
                    iq, ik, iv = 0, 2, 4
                else:
                    iq, ik, iv = 0, 1, 2
                so = state["st"] % 2
                if own:
                    state["st"] += 1
                    if typ == 0:
                        roped(iq, lambda: QT[:, c0:c0 + n], rQT)
                    else:
                        plain(iq, lambda: QT[:, c0:c0 + n], rQT)
                    ex = (lambda: KTs[:, 0:n], rKTs) if samp else None
                    if typ == 0:
                        roped(ik, lambda: kst[so][:, 0:n], rkst[so], ex)
                    else:
                        plain(ik, lambda: kst[so][:, 0:n], rkst[so], ex)
                    kout = kaT_o if typ == 0 else kbT_o
                    P.dma("pool", f"kso{so}", lambda e: e.dma_start(out=kout[head, :, c0:c0 + n], in_=kst[so][:, 0:n]), reads=[rkst[so]])
                    out_chans.add(f"kso{so}")
                else:
                    if typ == 0:
                        roped(ik, lambda: KT[:, c0:c0 + n], rKT)
                    else:
                        plain(ik, lambda: KT[:, c0:c0 + n], rKT)
                if KP < 2:
                    return
                chain(iv, 6)
                P.op("act", lambda e: e.copy(out=vst[so][:, 0:n], in_=pall[:, 6, 0:n]), reads=[rP[6]], writes=[rvst[so]])
                if own:
                    vout = vaT_o if typ == 0 else vbT_o
                    P.dma("pool", f"vso{so}", lambda e: e.dma_start(out=vout[head, :, c0:c0 + n], in_=vst[so][:, 0:n]), reads=[rvst[so]])
                    out_chans.add(f"vso{so}")
                    if samp and KP >= 7:
                        for sbi in range(SBC):
                            P.op("pe", lambda e, sbi=sbi: e.transpose(out=pall[0:ST, 7, sbi * 128:(sbi + 1) * 128], in_=vst[so][:, sbi * ST:(sbi + 1) * ST], identity=identf[:]),
                                 reads=[rvst[so], rC], writes=[rP[7]])
                            P.op("dve", lambda e, sbi=sbi: e.tensor_copy(out=Vs[0:ST, sbi, :], in_=pall[0:ST, 7, sbi * 128:(sbi + 1) * 128]), reads=[rP[7]], writes=[rVs])
                elif KP >= 3:
                    state["st"] += 1

                    def deferred():
                        nb_ = n // 128
                        for i in range(nb_):
                            P.op("pe", lambda e, i=i: e.transpose(out=pall[:, 7, i * 128:(i + 1) * 128], in_=vst[so][:, i * 128:(i + 1) * 128], identity=identf[:]),
                                 reads=[rvst[so], rC], writes=[rP[7]])
                        kb0 = c0 // 128
                        for i in range(nb_):
                            P.op("dve", lambda e, i=i: e.tensor_copy(out=V[:, kb0 + i, :], in_=pall[:, 7, i * 128:(i + 1) * 128]), reads=[rP[7]], writes=[rV])
                    return deferred
                return None

            def attn_diff(head, q0, n, blocks):
                nblk = len(blocks)

                P.op("pool", lambda e: e.memset(acc[:, :, 0:n], 0.0), writes=[racc])

                info = {}

                def stageA(bi, kT, vap, nk, mask, rds):
                    b0 = 2 + 2 * (state["sl"] % 3)
                    state["sl"] += 1
                    sl = state["s4"] % NSS
                    state["s4"] += 1
                    info[bi] = sl
                    for m in range(2):
                        P.op("pe", lambda e, m=m: e.matmul(out=pall[0:nk, b0 + m, 0:n], lhsT=kT(m), rhs=QT[m * 64:(m + 1) * 64, q0:q0 + n], start=True, stop=True),
                             reads=rds + [rQT], writes=[rP[b0 + m]])
                    P.op("act", lambda e: e.activation(out=Pt[sl][0:nk, :, 0:n], in_=pall[0:nk, b0:b0 + 2, 0:n], func=AF.Exp, scale=0.125),
                         reads=[rP[b0], rP[b0 + 1]], writes=[rPt[sl]])
                    if mask is not None:
                        for m in range(2):
                            P.op(("pool", "dve")[m], lambda e, m=m: e.tensor_tensor(out=Pt[sl][0:nk, m, 0:n], in0=Pt[sl][0:nk, m, 0:n], in1=mask, op=ALU.mult),
                                 reads=[rPt[sl], rmt], writes=[rPt[sl]])

                def stageB(bi, kT, vap, nk, mask, rds):
                    sl = info[bi]
                    st_, sp_ = (bi == 0), (bi == nblk - 1)
                    for m in range(2):
                        P.op("pe", lambda e, m=m: e.matmul(out=pall[:, m, 0:n], lhsT=vap, rhs=Pt[sl][0:nk, m, 0:n], start=st_, stop=sp_),
                             reads=rds + [rPt[sl]], writes=[rP[m]])
                    P.op("dve", lambda e: e.tensor_tensor(out=acc[0:nk, :, 0:n], in0=acc[0:nk, :, 0:n], in1=Pt[sl][0:nk, :, 0:n], op=ALU.add), reads=[racc, rPt[sl]], writes=[racc])
                SK = 3
                for t in range(nblk + SK):
                    if t < nblk:
                        stageA(t, *blocks[t])
                    if t - SK >= 0:
                        stageB(t - SK, *blocks[t - SK])
                a, b, o, q_ = f1[:, 0:n], f2[:, 0:n], f3[:, 0:n], f4[:, 0:n]
                for m in range(2):
                    P.op("pe", lambda e, m=m: e.matmul(out=pall[:, 2 + m, 0:n], lhsT=onesf[:], rhs=acc[:, m, 0:n], start=True, stop=True), reads=[rC, racc], writes=[rP[2 + m]])
                P.op("dve", lambda e: e.reciprocal(out=q_, in_=pall[:, 2, 0:n]), reads=[rP[2]], writes=[rf[3]])
                P.op("dve", lambda e: e.tensor_tensor(out=a, in0=pall[:, 0, 0:n], in1=q_, op=ALU.mult), reads=[rP[0], rf[3]], writes=[rf[0]])
                P.op("dve", lambda e: e.reciprocal(out=q_, in_=pall[:, 3, 0:n]), reads=[rP[3]], writes=[rf[3]])
                P.op("dve", lambda e: e.tensor_tensor(out=b, in0=pall[:, 1, 0:n], in1=q_, op=ALU.mult), reads=[rP[1], rf[3]], writes=[rf[1]])
                P.op("dve", lambda e: e.scalar_tensor_tensor(out=o, in0=b, scalar=neglam[:, 0:1], in1=a, op0=ALU.mult, op1=ALU.add), reads=[rf[0], rf[1], rC], writes=[rf[2]])
                P.op("act", lambda e: e.activation(out=a, in_=o, func=AF.Square), reads=[rf[2]], writes=[rf[0]])
                P.op("pe", lambda e: e.matmul(out=pall[:, 4, 0:n], lhsT=onesf[:], rhs=a, start=True, stop=True), reads=[rC, rf[0]], writes=[rP[4]])
                rstd_from(pall[:, 4, 0:n], b, 128, [rP[4]], [rf[1]])
                so = state["ob"] % 2
                state["ob"] += 1
                P.op("dve", lambda e: e.scalar_tensor_tensor(out=ob16[so][:, 0:n], in0=o, scalar=ghead[:, 0:1], in1=b, op0=ALU.mult, op1=ALU.mult), reads=[rf[2], rf[1], rC], writes=[rob16[so]])
                if q0 >= TO:
                    P.op("pool", lambda e: e.tensor_copy(out=ostage[:, q0 - TO:q0 - TO + n], in_=ob16[so][:, 0:n]), reads=[rob16[so]], writes=[rost])
                    if q0 + n == TO + NS:
                        P.dma("pool", "osst", lambda e: e.dma_start(out=oaT_s[:, head, TO:TO + 256], in_=ostage[:]), reads=[rost], writes=[])
                else:
                    P.dma("pool", f"oas{so}", lambda e: e.dma_start(out=oaT_s[:, head, q0:q0 + n], in_=ob16[so][:, 0:n]), reads=[rob16[so]], writes=[])

            def attn_sb(head, q0, n, blocks):
                nblk = len(blocks)
                P.op("pool", lambda e: e.memset(carry[:, 0:n], 0.0), writes=[rcarry])
                sc = 128.0 ** -0.5

                info = {}

                def stageA(bi, kT, vap, nk, mask, rds):
                    bz = 1 + state["sl"] % 2
                    state["sl"] += 1
                    sl = state["s4"] % NSS
                    state["s4"] += 1
                    info[bi] = sl
                    P.op("pe", lambda e: e.matmul(out=pall[0:nk, bz, 0:n], lhsT=kT(0), rhs=QT[:, q0:q0 + n], start=True, stop=True),
                         reads=rds + [rQT], writes=[rP[bz]])
                    P.op("act", lambda e: e.activation(out=Et[sl][0:nk, 0:n], in_=pall[0:nk, bz, 0:n], func=AF.Exp, scale=sc), reads=[rP[bz]], writes=[rEt[sl]])
                    P.op("act", lambda e: e.activation(out=Lt[sl][0:nk, 0:n], in_=Et[sl][0:nk, 0:n], func=AF.Ln, bias=1.0, scale=1.0), reads=[rEt[sl]], writes=[rLt[sl]])
                    if mask is not None:
                        P.op("pool", lambda e: e.tensor_tensor(out=Lt[sl][0:nk, 0:n], in0=Lt[sl][0:nk, 0:n], in1=mask, op=ALU.mult), reads=[rLt[sl], rmt], writes=[rLt[sl]])

                def stageB(bi, kT, vap, nk, mask, rds):
                    sl = info[bi]
                    slp = state["sp2"] % 2
                    state["sp2"] += 1
                    bc, bs_ = 3 + slp, 5 + slp
                    P.op("pe", lambda e: e.matmul(out=pall[0:nk, bc, 0:n], lhsT=tri[0:nk, 0:nk], rhs=Lt[sl][0:nk, 0:n], start=True, stop=True), reads=[rC, rLt[sl]], writes=[rP[bc]])
                    P.op("pe", lambda e: e.matmul(out=pall[:, bs_, 0:n], lhsT=onesb[0:nk, :], rhs=Lt[sl][0:nk, 0:n], start=True, stop=True), reads=[rC, rLt[sl]], writes=[rP[bs_]])
                    P.op("dve", lambda e: e.tensor_tensor(out=Tt[sl][0:nk, 0:n], in0=pall[0:nk, bc, 0:n], in1=carry[0:nk, 0:n], op=ALU.add), reads=[rP[bc], rcarry], writes=[rTt[sl]])
                    P.op("dve", lambda e: e.tensor_tensor(out=carry[:, 0:n], in0=pall[:, bs_, 0:n], in1=carry[:, 0:n], op=ALU.add), reads=[rP[bs_], rcarry], writes=[rcarry])
                    P.op("act", lambda e: e.activation(out=Tt[sl][0:nk, 0:n], in_=Tt[sl][0:nk, 0:n], func=AF.Exp, scale=-1.0), reads=[rTt[sl]], writes=[rTt[sl]])
                    P.op("pool", lambda e: e.tensor_tensor(out=At[sl][0:nk, 0:n], in0=Et[sl][0:nk, 0:n], in1=Tt[sl][0:nk, 0:n], op=ALU.mult), reads=[rEt[sl], rTt[sl]], writes=[rAt[sl]])
                    if mask is not None:
                        P.op("pool", lambda e: e.tensor_tensor(out=At[sl][0:nk, 0:n], in0=At[sl][0:nk, 0:n], in1=mask, op=ALU.mult), reads=[rAt[sl], rmt], writes=[rAt[sl]])

                def stageC(bi, kT, vap, nk, mask, rds):
                    sl = info[bi]
                    P.op("pe", lambda e: e.matmul(out=pall[:, 0, 0:n], lhsT=vap, rhs=At[sl][0:nk, 0:n], start=(bi == 0), stop=(bi == nblk - 1)),
                         reads=rds + [rAt[sl]], writes=[rP[0]])
                for t in range(nblk + 4):
                    if t < nblk:
                        stageA(t, *blocks[t])
                    if 0 <= t - 2 < nblk:
                        stageB(t - 2, *blocks[t - 2])
                    if 0 <= t - 4 < nblk:
                        stageC(t - 4, *blocks[t - 4])
                so = state["ob"] % 2
                state["ob"] += 1
                P.op("act", lambda e: e.copy(out=ob16[so][:, 0:n], in_=pall[:, 0, 0:n]), reads=[rP[0]], writes=[rob16[so]])
                if q0 >= TO:
                    P.op("pool", lambda e: e.tensor_copy(out=ostage[:, q0 - TO:q0 - TO + n], in_=ob16[so][:, 0:n]), reads=[rob16[so]], writes=[rost])
                    if q0 + n == TO + NS:
                        P.dma("pool", "osst", lambda e: e.dma_start(out=obT_s[:, head, TO:TO + 256], in_=ostage[:]), reads=[rost], writes=[])
                else:
                    P.dma("pool", f"obs{so}", lambda e: e.dma_start(out=obT_s[:, head, q0:q0 + n], in_=ob16[so][:, 0:n]), reads=[rob16[so]], writes=[])

            for typ in (KTYP if KSTOP >= 2 else []):
                md = mdiff_d if typ == 0 else msb_d
                P.dma("sp", "mtl", lambda e, md=md: e.dma_start(out=mt[:], in_=md), writes=[rmt])
                for head in range(8):
                    nw = 5 if typ == 0 else 3
                    base = 5 * head if typ == 0 else 40 + 3 * head
                    for i in range(nw):
                        P.dma("sp", f"whl{i}", lambda e, i=i, base=base: e.dma_start(out=wh[i][:], in_=win_s[base + i, :, :].rearrange("p (a b) -> p a b", b=128)), writes=[rwh[i]])
                    pend = None
                    for c0 in range(0, T, NK):
                        d_ = project_tile(typ, head, hT_b, c0, NK, False)
                        if pend is not None:
                            pend()
                        pend = d_
                    if pend is not None:
                        pend()
                    for c0 in range(0, TOT, NK):
                        project_tile(typ, head, hT_o, c0, min(NK, TOT - c0), True)
                    afn = attn_diff if typ == 0 else attn_sb
                    for j in (range(NJ) if KATT else []):
                        blocks = []
                        for kb in range(16 * j + 16):
                            if typ == 0:
                                kT = (lambda m, kb=kb: KT[m * 64:(m + 1) * 64, kb * 128:(kb + 1) * 128])
                            else:
                                kT = (lambda m, kb=kb: KT[:, kb * 128:(kb + 1) * 128])
                            mask = mt[:, kb - 16 * j, :] if kb >= 16 * j else None
                            blocks.append((kT, V[:, kb, :], 128, mask, [rKT, rV]))
                        if typ == 1:
                            blocks = blocks[::-1]
                        afn(head, j * 512, 512, blocks)
                    ck_d, cv_d = (cdk, cdv) if typ == 0 else (csk, csv)
                    for sbi in (range(SBC) if KSAMP else []):
                        for hf in range(2):
                            P.dma("sp", "ckl", lambda e, sbi=sbi, ck_d=ck_d, head=head, hf=hf: e.dma_start(out=ckf[:], in_=ck_d[sbi, head, :, hf * (PAST // 2):(hf + 1) * (PAST // 2)]), writes=[rckf])
                            P.dma("sp", "cvl", lambda e, sbi=sbi, cv_d=cv_d, head=head, hf=hf: e.dma_start(out=cvf[:], in_=cv_d[sbi, head, :, hf * (NPB // 2):(hf + 1) * (NPB // 2), :]), writes=[rcvf])
                            P.op("dve", lambda e, hf=hf: e.tensor_copy(out=ckb[:, hf * (PAST // 2):(hf + 1) * (PAST // 2)], in_=ckf[:]), reads=[rckf], writes=[rckb])
                            P.op("pool", lambda e, hf=hf: e.tensor_copy(out=cvb[:, hf * (NPB // 2):(hf + 1) * (NPB // 2), :], in_=cvf[:]), reads=[rcvf], writes=[rcvb])
                        blocks = []
                        for kb in range(NPB):
                            if typ == 0:
                                kT = (lambda m, kb=kb: ckb[m * 64:(m + 1) * 64, kb * 128:(kb + 1) * 128])
                            else:
                                kT = (lambda m, kb=kb: ckb[:, kb * 128:(kb + 1) * 128])
                            blocks.append((kT, cvb[:, kb, :], 128, None, [rckb, rcvb]))
                        if typ == 0:
                            kT = (lambda m, sbi=sbi: KTs[m * 64:(m + 1) * 64, sbi * ST:(sbi + 1) * ST])
                            blocks.append((kT, Vs[0:ST, sbi, :], ST, None, [rKTs, rVs]))
                        else:
                            kT = (lambda m, sbi=sbi: KTs[:, sbi * ST:(sbi + 1) * ST])
                            blocks.append((kT, Vs[0:ST, sbi, :], ST, msamp[0:ST, 0:ST], [rKTs, rVs, rC]))
                            blocks = blocks[::-1]
                        afn(head, TO + sbi * ST, ST, blocks)
            P.barrier()

        with contextlib.ExitStack() as s3:
            NQ = 512
            A = sb("A3", [128, KC, NQ], F32, s3)
            B = sb("B3", [128, KC, NQ], F32, s3)
            C = sb("C3", [128, KC, NQ], BF16, s3)
            Fb = sb("F3", [128, 64, NQ], BF16, s3)
            NR = 8
            ring = [sb(f"ring{i}", [128, KC, 128], BF16, s3) for i in range(NR)]
            sga = sb("sga", [128, NQ], F32, s3)
            sgb = sb("sgb", [128, NQ], F32, s3)
            m1 = sb("m1", [128, NQ], F32, s3)
            m2 = sb("m2", [128, NQ], F32, s3)
            sqt = [sb(f"sqt{i}", [128, NQ], F32, s3) for i in range(2)]
            rsd = sb("rsd", [128, NQ], F32, s3)
            tmp3 = [sb(f"tmp3{i}", [128, NQ], F32, s3) for i in range(2)]
            rA, rB, rCC, rFlo, rFhi, rsga, rsgb, rm1, rm2, rrsd = [Res() for _ in range(10)]
            rring = [Res() for _ in range(NR)]
            rsqt = [Res(), Res()]
            rtmp3 = [Res(), Res()]
            st3 = {"r": 0, "q": 0, "t": 0, "pb": 0}

            def wload(src_ap, ncols):
                i = st3["r"] % NR
                st3["r"] += 1
                nch = ncols // 128
                P.dma("sp", f"rg{i}", lambda e: e.dma_start(out=ring[i][:, 0:nch, :], in_=src_ap.rearrange("p (a b) -> p a b", b=128)), writes=[rring[i]])
                return i

            def rF(f):
                return rFlo if f < 32 else rFhi

            def sumsq_accum(src_ap, src_res, n, f, nf):
                q = st3["q"] % 2
                st3["q"] += 1
                P.op("act", lambda e: e.activation(out=sqt[q][:, 0:n], in_=src_ap, func=AF.Square), reads=src_res, writes=[rsqt[q]])
                P.op("pe", lambda e: e.matmul(out=pall[:, 4, 0:n], lhsT=onesf[:], rhs=sqt[q][:, 0:n], start=(f == 0), stop=(f == nf - 1)), reads=[rC, rsqt[q]], writes=[rP[4]])

            def resid_add(n, gi):
                for f in range(KC):
                    t = st3["t"] % 2
                    st3["t"] += 1
                    P.op("dve", lambda e, f=f, t=t: e.scalar_tensor_tensor(out=tmp3[t][:, 0:n], in0=B[:, f, 0:n], scalar=gains[:, gi, f:f + 1], in1=rsd[:, 0:n], op0=ALU.mult, op1=ALU.mult),
                         reads=[rB, rrsd, rC], writes=[rtmp3[t]])
                    P.op("pool", lambda e, f=f, t=t: e.tensor_tensor(out=A[:, f, 0:n], in0=A[:, f, 0:n], in1=tmp3[t][:, 0:n], op=ALU.add), reads=[rA, rtmp3[t]], writes=[rA])

            def tile3(c0, n):
                P.dma("sp", "xa", lambda e, c0=c0, n=n: e.dma_start(out=A[:, :, 0:n], in_=xT_o[:, :, c0:c0 + n]), writes=[rA])
                P.dma("sp", "hc", lambda e, c0=c0, n=n: e.dma_start(out=C[:, :, 0:n], in_=hT_o[:, :, c0:c0 + n]), reads=[rHO], writes=[rCC])
                P.dma("sp", "oab", lambda e, c0=c0, n=n: e.dma_start(out=Fb[:, 0:8, 0:n], in_=oaT_s[:, :, c0:c0 + n]), reads=[rOA], writes=[rFlo])
                P.dma("sp", "oab", lambda e, c0=c0, n=n: e.dma_start(out=Fb[:, 8:16, 0:n], in_=obT_s[:, :, c0:c0 + n]), reads=[rOB], writes=[rFlo])
                for f in range(KC):
                    ia = wload(wgate_s[f, :, :], 2048)
                    ib = wload(wgate_s[16 + f, :, :], 2048)
                    ipa = wload(wpa_s[f, :, :], 1024)
                    ipb = wload(wpb_s[f, :, :], 1024)
                    for kc in range(KC):
                        P.op("pe", lambda e, kc=kc, ia=ia: e.matmul(out=pall[:, 0, 0:n], lhsT=ring[ia][:, kc, :], rhs=C[:, kc, 0:n], start=(kc == 0), stop=(kc == KC - 1)), reads=[rring[ia], rCC], writes=[rP[0]])
                    for kc in range(KC):
                        P.op("pe", lambda e, kc=kc, ib=ib: e.matmul(out=pall[:, 1, 0:n], lhsT=ring[ib][:, kc, :], rhs=C[:, kc, 0:n], start=(kc == 0), stop=(kc == KC - 1)), reads=[rring[ib], rCC], writes=[rP[1]])
                    for h in range(8):
                        P.op("pe", lambda e, h=h, ipa=ipa: e.matmul(out=pall[:, 2, 0:n], lhsT=ring[ipa][:, h, :], rhs=Fb[:, h, 0:n], start=(h == 0), stop=(h == 7)), reads=[rring[ipa], rFlo], writes=[rP[2]])
                    for h in range(8):
                        P.op("pe", lambda e, h=h, ipb=ipb: e.matmul(out=pall[:, 3, 0:n], lhsT=ring[ipb][:, h, :], rhs=Fb[:, 8 + h, 0:n], start=(h == 0), stop=(h == 7)), reads=[rring[ipb], rFlo], writes=[rP[3]])
                    P.op("act", lambda e: e.activation(out=sga[:, 0:n], in_=pall[:, 0, 0:n], func=AF.Sigmoid), reads=[rP[0]], writes=[rsga])
                    P.op("act", lambda e: e.activation(out=sgb[:, 0:n], in_=pall[:, 1, 0:n], func=AF.Sigmoid), reads=[rP[1]], writes=[rsgb])
                    P.op("dve", lambda e: e.tensor_tensor(out=m1[:, 0:n], in0=pall[:, 2, 0:n], in1=sga[:, 0:n], op=ALU.mult), reads=[rP[2], rsga], writes=[rm1])
                    P.op("dve", lambda e: e.tensor_tensor(out=m2[:, 0:n], in0=pall[:, 3, 0:n], in1=sgb[:, 0:n], op=ALU.mult), reads=[rP[3], rsgb], writes=[rm2])
                    P.op("pool", lambda e, f=f: e.tensor_tensor(out=Fb[:, 16 + f, 0:n], in0=m1[:, 0:n], in1=m2[:, 0:n], op=ALU.add), reads=[rm1, rm2], writes=[rFlo])
                for f in range(KC):
                    iw = wload(wout_s[f, :, :], 2048)
                    pb = 5 + st3["pb"] % 2
                    st3["pb"] += 1
                    for kc in range(KC):
                        P.op("pe", lambda e, kc=kc, iw=iw, pb=pb: e.matmul(out=pall[:, pb, 0:n], lhsT=ring[iw][:, kc, :], rhs=Fb[:, 16 + kc, 0:n], start=(kc == 0), stop=(kc == KC - 1)), reads=[rring[iw], rFlo], writes=[rP[pb]])
                    P.op("dve", lambda e, f=f, pb=pb: e.tensor_copy(out=B[:, f, 0:n], in_=pall[:, pb, 0:n]), reads=[rP[pb]], writes=[rB])
                    sumsq_accum(B[:, f, 0:n], [rB], n, f, KC)
                rstd_from(pall[:, 4, 0:n], rsd[:, 0:n], D, [rP[4]], [rrsd])
                resid_add(n, 1)
                for f in range(KC):
                    sumsq_accum(A[:, f, 0:n], [rA], n, f, KC)
                rstd_from(pall[:, 4, 0:n], rsd[:, 0:n], D, [rP[4]], [rrsd])
                for f in range(KC):
                    P.op("dve", lambda e, f=f: e.scalar_tensor_tensor(out=C[:, f, 0:n], in0=A[:, f, 0:n], scalar=gains[:, 2, f:f + 1], in1=rsd[:, 0:n], op0=ALU.mult, op1=ALU.mult), reads=[rA, rrsd, rC], writes=[rCC])
                for f in range(64):
                    iw = wload(wup_s[f, :, :], 2048)
                    pb = 5 + st3["pb"] % 2
                    st3["pb"] += 1
                    for kc in range(KC):
                        P.op("pe", lambda e, kc=kc, iw=iw, pb=pb: e.matmul(out=pall[:, pb, 0:n], lhsT=ring[iw][:, kc, :], rhs=C[:, kc, 0:n], start=(kc == 0), stop=(kc == KC - 1)), reads=[rring[iw], rCC], writes=[rP[pb]])
                    t = st3["t"] % 2
                    st3["t"] += 1
                    P.op("act", lambda e, pb=pb, t=t: e.activation(out=tmp3[t][:, 0:n], in_=pall[:, pb, 0:n], func=AF.Relu), reads=[rP[pb]], writes=[rtmp3[t]])
                    P.op("pool", lambda e, f=f, t=t: e.tensor_tensor(out=Fb[:, f, 0:n], in0=tmp3[t][:, 0:n], in1=tmp3[t][:, 0:n], op=ALU.mult), reads=[rtmp3[t]], writes=[rF(f)])
                for f in range(KC):
                    pb = 5 + st3["pb"] % 2
                    st3["pb"] += 1
                    for part in range(4):
                        iw = wload(wdown_s[f, :, part * 2048:(part + 1) * 2048], 2048)
                        for kc in range(KC):
                            kk = part * 16 + kc
                            P.op("pe", lambda e, kc=kc, kk=kk, iw=iw, pb=pb: e.matmul(out=pall[:, pb, 0:n], lhsT=ring[iw][:, kc, :], rhs=Fb[:, kk, 0:n], start=(kk == 0), stop=(kk == 63)), reads=[rring[iw], rF(kk)], writes=[rP[pb]])
                    P.op("dve", lambda e, f=f, pb=pb: e.tensor_copy(out=B[:, f, 0:n], in_=pall[:, pb, 0:n]), reads=[rP[pb]], writes=[rB])
                    sumsq_accum(B[:, f, 0:n], [rB], n, f, KC)
                rstd_from(pall[:, 4, 0:n], rsd[:, 0:n], D, [rP[4]], [rrsd])
                resid_add(n, 3)
                P.dma("pool", "yst", lambda e, c0=c0, n=n: e.dma_start(out=yT[:, :, c0:c0 + n], in_=A[:, :, 0:n]), reads=[rA])
                out_chans.add("yst")
            for c0 in (range(0, TOT, NQ) if KSTOP >= 3 else []):
                tile3(c0, min(NQ, TOT - c0))
        P.wait_all("sp", sorted(out_chans))
        P.barrier()
        P.emit()
    return nc


def fblocks(W):
    K, N = W.shape
    return np.ascontiguousarray(W.reshape(K // 128, 128, N // 128, 128).transpose(2, 1, 0, 3).reshape(N // 128, 128, K))


def featmajor(x2d):
    n = x2d.shape[0]
    return np.ascontiguousarray(x2d.T.reshape(KC, 128, n).transpose(1, 0, 2))


def rope_tables(pos):
    inv = (np.float32(THETA) ** (-np.arange(32, dtype=np.float32) / np.float32(32))).astype(np.float32)
    ang = pos.astype(np.float32)[:, None] * inv[None, :]
    cos = np.cos(ang).astype(np.float32)
    sin = np.sin(ang).astype(np.float32)
    p = np.arange(128)
    Cc = cos[:, p % 32].T
    Ss = sin[:, p % 32].T * np.where((p % 64) < 32, -1.0, 1.0).astype(np.float32)[:, None]
    return np.ascontiguousarray(Cc, dtype=np.float32), np.ascontiguousarray(Ss, dtype=np.float32)


_CACHE = {}


def kernel(x_prompt, x_sample, cache_diff_k, cache_diff_v, cache_sb_k, cache_sb_v,
           g_pre_mix, w_in, lambda_q1, lambda_k1, lambda_q2, lambda_k2, g_diff_head,
           w_gate, w_proj_a, w_proj_b, w_out, g_post_mix, g_pre_mlp, w_up, w_down, g_post_mlp):
    f32 = np.float32
    x_prompt = np.asarray(x_prompt, f32)
    x_sample = np.asarray(x_sample, f32)
    B_, T, _ = x_prompt.shape
    SBT, ST, _ = x_sample.shape
    PAST = cache_diff_k.shape[2]
    assert B_ == 2 and SBT == 16
    cfg = Cfg(T=T, PAST=PAST, ST=ST, SBC=2)
    key = (T, PAST, ST)
    if key not in _CACHE:
        _CACHE[key] = build(cfg)
    nc = _CACHE[key]
    NJ, TO, TOT, NS, NPB = cfg.NJ, cfg.TO, cfg.TOT, cfg.NS, cfg.NPB

    w_in0 = np.asarray(w_in, f32)[0]
    perm = (np.arange(64) + 32) % 64
    slices = []
    for h in range(8):
        q = np.concatenate([h * 64 + np.arange(64), (8 + h) * 64 + np.arange(64)])
        qp = np.concatenate([h * 64 + perm, (8 + h) * 64 + perm])
        slices += [q, qp, 1024 + q, 1024 + qp, 2048 + h * 128 + np.arange(128)]
    for h in range(8):
        r = h * 128 + np.arange(128)
        slices += [3072 + r, 4096 + r, 5120 + r]
    win_r = np.stack([fblocks(w_in0[:, c])[0] for c in slices])
    wgate_r = fblocks(np.asarray(w_gate, f32)[0])
    wpa_r = fblocks(np.asarray(w_proj_a, f32)[0])
    wpb_r = fblocks(np.asarray(w_proj_b, f32)[0])
    wout_r = fblocks(np.asarray(w_out, f32)[0])
    wup_r = fblocks(np.asarray(w_up, f32)[0])
    wdown_r = fblocks(np.asarray(w_down, f32)[0])

    def gvec(g):
        return np.asarray(g, f32)[0].reshape(KC, 128).T
    gains = np.ascontiguousarray(np.stack([gvec(g_pre_mix), gvec(g_post_mix), gvec(g_pre_mlp), gvec(g_post_mlp)], axis=1))
    ghead = np.ascontiguousarray(np.asarray(g_diff_head, f32)[0].reshape(128, 1))
    lams = np.ascontiguousarray(np.broadcast_to(np.stack([np.asarray(v, f32)[0] for v in (lambda_q1, lambda_k1, lambda_q2, lambda_k2)])[None], (128, 4, 64)))
    tri = (np.arange(128)[:, None] >= np.arange(128)[None, :]).astype(NPBF)
    onesb = np.ones((128, 128), NPBF)
    onesf = np.ones((128, 128), f32)
    identf = np.eye(128, dtype=f32)
    msamp = np.zeros((128, 256), NPBF)
    msamp[:ST, :ST] = (np.arange(ST)[:, None] < np.arange(ST)[None, :]).astype(NPBF)
    ropeC_b, ropeS_b = rope_tables(np.arange(T))
    xTb = [featmajor(x_prompt[b]) for b in range(2)]
    cdk_all = np.asarray(cache_diff_k, f32)[0]
    cdv_all = np.asarray(cache_diff_v, f32)[0]
    csk_all = np.asarray(cache_sb_k, f32)[0]
    csv_all = np.asarray(cache_sb_v, f32)[0]

    in_maps = []
    own_tok = []
    for c in range(8):
        b, qtr = c // 4, c % 4
        toks = np.concatenate([np.arange((4 * j + qtr) * 512, (4 * j + qtr + 1) * 512) for j in range(NJ)])
        own_tok.append(toks)
        xo = np.concatenate([x_prompt[b][toks], x_sample[2 * c], x_sample[2 * c + 1], np.zeros((cfg.NSP - NS, D), f32)], axis=0)
        pos_o = np.concatenate([toks, PAST + np.arange(ST), PAST + np.arange(ST), np.zeros(cfg.NSP - NS, np.int64)])
        rC_o, rS_o = rope_tables(pos_o)
        kpos = (np.arange(4)[:, None, None] * 512 + np.arange(4)[None, :, None] * 128 + np.arange(128)[None, None, :])
        qpos = qtr * 512 + np.arange(512)
        md = (kpos[..., None] < ((qpos // 64 + 1) * 64)[None, None, None, :])
        ms = (kpos[..., None] < qpos[None, None, None, :])
        mdiff = np.ascontiguousarray(md.reshape(16, 128, 512).transpose(1, 0, 2)).astype(NPBF)
        msb = np.ascontiguousarray(ms.reshape(16, 128, 512).transpose(1, 0, 2)).astype(NPBF)
        sb_ids = [2 * c, 2 * c + 1]
        dk = cdk_all[sb_ids]
        cdk_c = np.concatenate([dk[:, :, 0:8, :], dk[:, :, 8:16, :]], axis=-1)
        cdk_c = np.ascontiguousarray(cdk_c.transpose(0, 2, 3, 1))
        csk_c = np.ascontiguousarray(csk_all[sb_ids].transpose(0, 2, 3, 1))
        def vlay(v):
            return np.ascontiguousarray(v.reshape(2, NPB, 128, 8, 128).transpose(0, 3, 2, 1, 4))
        in_maps.append(dict(
            xT_b=xTb[b], xT_o=featmajor(xo), ropeC_b=ropeC_b, ropeS_b=ropeS_b, ropeC_o=rC_o, ropeS_o=rS_o,
            mdiff=mdiff, msb=msb, msamp=msamp, tri=tri, onesb=onesb, onesf=onesf, identf=identf,
            gains=gains, ghead=ghead, lams=lams, win_r=win_r, wgate_r=wgate_r, wpa_r=wpa_r, wpb_r=wpb_r,
            wout_r=wout_r, wup_r=wup_r, wdown_r=wdown_r,
            cdk=cdk_c, cdv=vlay(cdv_all[sb_ids]), csk=csk_c, csv=vlay(csv_all[sb_ids])))

    res = run_bass_kernel_spmd(nc, in_maps, core_ids=list(range(8)))

    y_p = np.zeros((2, T, D), f32)
    y_s = np.zeros((16, ST, D), f32)
    dk_p = np.zeros((1, 2, T, 16, 64), f32)
    dv_p = np.zeros((1, 2, T, 8, 128), f32)
    sk_p = np.zeros((1, 2, T, 8, 128), f32)
    sv_p = np.zeros((1, 2, T, 8, 128), f32)
    dk_s = np.zeros((1, 16, ST, 16, 64), f32)
    dv_s = np.zeros((1, 16, ST, 8, 128), f32)
    sk_s = np.zeros((1, 16, ST, 8, 128), f32)
    sv_s = np.zeros((1, 16, ST, 8, 128), f32)
    for c in range(8):
        r = res.results[c]
        b = c // 4
        toks = own_tok[c]
        y = np.asarray(r["yT"]).transpose(1, 0, 2).reshape(D, TOT).T
        y_p[b, toks] = y[:TO]
        ka = np.asarray(r["kaT_o"]).transpose(2, 0, 1)
        va = np.asarray(r["vaT_o"]).transpose(2, 0, 1)
        kb = np.asarray(r["kbT_o"]).transpose(2, 0, 1)
        vb = np.asarray(r["vbT_o"]).transpose(2, 0, 1)
        ka16 = np.concatenate([ka[:, :, 0:64], ka[:, :, 64:128]], axis=1)
        dk_p[0, b, toks] = ka16[:TO]
        dv_p[0, b, toks] = va[:TO]
        sk_p[0, b, toks] = kb[:TO]
        sv_p[0, b, toks] = vb[:TO]
        for i in range(2):
            sl = slice(TO + i * ST, TO + (i + 1) * ST)
            y_s[2 * c + i] = y[sl]
            dk_s[0, 2 * c + i] = ka16[sl]
            dv_s[0, 2 * c + i] = va[sl]
            sk_s[0, 2 * c + i] = kb[sl]
            sv_s[0, 2 * c + i] = vb[sl]
    return (y_p, y_s, dk_p, dv_p, sk_p, sv_p, dk_s, dv_s, sk_s, sv_s)
```

```python
import contextlib
import numpy as np
import ml_dtypes
import concourse.bass as bass
import concourse.mybir as mybir
from concourse.bass_utils import run_bass_kernel_spmd

F32 = mybir.dt.float32
BF16 = mybir.dt.bfloat16
AF = mybir.ActivationFunctionType
ALU = mybir.AluOpType
NPBF = ml_dtypes.bfloat16

D = 2048
KC = 16
EPS = 1e-6
THETA = 10000.0
LAM_INIT = 0.8 - 0.6 * 1.0


class Res:
    __slots__ = ("lw", "rd")

    def __init__(self):
        self.lw = None
        self.rd = {}


class Prog:
    ENGS = ("pe", "act", "dve", "pool", "sp")

    def __init__(self, nc, stack):
        self.nc = nc
        self.stack = stack
        self.ops = {e: [] for e in self.ENGS}
        self.cnt = {}
        self.seen = {e: {} for e in self.ENGS}
        self.sems = {}
        self.noself = {"pe"}

    def sem(self, k):
        if k not in self.sems:
            self.sems[k] = self.stack.enter_context(self.nc.semaphore("s_" + k))
        return self.sems[k]

    def _deps(self, eng, reads, writes):
        need = {}

        def add(k, v):
            if need.get(k, 0) < v:
                need[k] = v
        for r in reads:
            if r.lw is not None:
                add(*r.lw)
        for w in writes:
            if w.lw is not None:
                add(*w.lw)
            for k, v in w.rd.items():
                add(k, v)
        seen = self.seen[eng]
        out = []
        for k, v in need.items():
            if k == eng and eng in self.noself:
                continue
            if seen.get(k, 0) < v:
                seen[k] = v
                out.append((k, v))
        return out

    def _fin(self, key, val, reads, writes):
        for r in reads:
            if r.rd.get(key, 0) < val:
                r.rd[key] = val
        for w in writes:
            w.lw = (key, val)
            w.rd = {}

    def op(self, eng, fn, reads=(), writes=()):
        waits = self._deps(eng, reads, writes)
        self.cnt[eng] = self.cnt.get(eng, 0) + 1
        self.ops[eng].append((waits, fn, eng, 1))
        self._fin(eng, self.cnt[eng], reads, writes)

    def dma(self, q, chan, fn, reads=(), writes=()):
        waits = self._deps(q, reads, writes)
        self.cnt[chan] = self.cnt.get(chan, 0) + 16
        self.ops[q].append((waits, fn, chan, 16))
        self._fin(chan, self.cnt[chan], reads, writes)

    def wait_all(self, eng, keys):
        waits = []
        for k in keys:
            v = self.cnt.get(k, 0)
            if v and self.seen[eng].get(k, 0) < v:
                self.seen[eng][k] = v
                waits.append((k, v))
        self.ops[eng].append((waits, None, None, 0))

    def barrier(self):
        keys = list(self.cnt.keys())
        for e in self.ENGS:
            self.wait_all(e, keys)

    def emit(self):
        for k in list(self.cnt.keys()):
            self.sem(k)
        sems = self.sems
        ops = self.ops
        self.ops = {e: [] for e in self.ENGS}

        def run(name):
            def body(e):
                for waits, fn, key, inc in ops[name]:
                    for k, v in waits:
                        e.wait_ge(sems[k], v)
                    if fn is not None:
                        fn(e).then_inc(sems[key], inc)
            return body
        with self.nc.Block() as block:
            block.tensor(run("pe"))
            block.scalar(run("act"))
            block.vector(run("dve"))
            block.gpsimd(run("pool"))
            block.sync(run("sp"))


class Cfg:
    def __init__(self, T=16384, PAST=2048, ST=32, SBC=2):
        self.T = T
        self.TQ = 512
        self.NT = T // 512
        self.NJ = self.NT // 4
        self.TO = self.NJ * 512
        self.ST = ST
        self.SBC = SBC
        self.NS = ST * SBC
        self.NSP = 256
        self.TOT = self.TO + self.NSP
        self.PAST = PAST
        self.NPB = PAST // 128


def build(cfg):
    nc = bass.Bass("TRN2", target_bir_lowering=False)
    T, TO, TOT, NS, NJ, PAST, NPB, ST, SBC = (cfg.T, cfg.TO, cfg.TOT, cfg.NS, cfg.NJ, cfg.PAST,
                                              cfg.NPB, cfg.ST, cfg.SBC)

    def din(name, shape, dt=F32):
        return nc.dram_tensor(name, list(shape), dt, kind="ExternalInput").ap()

    def dout(name, shape, dt=F32):
        return nc.dram_tensor(name, list(shape), dt, kind="ExternalOutput").ap()

    def dscr(name, shape, dt):
        return nc.dram_tensor(name, list(shape), dt, kind="Internal").ap()

    xT_b = din("xT_b", [128, KC, T])
    xT_o = din("xT_o", [128, KC, TOT])
    ropeC_b = din("ropeC_b", [128, T])
    ropeS_b = din("ropeS_b", [128, T])
    ropeC_o = din("ropeC_o", [128, TOT])
    ropeS_o = din("ropeS_o", [128, TOT])
    mdiff_d = din("mdiff", [128, 16, 512], BF16)
    msb_d = din("msb", [128, 16, 512], BF16)
    msamp_d = din("msamp", [128, 256], BF16)
    tri_d = din("tri", [128, 128], BF16)
    onesb_d = din("onesb", [128, 128], BF16)
    onesf_d = din("onesf", [128, 128])
    identf_d = din("identf", [128, 128])
    gains_d = din("gains", [128, 4, KC])
    ghead_d = din("ghead", [128, 1])
    lams_d = din("lams", [128, 4, 64])
    win_r = din("win_r", [64, 128, 2048])
    wgate_r = din("wgate_r", [32, 128, 2048])
    wpa_r = din("wpa_r", [16, 128, 1024])
    wpb_r = din("wpb_r", [16, 128, 1024])
    wout_r = din("wout_r", [16, 128, 2048])
    wup_r = din("wup_r", [64, 128, 2048])
    wdown_r = din("wdown_r", [16, 128, 8192])
    cdk = din("cdk", [SBC, 8, 128, PAST])
    cdv = din("cdv", [SBC, 8, 128, NPB, 128])
    csk = din("csk", [SBC, 8, 128, PAST])
    csv = din("csv", [SBC, 8, 128, NPB, 128])

    yT = dout("yT", [128, KC, TOT])
    kaT_o = dout("kaT_o", [8, 128, TOT])
    vaT_o = dout("vaT_o", [8, 128, TOT])
    kbT_o = dout("kbT_o", [8, 128, TOT])
    vbT_o = dout("vbT_o", [8, 128, TOT])

    hT_b = dscr("hT_b", [128, KC, T], BF16)
    hT_o = dscr("hT_o", [128, KC, TOT], BF16)
    win_s = dscr("win_s", [64, 128, 2048], BF16)
    wgate_s = dscr("wgate_s", [32, 128, 2048], BF16)
    wpa_s = dscr("wpa_s", [16, 128, 1024], BF16)
    wpb_s = dscr("wpb_s", [16, 128, 1024], BF16)
    wout_s = dscr("wout_s", [16, 128, 2048], BF16)
    wup_s = dscr("wup_s", [64, 128, 2048], BF16)
    wdown_s = dscr("wdown_s", [16, 128, 8192], BF16)
    oaT_s = dscr("oaT_s", [128, 8, TOT], BF16)
    obT_s = dscr("obT_s", [128, 8, TOT], BF16)

    out_chans = set()

    with contextlib.ExitStack() as top:
        P = Prog(nc, top)

        def sb(name, shape, dt, st=top):
            return st.enter_context(nc.sbuf_tensor("sb_" + name, list(shape), dt))

        tri = sb("tri", [128, 128], BF16)
        onesb = sb("onesb", [128, 128], BF16)
        onesf = sb("onesf", [128, 128], F32)
        identf = sb("identf", [128, 128], F32)
        gains = sb("gains", [128, 4, KC], F32)
        ghead = sb("ghead", [128, 1], F32)
        lams = sb("lams", [128, 4, 64], F32)
        lamt = sb("lamt", [128, 4, 64], F32)
        lamv = sb("lamv", [128, 4], F32)
        neglam = sb("neglam", [128, 1], F32)
        msamp = sb("msamp", [128, 256], BF16)
        rC = Res()
        for t_, d_ in ((tri, tri_d), (onesb, onesb_d), (onesf, onesf_d), (identf, identf_d),
                       (gains, gains_d), (ghead, ghead_d), (lams, lams_d), (msamp, msamp_d)):
            P.dma("sp", "const", lambda e, t_=t_, d_=d_: e.dma_start(out=t_[:], in_=d_), writes=[rC])
        P.op("dve", lambda e: e.tensor_tensor(out=lamt[:, 0, :], in0=lams[:, 0, :], in1=lams[:, 1, :], op=ALU.mult), reads=[rC], writes=[rC])
        P.op("dve", lambda e: e.tensor_tensor(out=lamt[:, 1, :], in0=lams[:, 2, :], in1=lams[:, 3, :], op=ALU.mult), reads=[rC], writes=[rC])
        P.op("dve", lambda e: e.tensor_reduce(out=lamv[:, 0:2], in_=lamt[:, 0:2, :], axis=mybir.AxisListType.X, op=ALU.add), reads=[rC], writes=[rC])
        P.op("act", lambda e: e.activation(out=lamv[:, 2:4], in_=lamv[:, 0:2], func=AF.Exp), reads=[rC], writes=[rC])
        P.op("dve", lambda e: e.scalar_tensor_tensor(out=neglam[:], in0=lamv[:, 3:4], scalar=-LAM_INIT, in1=lamv[:, 2:3], op0=ALU.add, op1=ALU.subtract), reads=[rC], writes=[rC])
        P.op("dve", lambda e: e.tensor_scalar(out=ghead[:], in0=ghead[:], scalar1=1.0 - LAM_INIT, scalar2=None, op0=ALU.mult), reads=[rC], writes=[rC])

        pall = top.enter_context(nc.psum_tensor("pall", [128, 8, 512], F32))
        rP = [Res() for _ in range(8)]

        def rstd_from(ps_ap, dst_ap, n_feat, eng_reads, eng_writes):
            P.op("dve", lambda e: e.tensor_scalar(out=dst_ap, in0=ps_ap, scalar1=1.0 / n_feat, scalar2=EPS, op0=ALU.mult, op1=ALU.add), reads=eng_reads, writes=eng_writes)
            P.op("act", lambda e: e.activation(out=dst_ap, in_=dst_ap, func=AF.Ln), reads=eng_writes, writes=eng_writes)
            P.op("act", lambda e: e.activation(out=dst_ap, in_=dst_ap, func=AF.Exp, scale=-0.5), reads=eng_writes, writes=eng_writes)

        with contextlib.ExitStack() as s0:
            NSL = 3
            wf = [sb(f"wf{i}", [128, 2048], F32, s0) for i in range(NSL)]
            wb = [sb(f"wb{i}", [128, 2048], BF16, s0) for i in range(NSL)]
            rwf = [Res() for _ in range(NSL)]
            rwb = [Res() for _ in range(NSL)]
            cnt = 0
            for src, dst, nb, w in ((win_r, win_s, 64, 2048), (wgate_r, wgate_s, 32, 2048), (wpa_r, wpa_s, 16, 1024),
                                    (wpb_r, wpb_s, 16, 1024), (wout_r, wout_s, 16, 2048), (wup_r, wup_s, 64, 2048),
                                    (wdown_r, wdown_s, 16, 8192)):
                for b_ in range(nb):
                    for c0 in range(0, w, 2048):
                        cw = min(2048, w - c0)
                        s = cnt % NSL
                        P.dma("sp", f"wl{s}", lambda e, s=s, src=src, b_=b_, c0=c0, cw=cw: e.dma_start(out=wf[s][:, 0:cw], in_=src[b_, :, c0:c0 + cw]), writes=[rwf[s]])
                        ce = ("dve", "pool", "act")[cnt % 3]
                        if ce == "act":
                            P.op("act", lambda e, s=s, cw=cw: e.copy(out=wb[s][:, 0:cw], in_=wf[s][:, 0:cw]), reads=[rwf[s]], writes=[rwb[s]])
                        else:
                            P.op(ce, lambda e, s=s, cw=cw: e.tensor_copy(out=wb[s][:, 0:cw], in_=wf[s][:, 0:cw]), reads=[rwf[s]], writes=[rwb[s]])
                        P.dma("pool", f"ws{s}", lambda e, s=s, dst=dst, b_=b_, c0=c0, cw=cw: e.dma_start(out=dst[b_, :, c0:c0 + cw], in_=wb[s][:, 0:cw]), reads=[rwb[s]], writes=[])
                        cnt += 1
            wstore_keys = [f"ws{s}" for s in range(NSL)]

            NH = 256
            xs = [sb(f"xs{i}", [128, KC, NH], F32, s0) for i in range(2)]
            sq = sb("sq0", [128, KC, NH], F32, s0)
            hb = [sb(f"hb{i}", [128, KC, NH], BF16, s0) for i in range(2)]
            rs0 = sb("rs0", [128, NH], F32, s0)
            rxs = [Res(), Res()]
            rsq = Res()
            rhb = [Res(), Res()]
            rrs = Res()
            rHB = Res()
            rHO = Res()
            it = 0
            for src, dst, ntok, rdst in ((xT_b, hT_b, T, rHB), (xT_o, hT_o, TOT, rHO)):
                for c0 in range(0, ntok, NH):
                    n = min(NH, ntok - c0)
                    s = it % 2
                    P.dma("sp", f"xl{s}", lambda e, s=s, src=src, c0=c0, n=n: e.dma_start(out=xs[s][:, :, 0:n], in_=src[:, :, c0:c0 + n]), writes=[rxs[s]])
                    P.op("act", lambda e, s=s, n=n: e.activation(out=sq[:, :, 0:n], in_=xs[s][:, :, 0:n], func=AF.Square), reads=[rxs[s]], writes=[rsq])
                    for kc in range(KC):
                        P.op("pe", lambda e, kc=kc, n=n: e.matmul(out=pall[:, 0, 0:n], lhsT=onesf[:], rhs=sq[:, kc, 0:n], start=(kc == 0), stop=(kc == KC - 1)),
                             reads=[rsq, rC], writes=[rP[0]])
                    rstd_from(pall[:, 0, 0:n], rs0[:, 0:n], D, [rP[0]], [rrs])
                    for kc in range(KC):
                        P.op("dve", lambda e, s=s, kc=kc, n=n: e.scalar_tensor_tensor(out=hb[s][:, kc, 0:n], in0=xs[s][:, kc, 0:n], scalar=gains[:, 0, kc:kc + 1], in1=rs0[:, 0:n], op0=ALU.mult, op1=ALU.mult),
                             reads=[rxs[s], rrs, rC], writes=[rhb[s]])
                    P.dma("pool", f"hs{s}", lambda e, s=s, dst=dst, c0=c0, n=n: e.dma_start(out=dst[:, :, c0:c0 + n], in_=hb[s][:, :, 0:n]), reads=[rhb[s]], writes=[rdst])
                    it += 1
            hstore_keys = ["hs0", "hs1"]
            P.barrier()

        rOA = Res()
        rOB = Res()
        ostore_keys = []
        import os
        KSTOP = int(os.environ.get('KSTOP', '9'))
        KATT = int(os.environ.get('KATT', '1'))
        KP = int(os.environ.get('KP', '9'))
        KSAMP = int(os.environ.get('KSAMP', '1'))
        KTYP = [int(c) for c in os.environ.get('KTYP', '01')]
        with contextlib.ExitStack() as s1:
            NK = 256
            KT = sb("KT", [128, T], BF16, s1)
            V = sb("V", [128, T // 128, 128], BF16, s1)
            QT = sb("QT", [128, TOT], BF16, s1)
            wh = [sb(f"wh{i}", [128, KC, 128], BF16, s1) for i in range(5)]
            ht = [sb(f"ht{i}", [128, KC, NK], BF16, s1) for i in range(2)]
            tC = [sb(f"tC{i}", [128, NK], F32, s1) for i in range(2)]
            tS = [sb(f"tS{i}", [128, NK], F32, s1) for i in range(2)]
            mt = sb("mt", [128, 16, 512], BF16, s1)
            tA = sb("tA", [128, NK], F32, s1)
            tB = sb("tB", [128, NK], F32, s1)
            kst = [sb(f"kst{i}", [128, NK], F32, s1) for i in range(2)]
            vst = [sb(f"vst{i}", [128, NK], F32, s1) for i in range(2)]
            KTs = sb("KTs", [128, 256], BF16, s1)
            ostage = sb("ostage", [128, 256], BF16, s1)
            rost = Res()
            Vs = sb("Vs", [128, SBC, 128], BF16, s1)
            NSS = 4
            Pt = [sb(f"Pt{i}", [128, 2, 512], BF16, s1) for i in range(NSS)]
            Et = [sb(f"Et{i}", [128, 512], F32, s1) for i in range(NSS)]
            Lt = [sb(f"Lt{i}", [128, 512], BF16, s1) for i in range(NSS)]
            Tt = [sb(f"Tt{i}", [128, 512], F32, s1) for i in range(NSS)]
            At = [sb(f"At{i}", [128, 512], BF16, s1) for i in range(NSS)]
            carry = sb("carry", [128, 512], F32, s1)
            acc = sb("acc", [128, 2, 512], F32, s1)
            racc = Res()
            f1 = sb("f1", [128, 512], F32, s1)
            f2 = sb("f2", [128, 512], F32, s1)
            f3 = sb("f3", [128, 512], F32, s1)
            f4 = sb("f4", [128, 512], F32, s1)
            ob16 = [sb(f"ob16{i}", [128, 512], BF16, s1) for i in range(2)]
            ckf = sb("ckf", [128, PAST // 2], F32, s1)
            ckb = sb("ckb", [128, PAST], BF16, s1)
            cvf = sb("cvf", [128, NPB // 2, 128], F32, s1)
            cvb = sb("cvb", [128, NPB, 128], BF16, s1)

            rKT, rV, rQT, rmt, rtA, rtB, rKTs, rVs, rcarry = [Res() for _ in range(9)]
            rwh = [Res() for _ in range(5)]
            rht = [Res(), Res()]
            rtab = [Res(), Res()]
            rkst = [Res(), Res()]
            rvst = [Res(), Res()]
            rPt, rEt, rLt, rTt, rAt = [[Res() for _ in range(NSS)] for _ in range(5)]
            rf = [Res() for _ in range(4)]
            rob16 = [Res(), Res()]
            rckf, rckb, rcvf, rcvb = [Res() for _ in range(4)]
            state = {"ht": 0, "st": 0, "sl": 0, "ob": 0, "s4": 0, "sp2": 0}
            P.op("pool", lambda e: e.memset(ostage[:], 0.0), writes=[rost])

            def project_tile(typ, head, src, c0, n, own):
                s = state["ht"] % 2
                state["ht"] += 1
                rsrc = rHO if own else rHB
                P.dma("sp", f"htl{s}", lambda e: e.dma_start(out=ht[s][:, :, 0:n], in_=src[:, :, c0:c0 + n]), reads=[rsrc], writes=[rht[s]])
                if typ == 0:
                    cs, ss_ = (ropeC_o, ropeS_o) if own else (ropeC_b, ropeS_b)
                    P.dma("sp", f"tbl{s}", lambda e: e.dma_start(out=tC[s][:, 0:n], in_=cs[:, c0:c0 + n]), writes=[rtab[s]])
                    P.dma("sp", f"tbl{s}", lambda e: e.dma_start(out=tS[s][:, 0:n], in_=ss_[:, c0:c0 + n]), writes=[rtab[s]])

                def chain(widx, bank):
                    for kc in range(KC):
                        P.op("pe", lambda e, kc=kc: e.matmul(out=pall[:, bank, 0:n], lhsT=wh[widx][:, kc, :], rhs=ht[s][:, kc, 0:n], start=(kc == 0), stop=(kc == KC - 1)),
                             reads=[rwh[widx], rht[s]], writes=[rP[bank]])

                def roped(w0, dst_fn, dst_res, extra=None):
                    chain(w0, 4)
                    chain(w0 + 1, 5)
                    P.op("dve", lambda e: e.tensor_tensor(out=tA[:, 0:n], in0=pall[:, 4, 0:n], in1=tC[s][:, 0:n], op=ALU.mult), reads=[rP[4], rtab[s]], writes=[rtA])
                    P.op("dve", lambda e: e.tensor_tensor(out=tB[:, 0:n], in0=pall[:, 5, 0:n], in1=tS[s][:, 0:n], op=ALU.mult), reads=[rP[5], rtab[s]], writes=[rtB])
                    P.op("pool", lambda e: e.tensor_tensor(out=dst_fn(), in0=tA[:, 0:n], in1=tB[:, 0:n], op=ALU.add), reads=[rtA, rtB], writes=[dst_res])
                    if extra is not None:
                        P.op("pool", lambda e: e.tensor_tensor(out=extra[0](), in0=tA[:, 0:n], in1=tB[:, 0:n], op=ALU.add), reads=[rtA, rtB], writes=[extra[1]])

                def plain(widx, dst_fn, dst_res, extra=None):
                    chain(widx, 4)
                    P.op("act", lambda e: e.copy(out=dst_fn(), in_=pall[:, 4, 0:n]), reads=[rP[4]], writes=[dst_res])
                    if extra is not None:
                        P.op("pool", lambda e: e.tensor_copy(out=extra[0](), in_=dst_fn()), reads=[dst_res], writes=[extra[1]])

                samp = own and c0 >= TO
                if KP < 1 or (own and KP < 4) or (samp and KP < 6):
                    return
                if typ == 0:
                    iq, ik, iv = 0, 2, 4
                else:
                    iq, ik, iv = 0, 1, 2
                so = state["st"] % 2
                if own:
                    state["st"] += 1
                    if typ == 0:
                        roped(iq, lambda: QT[:, c0:c0 + n], rQT)
                    else:
                        plain(iq, lambda: QT[:, c0:c0 + n], rQT)
                    ex = (lambda: KTs[:, 0:n], rKTs) if samp else None
                    if typ == 0:
                        roped(ik, lambda: kst[so][:, 0:n], rkst[so], ex)
                    else:
                        plain(ik, lambda: kst[so][:, 0:n], rkst[so], ex)
                    kout = kaT_o if typ == 0 else kbT_o
                    P.dma("pool", f"kso{so}", lambda e: e.dma_start(out=kout[head, :, c0:c0 + n], in_=kst[so][:, 0:n]), reads=[rkst[so]])
                    out_chans.add(f"kso{so}")
                else:
                    if typ == 0:
                        roped(ik, lambda: KT[:, c0:c0 + n], rKT)
                    else:
                        plain(ik, lambda: KT[:, c0:c0 + n], rKT)
                if KP < 2:
                    return
                chain(iv, 6)
                P.op("act", lambda e: e.copy(out=vst[so][:, 0:n], in_=pall[:, 6, 0:n]), reads=[rP[6]], writes=[rvst[so]])
                if own:
                    vout = vaT_o if typ == 0 else vbT_o
                    P.dma("pool", f"vso{so}", lambda e: e.dma_start(out=vout[head, :, c0:c0 + n], in_=vst[so][:, 0:n]), reads=[rvst[so]])
                    out_chans.add(f"vso{so}")
                    if samp and KP >= 7:
                        for sbi in range(SBC):
                            P.op("pe", lambda e, sbi=sbi: e.transpose(out=pall[0:ST, 7, sbi * 128:(sbi + 1) * 128], in_=vst[so][:, sbi * ST:(sbi + 1) * ST], identity=identf[:]),
                                 reads=[rvst[so], rC], writes=[rP[7]])
                            P.op("dve", lambda e, sbi=sbi: e.tensor_copy(out=Vs[0:ST, sbi, :], in_=pall[0:ST, 7, sbi * 128:(sbi + 1) * 128]), reads=[rP[7]], writes=[rVs])
                elif KP >= 3:
                    state["st"] += 1

                    def deferred():
                        nb_ = n // 128
                        for i in range(nb_):
                            P.op("pe", lambda e, i=i: e.transpose(out=pall[:, 7, i * 128:(i + 1) * 128], in_=vst[so][:, i * 128:(i + 1) * 128], identity=identf[:]),
                                 reads=[rvst[so], rC], writes=[rP[7]])
                        kb0 = c0 // 128
                        for i in range(nb_):
                            P.op("dve", lambda e, i=i: e.tensor_copy(out=V[:, kb0 + i, :], in_=pall[:, 7, i * 128:(i + 1) * 128]), reads=[rP[7]], writes=[rV])
                    return deferred
                return None

            def attn_diff(head, q0, n, blocks):
                nblk = len(blocks)

                P.op("pool", lambda e: e.memset(acc[:, :, 0:n], 0.0), writes=[racc])

                info = {}

                def stageA(bi, kT, vap, nk, mask, rds):
                    b0 = 2 + 2 * (state["sl"] % 3)
                    state["sl"] += 1
                    sl = state["s4"] % NSS
                    state["s4"] += 1
                    info[bi] = sl
                    for m in range(2):
                        P.op("pe", lambda e, m=m: e.matmul(out=pall[0:nk, b0 + m, 0:n], lhsT=kT(m), rhs=QT[m * 64:(m + 1) * 64, q0:q0 + n], start=True, stop=True),
                             reads=rds + [rQT], writes=[rP[b0 + m]])
                    P.op("act", lambda e: e.activation(out=Pt[sl][0:nk, :, 0:n], in_=pall[0:nk, b0:b0 + 2, 0:n], func=AF.Exp, scale=0.125),
                         reads=[rP[b0], rP[b0 + 1]], writes=[rPt[sl]])
                    if mask is not None:
                        for m in range(2):
                            P.op(("pool", "dve")[m], lambda e, m=m: e.tensor_tensor(out=Pt[sl][0:nk, m, 0:n], in0=Pt[sl][0:nk, m, 0:n], in1=mask, op=ALU.mult),
                                 reads=[rPt[sl], rmt], writes=[rPt[sl]])

                def stageB(bi, kT, vap, nk, mask, rds):
                    sl = info[bi]
                    st_, sp_ = (bi == 0), (bi == nblk - 1)
                    for m in range(2):
                        P.op("pe", lambda e, m=m: e.matmul(out=pall[:, m, 0:n], lhsT=vap, rhs=Pt[sl][0:nk, m, 0:n], start=st_, stop=sp_),
                             reads=rds + [rPt[sl]], writes=[rP[m]])
                    P.op("dve", lambda e: e.tensor_tensor(out=acc[0:nk, :, 0:n], in0=acc[0:nk, :, 0:n], in1=Pt[sl][0:nk, :, 0:n], op=ALU.add), reads=[racc, rPt[sl]], writes=[racc])
                SK = 3
                for t in range(nblk + SK):
                    if t < nblk:
                        stageA(t, *blocks[t])
                    if t - SK >= 0:
                        stageB(t - SK, *blocks[t - SK])
                a, b, o, q_ = f1[:, 0:n], f2[:, 0:n], f3[:, 0:n], f4[:, 0:n]
                for m in range(2):
                    P.op("pe", lambda e, m=m: e.matmul(out=pall[:, 2 + m, 0:n], lhsT=onesf[:], rhs=acc[:, m, 0:n], start=True, stop=True), reads=[rC, racc], writes=[rP[2 + m]])
                P.op("dve", lambda e: e.reciprocal(out=q_, in_=pall[:, 2, 0:n]), reads=[rP[2]], writes=[rf[3]])
                P.op("dve", lambda e: e.tensor_tensor(out=a, in0=pall[:, 0, 0:n], in1=q_, op=ALU.mult), reads=[rP[0], rf[3]], writes=[rf[0]])
                P.op("dve", lambda e: e.reciprocal(out=q_, in_=pall[:, 3, 0:n]), reads=[rP[3]], writes=[rf[3]])
                P.op("dve", lambda e: e.tensor_tensor(out=b, in0=pall[:, 1, 0:n], in1=q_, op=ALU.mult), reads=[rP[1], rf[3]], writes=[rf[1]])
                P.op("dve", lambda e: e.scalar_tensor_tensor(out=o, in0=b, scalar=neglam[:, 0:1], in1=a, op0=ALU.mult, op1=ALU.add), reads=[rf[0], rf[1], rC], writes=[rf[2]])
                P.op("act", lambda e: e.activation(out=a, in_=o, func=AF.Square), reads=[rf[2]], writes=[rf[0]])
                P.op("pe", lambda e: e.matmul(out=pall[:, 4, 0:n], lhsT=onesf[:], rhs=a, start=True, stop=True), reads=[rC, rf[0]], writes=[rP[4]])
                rstd_from(pall[:, 4, 0:n], b, 128, [rP[4]], [rf[1]])
                so = state["ob"] % 2
                state["ob"] += 1
                P.op("dve", lambda e: e.scalar_tensor_tensor(out=ob16[so][:, 0:n], in0=o, scalar=ghead[:, 0:1], in1=b, op0=ALU.mult, op1=ALU.mult), reads=[rf[2], rf[1], rC], writes=[rob16[so]])
                if q0 >= TO:
                    P.op("pool", lambda e: e.tensor_copy(out=ostage[:, q0 - TO:q0 - TO + n], in_=ob16[so][:, 0:n]), reads=[rob16[so]], writes=[rost])
                    if q0 + n == TO + NS:
                        P.dma("pool", "osst", lambda e: e.dma_start(out=oaT_s[:, head, TO:TO + 256], in_=ostage[:]), reads=[rost], writes=[])
                else:
                    P.dma("pool", f"oas{so}", lambda e: e.dma_start(out=oaT_s[:, head, q0:q0 + n], in_=ob16[so][:, 0:n]), reads=[rob16[so]], writes=[])

            def attn_sb(head, q0, n, blocks):
                nblk = len(blocks)
                P.op("pool", lambda e: e.memset(carry[:, 0:n], 0.0), writes=[rcarry])
                sc = 128.0 ** -0.5

                info = {}

                def stageA(bi, kT, vap, nk, mask, rds):
                    bz = 1 + state["sl"] % 2
                    state["sl"] += 1
                    sl = state["s4"] % NSS
                    state["s4"] += 1
                    info[bi] = sl
                    P.op("pe", lambda e: e.matmul(out=pall[0:nk, bz, 0:n], lhsT=kT(0), rhs=QT[:, q0:q0 + n], start=True, stop=True),
                         reads=rds + [rQT], writes=[rP[bz]])
                    P.op("act", lambda e: e.activation(out=Et[sl][0:nk, 0:n], in_=pall[0:nk, bz, 0:n], func=AF.Exp, scale=sc), reads=[rP[bz]], writes=[rEt[sl]])

                def stageA2(bi, kT, vap, nk, mask, rds):
                    sl = info[bi]
                    P.op("act", lambda e: e.activation(out=Lt[sl][0:nk, 0:n], in_=Et[sl][0:nk, 0:n], func=AF.Ln, bias=1.0, scale=1.0), reads=[rEt[sl]], writes=[rLt[sl]])
                    if mask is not None:
                        P.op("pool", lambda e: e.tensor_tensor(out=Lt[sl][0:nk, 0:n], in0=Lt[sl][0:nk, 0:n], in1=mask, op=ALU.mult), reads=[rLt[sl], rmt], writes=[rLt[sl]])

                def stageB(bi, kT, vap, nk, mask, rds):
                    sl = info[bi]
                    slp = state["sp2"] % 2
                    state["sp2"] += 1
                    bc, bs_ = 3 + slp, 5 + slp
                    P.op("pe", lambda e: e.matmul(out=pall[0:nk, bc, 0:n], lhsT=tri[0:nk, 0:nk], rhs=Lt[sl][0:nk, 0:n], start=True, stop=True), reads=[rC, rLt[sl]], writes=[rP[bc]])
                    P.op("pe", lambda e: e.matmul(out=pall[:, bs_, 0:n], lhsT=onesb[0:nk, :], rhs=Lt[sl][0:nk, 0:n], start=True, stop=True), reads=[rC, rLt[sl]], writes=[rP[bs_]])
                    P.op("dve", lambda e: e.tensor_tensor(out=Tt[sl][0:nk, 0:n], in0=pall[0:nk, bc, 0:n], in1=carry[0:nk, 0:n], op=ALU.add), reads=[rP[bc], rcarry], writes=[rTt[sl]])
                    P.op("dve", lambda e: e.tensor_tensor(out=carry[:, 0:n], in0=pall[:, bs_, 0:n], in1=carry[:, 0:n], op=ALU.add), reads=[rP[bs_], rcarry], writes=[rcarry])
                    P.op("act", lambda e: e.activation(out=Tt[sl][0:nk, 0:n], in_=Tt[sl][0:nk, 0:n], func=AF.Exp, scale=-1.0), reads=[rTt[sl]], writes=[rTt[sl]])
                    P.op("pool", lambda e: e.tensor_tensor(out=At[sl][0:nk, 0:n], in0=Et[sl][0:nk, 0:n], in1=Tt[sl][0:nk, 0:n], op=ALU.mult), reads=[rEt[sl], rTt[sl]], writes=[rAt[sl]])
                    if mask is not None:
                        P.op("pool", lambda e: e.tensor_tensor(out=At[sl][0:nk, 0:n], in0=At[sl][0:nk, 0:n], in1=mask, op=ALU.mult), reads=[rAt[sl], rmt], writes=[rAt[sl]])

                def stageC(bi, kT, vap, nk, mask, rds):
                    sl = info[bi]
                    P.op("pe", lambda e: e.matmul(out=pall[:, 0, 0:n], lhsT=vap, rhs=At[sl][0:nk, 0:n], start=(bi == 0), stop=(bi == nblk - 1)),
                         reads=rds + [rAt[sl]], writes=[rP[0]])
                for t in range(nblk + 4):
                    if t < nblk:
                        stageA(t, *blocks[t])
                    if 0 <= t - 2 < nblk:
                        stageB(t - 2, *blocks[t - 2])
                    if t < nblk:
                        stageA2(t, *blocks[t])
                    if 0 <= t - 4 < nblk:
                        stageC(t - 4, *blocks[t - 4])
                so = state["ob"] % 2
                state["ob"] += 1
                P.op("act", lambda e: e.copy(out=ob16[so][:, 0:n], in_=pall[:, 0, 0:n]), reads=[rP[0]], writes=[rob16[so]])
                if q0 >= TO:
                    P.op("pool", lambda e: e.tensor_copy(out=ostage[:, q0 - TO:q0 - TO + n], in_=ob16[so][:, 0:n]), reads=[rob16[so]], writes=[rost])
                    if q0 + n == TO + NS:
                        P.dma("pool", "osst", lambda e: e.dma_start(out=obT_s[:, head, TO:TO + 256], in_=ostage[:]), reads=[rost], writes=[])
                else:
                    P.dma("pool", f"obs{so}", lambda e: e.dma_start(out=obT_s[:, head, q0:q0 + n], in_=ob16[so][:, 0:n]), reads=[rob16[so]], writes=[])

            for typ in (KTYP if KSTOP >= 2 else []):
                md = mdiff_d if typ == 0 else msb_d
                P.dma("sp", "mtl", lambda e, md=md: e.dma_start(out=mt[:], in_=md), writes=[rmt])
                for head in range(8):
                    nw = 5 if typ == 0 else 3
                    base = 5 * head if typ == 0 else 40 + 3 * head
                    for i in range(nw):
                        P.dma("sp", f"whl{i}", lambda e, i=i, base=base: e.dma_start(out=wh[i][:], in_=win_s[base + i, :, :].rearrange("p (a b) -> p a b", b=128)), writes=[rwh[i]])
                    pend = None
                    for c0 in range(0, T, NK):
                        d_ = project_tile(typ, head, hT_b, c0, NK, False)
                        if pend is not None:
                            pend()
                        pend = d_
                    if pend is not None:
                        pend()
                    for c0 in range(0, TOT, NK):
                        project_tile(typ, head, hT_o, c0, min(NK, TOT - c0), True)
                    afn = attn_diff if typ == 0 else attn_sb
                    for j in (range(NJ) if KATT else []):
                        blocks = []
                        for kb in range(16 * j + 16):
                            if typ == 0:
                                kT = (lambda m, kb=kb: KT[m * 64:(m + 1) * 64, kb * 128:(kb + 1) * 128])
                            else:
                                kT = (lambda m, kb=kb: KT[:, kb * 128:(kb + 1) * 128])
                            mask = mt[:, kb - 16 * j, :] if kb >= 16 * j else None
                            blocks.append((kT, V[:, kb, :], 128, mask, [rKT, rV]))
                        if typ == 1:
                            blocks = blocks[::-1]
                        afn(head, j * 512, 512, blocks)
                    ck_d, cv_d = (cdk, cdv) if typ == 0 else (csk, csv)
                    for sbi in (range(SBC) if KSAMP else []):
                        for hf in range(2):
                            P.dma("sp", "ckl", lambda e, sbi=sbi, ck_d=ck_d, head=head, hf=hf: e.dma_start(out=ckf[:], in_=ck_d[sbi, head, :, hf * (PAST // 2):(hf + 1) * (PAST // 2)]), writes=[rckf])
                            P.dma("sp", "cvl", lambda e, sbi=sbi, cv_d=cv_d, head=head, hf=hf: e.dma_start(out=cvf[:], in_=cv_d[sbi, head, :, hf * (NPB // 2):(hf + 1) * (NPB // 2), :]), writes=[rcvf])
                            P.op("dve", lambda e, hf=hf: e.tensor_copy(out=ckb[:, hf * (PAST // 2):(hf + 1) * (PAST // 2)], in_=ckf[:]), reads=[rckf], writes=[rckb])
                            P.op("pool", lambda e, hf=hf: e.tensor_copy(out=cvb[:, hf * (NPB // 2):(hf + 1) * (NPB // 2), :], in_=cvf[:]), reads=[rcvf], writes=[rcvb])
                        blocks = []
                        for kb in range(NPB):
                            if typ == 0:
                                kT = (lambda m, kb=kb: ckb[m * 64:(m + 1) * 64, kb * 128:(kb + 1) * 128])
                            else:
                                kT = (lambda m, kb=kb: ckb[:, kb * 128:(kb + 1) * 128])
                            blocks.append((kT, cvb[:, kb, :], 128, None, [rckb, rcvb]))
                        if typ == 0:
                            kT = (lambda m, sbi=sbi: KTs[m * 64:(m + 1) * 64, sbi * ST:(sbi + 1) * ST])
                            blocks.append((kT, Vs[0:ST, sbi, :], ST, None, [rKTs, rVs]))
                        else:
                            kT = (lambda m, sbi=sbi: KTs[:, sbi * ST:(sbi + 1) * ST])
                            blocks.append((kT, Vs[0:ST, sbi, :], ST, msamp[0:ST, 0:ST], [rKTs, rVs, rC]))
                            blocks = blocks[::-1]
                        afn(head, TO + sbi * ST, ST, blocks)
            P.barrier()

        with contextlib.ExitStack() as s3:
            NQ = 512
            A = sb("A3", [128, KC, NQ], F32, s3)
            B = sb("B3", [128, KC, NQ], F32, s3)
            C = sb("C3", [128, KC, NQ], BF16, s3)
            Fb = sb("F3", [128, 64, NQ], BF16, s3)
            NR = 8
            ring = [sb(f"ring{i}", [128, KC, 128], BF16, s3) for i in range(NR)]
            sga = sb("sga", [128, NQ], F32, s3)
            sgb = sb("sgb", [128, NQ], F32, s3)
            m1 = sb("m1", [128, NQ], F32, s3)
            m2 = sb("m2", [128, NQ], F32, s3)
            sqt = [sb(f"sqt{i}", [128, NQ], F32, s3) for i in range(2)]
            rsd = sb("rsd", [128, NQ], F32, s3)
            tmp3 = [sb(f"tmp3{i}", [128, NQ], F32, s3) for i in range(2)]
            rA, rB, rCC, rFlo, rFhi, rsga, rsgb, rm1, rm2, rrsd = [Res() for _ in range(10)]
            rring = [Res() for _ in range(NR)]
            rsqt = [Res(), Res()]
            rtmp3 = [Res(), Res()]
            st3 = {"r": 0, "q": 0, "t": 0, "pb": 0}

            def wload(src_ap, ncols):
                i = st3["r"] % NR
                st3["r"] += 1
                nch = ncols // 128
                P.dma("sp", f"rg{i}", lambda e: e.dma_start(out=ring[i][:, 0:nch, :], in_=src_ap.rearrange("p (a b) -> p a b", b=128)), writes=[rring[i]])
                return i

            def rF(f):
                return rFlo if f < 32 else rFhi

            def sumsq_accum(src_ap, src_res, n, f, nf):
                q = st3["q"] % 2
                st3["q"] += 1
                P.op("act", lambda e: e.activation(out=sqt[q][:, 0:n], in_=src_ap, func=AF.Square), reads=src_res, writes=[rsqt[q]])
                P.op("pe", lambda e: e.matmul(out=pall[:, 4, 0:n], lhsT=onesf[:], rhs=sqt[q][:, 0:n], start=(f == 0), stop=(f == nf - 1)), reads=[rC, rsqt[q]], writes=[rP[4]])

            def resid_add(n, gi):
                for f in range(KC):
                    t = st3["t"] % 2
                    st3["t"] += 1
                    P.op("dve", lambda e, f=f, t=t: e.scalar_tensor_tensor(out=tmp3[t][:, 0:n], in0=B[:, f, 0:n], scalar=gains[:, gi, f:f + 1], in1=rsd[:, 0:n], op0=ALU.mult, op1=ALU.mult),
                         reads=[rB, rrsd, rC], writes=[rtmp3[t]])
                    P.op("pool", lambda e, f=f, t=t: e.tensor_tensor(out=A[:, f, 0:n], in0=A[:, f, 0:n], in1=tmp3[t][:, 0:n], op=ALU.add), reads=[rA, rtmp3[t]], writes=[rA])

            def tile3(c0, n):
                P.dma("sp", "xa", lambda e, c0=c0, n=n: e.dma_start(out=A[:, :, 0:n], in_=xT_o[:, :, c0:c0 + n]), writes=[rA])
                P.dma("sp", "hc", lambda e, c0=c0, n=n: e.dma_start(out=C[:, :, 0:n], in_=hT_o[:, :, c0:c0 + n]), reads=[rHO], writes=[rCC])
                P.dma("sp", "oab", lambda e, c0=c0, n=n: e.dma_start(out=Fb[:, 0:8, 0:n], in_=oaT_s[:, :, c0:c0 + n]), reads=[rOA], writes=[rFlo])
                P.dma("sp", "oab", lambda e, c0=c0, n=n: e.dma_start(out=Fb[:, 8:16, 0:n], in_=obT_s[:, :, c0:c0 + n]), reads=[rOB], writes=[rFlo])
                for f in range(KC):
                    ia = wload(wgate_s[f, :, :], 2048)
                    ib = wload(wgate_s[16 + f, :, :], 2048)
                    ipa = wload(wpa_s[f, :, :], 1024)
                    ipb = wload(wpb_s[f, :, :], 1024)
                    for kc in range(KC):
                        P.op("pe", lambda e, kc=kc, ia=ia: e.matmul(out=pall[:, 0, 0:n], lhsT=ring[ia][:, kc, :], rhs=C[:, kc, 0:n], start=(kc == 0), stop=(kc == KC - 1)), reads=[rring[ia], rCC], writes=[rP[0]])
                    for kc in range(KC):
                        P.op("pe", lambda e, kc=kc, ib=ib: e.matmul(out=pall[:, 1, 0:n], lhsT=ring[ib][:, kc, :], rhs=C[:, kc, 0:n], start=(kc == 0), stop=(kc == KC - 1)), reads=[rring[ib], rCC], writes=[rP[1]])
                    for h in range(8):
                        P.op("pe", lambda e, h=h, ipa=ipa: e.matmul(out=pall[:, 2, 0:n], lhsT=ring[ipa][:, h, :], rhs=Fb[:, h, 0:n], start=(h == 0), stop=(h == 7)), reads=[rring[ipa], rFlo], writes=[rP[2]])
                    for h in range(8):
                        P.op("pe", lambda e, h=h, ipb=ipb: e.matmul(out=pall[:, 3, 0:n], lhsT=ring[ipb][:, h, :], rhs=Fb[:, 8 + h, 0:n], start=(h == 0), stop=(h == 7)), reads=[rring[ipb], rFlo], writes=[rP[3]])
                    P.op("act", lambda e: e.activation(out=sga[:, 0:n], in_=pall[:, 0, 0:n], func=AF.Sigmoid), reads=[rP[0]], writes=[rsga])
                    P.op("act", lambda e: e.activation(out=sgb[:, 0:n], in_=pall[:, 1, 0:n], func=AF.Sigmoid), reads=[rP[1]], writes=[rsgb])
                    P.op("dve", lambda e: e.tensor_tensor(out=m1[:, 0:n], in0=pall[:, 2, 0:n], in1=sga[:, 0:n], op=ALU.mult), reads=[rP[2], rsga], writes=[rm1])
                    P.op("dve", lambda e: e.tensor_tensor(out=m2[:, 0:n], in0=pall[:, 3, 0:n], in1=sgb[:, 0:n], op=ALU.mult), reads=[rP[3], rsgb], writes=[rm2])
                    P.op("pool", lambda e, f=f: e.tensor_tensor(out=Fb[:, 16 + f, 0:n], in0=m1[:, 0:n], in1=m2[:, 0:n], op=ALU.add), reads=[rm1, rm2], writes=[rFlo])
                for f in range(KC):
                    iw = wload(wout_s[f, :, :], 2048)
                    pb = 5 + st3["pb"] % 2
                    st3["pb"] += 1
                    for kc in range(KC):
                        P.op("pe", lambda e, kc=kc, iw=iw, pb=pb: e.matmul(out=pall[:, pb, 0:n], lhsT=ring[iw][:, kc, :], rhs=Fb[:, 16 + kc, 0:n], start=(kc == 0), stop=(kc == KC - 1)), reads=[rring[iw], rFlo], writes=[rP[pb]])
                    P.op("dve", lambda e, f=f, pb=pb: e.tensor_copy(out=B[:, f, 0:n], in_=pall[:, pb, 0:n]), reads=[rP[pb]], writes=[rB])
                    sumsq_accum(B[:, f, 0:n], [rB], n, f, KC)
                rstd_from(pall[:, 4, 0:n], rsd[:, 0:n], D, [rP[4]], [rrsd])
                resid_add(n, 1)
                for f in range(KC):
                    sumsq_accum(A[:, f, 0:n], [rA], n, f, KC)
                rstd_from(pall[:, 4, 0:n], rsd[:, 0:n], D, [rP[4]], [rrsd])
                for f in range(KC):
                    P.op("dve", lambda e, f=f: e.scalar_tensor_tensor(out=C[:, f, 0:n], in0=A[:, f, 0:n], scalar=gains[:, 2, f:f + 1], in1=rsd[:, 0:n], op0=ALU.mult, op1=ALU.mult), reads=[rA, rrsd, rC], writes=[rCC])
                for f in range(64):
                    iw = wload(wup_s[f, :, :], 2048)
                    pb = 5 + st3["pb"] % 2
                    st3["pb"] += 1
                    for kc in range(KC):
                        P.op("pe", lambda e, kc=kc, iw=iw, pb=pb: e.matmul(out=pall[:, pb, 0:n], lhsT=ring[iw][:, kc, :], rhs=C[:, kc, 0:n], start=(kc == 0), stop=(kc == KC - 1)), reads=[rring[iw], rCC], writes=[rP[pb]])
                    t = st3["t"] % 2
                    st3["t"] += 1
                    P.op("act", lambda e, pb=pb, t=t: e.activation(out=tmp3[t][:, 0:n], in_=pall[:, pb, 0:n], func=AF.Relu), reads=[rP[pb]], writes=[rtmp3[t]])
                    P.op("pool", lambda e, f=f, t=t: e.tensor_tensor(out=Fb[:, f, 0:n], in0=tmp3[t][:, 0:n], in1=tmp3[t][:, 0:n], op=ALU.mult), reads=[rtmp3[t]], writes=[rF(f)])
                for f in range(KC):
                    pb = 5 + st3["pb"] % 2
                    st3["pb"] += 1
                    for part in range(4):
                        iw = wload(wdown_s[f, :, part * 2048:(part + 1) * 2048], 2048)
                        for kc in range(KC):
                            kk = part * 16 + kc
                            P.op("pe", lambda e, kc=kc, kk=kk, iw=iw, pb=pb: e.matmul(out=pall[:, pb, 0:n], lhsT=ring[iw][:, kc, :], rhs=Fb[:, kk, 0:n], start=(kk == 0), stop=(kk == 63)), reads=[rring[iw], rF(kk)], writes=[rP[pb]])
                    P.op("dve", lambda e, f=f, pb=pb: e.tensor_copy(out=B[:, f, 0:n], in_=pall[:, pb, 0:n]), reads=[rP[pb]], writes=[rB])
                    sumsq_accum(B[:, f, 0:n], [rB], n, f, KC)
                rstd_from(pall[:, 4, 0:n], rsd[:, 0:n], D, [rP[4]], [rrsd])
                resid_add(n, 3)
                P.dma("pool", "yst", lambda e, c0=c0, n=n: e.dma_start(out=yT[:, :, c0:c0 + n], in_=A[:, :, 0:n]), reads=[rA])
                out_chans.add("yst")
            for c0 in (range(0, TOT, NQ) if KSTOP >= 3 else []):
                tile3(c0, min(NQ, TOT - c0))
        P.wait_all("sp", sorted(out_chans))
        P.barrier()
        P.emit()
    return nc


def fblocks(W):
    K, N = W.shape
    return np.ascontiguousarray(W.reshape(K // 128, 128, N // 128, 128).transpose(2, 1, 0, 3).reshape(N // 128, 128, K))


def featmajor(x2d):
    n = x2d.shape[0]
    return np.ascontiguousarray(x2d.T.reshape(KC, 128, n).transpose(1, 0, 2))


def rope_tables(pos):
    inv = (np.float32(THETA) ** (-np.arange(32, dtype=np.float32) / np.float32(32))).astype(np.float32)
    ang = pos.astype(np.float32)[:, None] * inv[None, :]
    cos = np.cos(ang).astype(np.float32)
    sin = np.sin(ang).astype(np.float32)
    p = np.arange(128)
    Cc = cos[:, p % 32].T
    Ss = sin[:, p % 32].T * np.where((p % 64) < 32, -1.0, 1.0).astype(np.float32)[:, None]
    return np.ascontiguousarray(Cc, dtype=np.float32), np.ascontiguousarray(Ss, dtype=np.float32)


_CACHE = {}


def kernel(x_prompt, x_sample, cache_diff_k, cache_diff_v, cache_sb_k, cache_sb_v,
           g_pre_mix, w_in, lambda_q1, lambda_k1, lambda_q2, lambda_k2, g_diff_head,
           w_gate, w_proj_a, w_proj_b, w_out, g_post_mix, g_pre_mlp, w_up, w_down, g_post_mlp):
    f32 = np.float32
    x_prompt = np.asarray(x_prompt, f32)
    x_sample = np.asarray(x_sample, f32)
    B_, T, _ = x_prompt.shape
    SBT, ST, _ = x_sample.shape
    PAST = cache_diff_k.shape[2]
    assert B_ == 2 and SBT == 16
    cfg = Cfg(T=T, PAST=PAST, ST=ST, SBC=2)
    key = (T, PAST, ST)
    if key not in _CACHE:
        _CACHE[key] = build(cfg)
    nc = _CACHE[key]
    NJ, TO, TOT, NS, NPB = cfg.NJ, cfg.TO, cfg.TOT, cfg.NS, cfg.NPB

    w_in0 = np.asarray(w_in, f32)[0]
    perm = (np.arange(64) + 32) % 64
    slices = []
    for h in range(8):
        q = np.concatenate([h * 64 + np.arange(64), (8 + h) * 64 + np.arange(64)])
        qp = np.concatenate([h * 64 + perm, (8 + h) * 64 + perm])
        slices += [q, qp, 1024 + q, 1024 + qp, 2048 + h * 128 + np.arange(128)]
    for h in range(8):
        r = h * 128 + np.arange(128)
        slices += [3072 + r, 4096 + r, 5120 + r]
    win_r = np.stack([fblocks(w_in0[:, c])[0] for c in slices])
    wgate_r = fblocks(np.asarray(w_gate, f32)[0])
    wpa_r = fblocks(np.asarray(w_proj_a, f32)[0])
    wpb_r = fblocks(np.asarray(w_proj_b, f32)[0])
    wout_r = fblocks(np.asarray(w_out, f32)[0])
    wup_r = fblocks(np.asarray(w_up, f32)[0])
    wdown_r = fblocks(np.asarray(w_down, f32)[0])

    def gvec(g):
        return np.asarray(g, f32)[0].reshape(KC, 128).T
    gains = np.ascontiguousarray(np.stack([gvec(g_pre_mix), gvec(g_post_mix), gvec(g_pre_mlp), gvec(g_post_mlp)], axis=1))
    ghead = np.ascontiguousarray(np.asarray(g_diff_head, f32)[0].reshape(128, 1))
    lams = np.ascontiguousarray(np.broadcast_to(np.stack([np.asarray(v, f32)[0] for v in (lambda_q1, lambda_k1, lambda_q2, lambda_k2)])[None], (128, 4, 64)))
    tri = (np.arange(128)[:, None] >= np.arange(128)[None, :]).astype(NPBF)
    onesb = np.ones((128, 128), NPBF)
    onesf = np.ones((128, 128), f32)
    identf = np.eye(128, dtype=f32)
    msamp = np.zeros((128, 256), NPBF)
    msamp[:ST, :ST] = (np.arange(ST)[:, None] < np.arange(ST)[None, :]).astype(NPBF)
    ropeC_b, ropeS_b = rope_tables(np.arange(T))
    xTb = [featmajor(x_prompt[b]) for b in range(2)]
    cdk_all = np.asarray(cache_diff_k, f32)[0]
    cdv_all = np.asarray(cache_diff_v, f32)[0]
    csk_all = np.asarray(cache_sb_k, f32)[0]
    csv_all = np.asarray(cache_sb_v, f32)[0]

    in_maps = []
    own_tok = []
    for c in range(8):
        b, qtr = c // 4, c % 4
        toks = np.concatenate([np.arange((4 * j + qtr) * 512, (4 * j + qtr + 1) * 512) for j in range(NJ)])
        own_tok.append(toks)
        xo = np.concatenate([x_prompt[b][toks], x_sample[2 * c], x_sample[2 * c + 1], np.zeros((cfg.NSP - NS, D), f32)], axis=0)
        pos_o = np.concatenate([toks, PAST + np.arange(ST), PAST + np.arange(ST), np.zeros(cfg.NSP - NS, np.int64)])
        rC_o, rS_o = rope_tables(pos_o)
        kpos = (np.arange(4)[:, None, None] * 512 + np.arange(4)[None, :, None] * 128 + np.arange(128)[None, None, :])
        qpos = qtr * 512 + np.arange(512)
        md = (kpos[..., None] < ((qpos // 64 + 1) * 64)[None, None, None, :])
        ms = (kpos[..., None] < qpos[None, None, None, :])
        mdiff = np.ascontiguousarray(md.reshape(16, 128, 512).transpose(1, 0, 2)).astype(NPBF)
        msb = np.ascontiguousarray(ms.reshape(16, 128, 512).transpose(1, 0, 2)).astype(NPBF)
        sb_ids = [2 * c, 2 * c + 1]
        dk = cdk_all[sb_ids]
        cdk_c = np.concatenate([dk[:, :, 0:8, :], dk[:, :, 8:16, :]], axis=-1)
        cdk_c = np.ascontiguousarray(cdk_c.transpose(0, 2, 3, 1))
        csk_c = np.ascontiguousarray(csk_all[sb_ids].transpose(0, 2, 3, 1))
        def vlay(v):
            return np.ascontiguousarray(v.reshape(2, NPB, 128, 8, 128).transpose(0, 3, 2, 1, 4))
        in_maps.append(dict(
            xT_b=xTb[b], xT_o=featmajor(xo), ropeC_b=ropeC_b, ropeS_b=ropeS_b, ropeC_o=rC_o, ropeS_o=rS_o,
            mdiff=mdiff, msb=msb, msamp=msamp, tri=tri, onesb=onesb, onesf=onesf, identf=identf,
            gains=gains, ghead=ghead, lams=lams, win_r=win_r, wgate_r=wgate_r, wpa_r=wpa_r, wpb_r=wpb_r,
            wout_r=wout_r, wup_r=wup_r, wdown_r=wdown_r,
            cdk=cdk_c, cdv=vlay(cdv_all[sb_ids]), csk=csk_c, csv=vlay(csv_all[sb_ids])))

    res = run_bass_kernel_spmd(nc, in_maps, core_ids=list(range(8)))

    y_p = np.zeros((2, T, D), f32)
    y_s = np.zeros((16, ST, D), f32)
    dk_p = np.zeros((1, 2, T, 16, 64), f32)
    dv_p = np.zeros((1, 2, T, 8, 128), f32)
    sk_p = np.zeros((1, 2, T, 8, 128), f32)
    sv_p = np.zeros((1, 2, T, 8, 128), f32)
    dk_s = np.zeros((1, 16, ST, 16, 64), f32)
    dv_s = np.zeros((1, 16, ST, 8, 128), f32)
    sk_s = np.zeros((1, 16, ST, 8, 128), f32)
    sv_s = np.zeros((1, 16, ST, 8, 128), f32)
    for c in range(8):
        r = res.results[c]
        b = c // 4
        toks = own_tok[c]
        y = np.asarray(r["yT"]).transpose(1, 0, 2).reshape(D, TOT).T
        y_p[b, toks] = y[:TO]
        ka = np.asarray(r["kaT_o"]).transpose(2, 0, 1)
        va = np.asarray(r["vaT_o"]).transpose(2, 0, 1)
        kb = np.asarray(r["kbT_o"]).transpose(2, 0, 1)
        vb = np.asarray(r["vbT_o"]).transpose(2, 0, 1)
        ka16 = np.concatenate([ka[:, :, 0:64], ka[:, :, 64:128]], axis=1)
        dk_p[0, b, toks] = ka16[:TO]
        dv_p[0, b, toks] = va[:TO]
        sk_p[0, b, toks] = kb[:TO]
        sv_p[0, b, toks] = vb[:TO]
        for i in range(2):
            sl = slice(TO + i * ST, TO + (i + 1) * ST)
            y_s[2 * c + i] = y[sl]
            dk_s[0, 2 * c + i] = ka16[sl]
            dv_s[0, 2 * c + i] = va[sl]
            sk_s[0, 2 * c + i] = kb[sl]
            sv_s[0, 2 * c + i] = vb[sl]
    return (y_p, y_s, dk_p, dv_p, sk_p, sv_p, dk_s, dv_s, sk_s, sv_s)
```

```python
import contextlib
import numpy as np
import ml_dtypes
import concourse.bass as bass
import concourse.mybir as mybir
from concourse.bass_utils import run_bass_kernel_spmd

F32 = mybir.dt.float32
BF16 = mybir.dt.bfloat16
AF = mybir.ActivationFunctionType
ALU = mybir.AluOpType
NPBF = ml_dtypes.bfloat16

D = 2048
KC = 16
EPS = 1e-6
THETA = 10000.0
LAM_INIT = 0.8 - 0.6 * 1.0


class Res:
    __slots__ = ("lw", "rd")

    def __init__(self):
        self.lw = None
        self.rd = {}


class Prog:
    ENGS = ("pe", "act", "dve", "pool", "sp")

    def __init__(self, nc, stack):
        self.nc = nc
        self.stack = stack
        self.ops = {e: [] for e in self.ENGS}
        self.cnt = {}
        self.seen = {e: {} for e in self.ENGS}
        self.sems = {}
        self.noself = {"pe"}

    def sem(self, k):
        if k not in self.sems:
            self.sems[k] = self.stack.enter_context(self.nc.semaphore("s_" + k))
        return self.sems[k]

    def _deps(self, eng, reads, writes):
        need = {}

        def add(k, v):
            if need.get(k, 0) < v:
                need[k] = v
        for r in reads:
            if r.lw is not None:
                add(*r.lw)
        for w in writes:
            if w.lw is not None:
                add(*w.lw)
            for k, v in w.rd.items():
                add(k, v)
        seen = self.seen[eng]
        out = []
        for k, v in need.items():
            if k == eng and eng in self.noself:
                continue
            if seen.get(k, 0) < v:
                seen[k] = v
                out.append((k, v))
        return out

    def _fin(self, key, val, reads, writes):
        for r in reads:
            if r.rd.get(key, 0) < val:
                r.rd[key] = val
        for w in writes:
            w.lw = (key, val)
            w.rd = {}

    def op(self, eng, fn, reads=(), writes=()):
        waits = self._deps(eng, reads, writes)
        self.cnt[eng] = self.cnt.get(eng, 0) + 1
        self.ops[eng].append((waits, fn, eng, 1))
        self._fin(eng, self.cnt[eng], reads, writes)

    def dma(self, q, chan, fn, reads=(), writes=()):
        waits = self._deps(q, reads, writes)
        self.cnt[chan] = self.cnt.get(chan, 0) + 16
        self.ops[q].append((waits, fn, chan, 16))
        self._fin(chan, self.cnt[chan], reads, writes)

    def wait_all(self, eng, keys):
        waits = []
        for k in keys:
            v = self.cnt.get(k, 0)
            if v and self.seen[eng].get(k, 0) < v:
                self.seen[eng][k] = v
                waits.append((k, v))
        self.ops[eng].append((waits, None, None, 0))

    def barrier(self):
        keys = list(self.cnt.keys())
        for e in self.ENGS:
            self.wait_all(e, keys)

    def emit(self):
        for k in list(self.cnt.keys()):
            self.sem(k)
        sems = self.sems
        ops = self.ops
        self.ops = {e: [] for e in self.ENGS}

        def run(name):
            def body(e):
                for waits, fn, key, inc in ops[name]:
                    for k, v in waits:
                        e.wait_ge(sems[k], v)
                    if fn is not None:
                        fn(e).then_inc(sems[key], inc)
            return body
        with self.nc.Block() as block:
            block.tensor(run("pe"))
            block.scalar(run("act"))
            block.vector(run("dve"))
            block.gpsimd(run("pool"))
            block.sync(run("sp"))


class Cfg:
    def __init__(self, T=16384, PAST=2048, ST=32, SBC=2):
        self.T = T
        self.TQ = 512
        self.NT = T // 512
        self.NJ = self.NT // 4
        self.TO = self.NJ * 512
        self.ST = ST
        self.SBC = SBC
        self.NS = ST * SBC
        self.NSP = 256
        self.TOT = self.TO + self.NSP
        self.PAST = PAST
        self.NPB = PAST // 128


def build(cfg):
    nc = bass.Bass("TRN2", target_bir_lowering=False)
    T, TO, TOT, NS, NJ, PAST, NPB, ST, SBC = (cfg.T, cfg.TO, cfg.TOT, cfg.NS, cfg.NJ, cfg.PAST,
                                              cfg.NPB, cfg.ST, cfg.SBC)

    def din(name, shape, dt=F32):
        return nc.dram_tensor(name, list(shape), dt, kind="ExternalInput").ap()

    def dout(name, shape, dt=F32):
        return nc.dram_tensor(name, list(shape), dt, kind="ExternalOutput").ap()

    def dscr(name, shape, dt):
        return nc.dram_tensor(name, list(shape), dt, kind="Internal").ap()

    xT_b = din("xT_b", [128, KC, T])
    xT_o = din("xT_o", [128, KC, TOT])
    ropeC_b = din("ropeC_b", [128, T])
    ropeS_b = din("ropeS_b", [128, T])
    ropeC_o = din("ropeC_o", [128, TOT])
    ropeS_o = din("ropeS_o", [128, TOT])
    mdiff_d = din("mdiff", [128, 16, 512], BF16)
    msb_d = din("msb", [128, 16, 512], BF16)
    msamp_d = din("msamp", [128, 256], BF16)
    tri_d = din("tri", [128, 128], BF16)
    onesb_d = din("onesb", [128, 128], BF16)
    onesf_d = din("onesf", [128, 128])
    identf_d = din("identf", [128, 128])
    gains_d = din("gains", [128, 4, KC])
    ghead_d = din("ghead", [128, 1])
    lams_d = din("lams", [128, 4, 64])
    win_r = din("win_r", [64, 128, 2048])
    wgate_r = din("wgate_r", [32, 128, 2048])
    wpa_r = din("wpa_r", [16, 128, 1024])
    wpb_r = din("wpb_r", [16, 128, 1024])
    wout_r = din("wout_r", [16, 128, 2048])
    wup_r = din("wup_r", [64, 128, 2048])
    wdown_r = din("wdown_r", [16, 128, 8192])
    cdk = din("cdk", [SBC, 8, 128, PAST])
    cdv = din("cdv", [SBC, 8, 128, NPB, 128])
    csk = din("csk", [SBC, 8, 128, PAST])
    csv = din("csv", [SBC, 8, 128, NPB, 128])

    yT = dout("yT", [128, KC, TOT])
    kaT_o = dout("kaT_o", [8, 128, TOT])
    vaT_o = dout("vaT_o", [8, 128, TOT])
    kbT_o = dout("kbT_o", [8, 128, TOT])
    vbT_o = dout("vbT_o", [8, 128, TOT])

    hT_b = dscr("hT_b", [128, KC, T], BF16)
    hT_o = dscr("hT_o", [128, KC, TOT], BF16)
    win_s = dscr("win_s", [64, 128, 2048], BF16)
    wgate_s = dscr("wgate_s", [32, 128, 2048], BF16)
    wpa_s = dscr("wpa_s", [16, 128, 1024], BF16)
    wpb_s = dscr("wpb_s", [16, 128, 1024], BF16)
    wout_s = dscr("wout_s", [16, 128, 2048], BF16)
    wup_s = dscr("wup_s", [64, 128, 2048], BF16)
    wdown_s = dscr("wdown_s", [16, 128, 8192], BF16)
    oaT_s = dscr("oaT_s", [128, 8, TOT], BF16)
    obT_s = dscr("obT_s", [128, 8, TOT], BF16)

    out_chans = set()

    with contextlib.ExitStack() as top:
        P = Prog(nc, top)

        def sb(name, shape, dt, st=top):
            return st.enter_context(nc.sbuf_tensor("sb_" + name, list(shape), dt))

        tri = sb("tri", [128, 128], BF16)
        onesb = sb("onesb", [128, 128], BF16)
        onesf = sb("onesf", [128, 128], F32)
        identf = sb("identf", [128, 128], F32)
        gains = sb("gains", [128, 4, KC], F32)
        ghead = sb("ghead", [128, 1], F32)
        lams = sb("lams", [128, 4, 64], F32)
        lamt = sb("lamt", [128, 4, 64], F32)
        lamv = sb("lamv", [128, 4], F32)
        neglam = sb("neglam", [128, 1], F32)
        msamp = sb("msamp", [128, 256], BF16)
        rC = Res()
        for t_, d_ in ((tri, tri_d), (onesb, onesb_d), (onesf, onesf_d), (identf, identf_d),
                       (gains, gains_d), (ghead, ghead_d), (lams, lams_d), (msamp, msamp_d)):
            P.dma("sp", "const", lambda e, t_=t_, d_=d_: e.dma_start(out=t_[:], in_=d_), writes=[rC])
        P.op("dve", lambda e: e.tensor_tensor(out=lamt[:, 0, :], in0=lams[:, 0, :], in1=lams[:, 1, :], op=ALU.mult), reads=[rC], writes=[rC])
        P.op("dve", lambda e: e.tensor_tensor(out=lamt[:, 1, :], in0=lams[:, 2, :], in1=lams[:, 3, :], op=ALU.mult), reads=[rC], writes=[rC])
        P.op("dve", lambda e: e.tensor_reduce(out=lamv[:, 0:2], in_=lamt[:, 0:2, :], axis=mybir.AxisListType.X, op=ALU.add), reads=[rC], writes=[rC])
        P.op("act", lambda e: e.activation(out=lamv[:, 2:4], in_=lamv[:, 0:2], func=AF.Exp), reads=[rC], writes=[rC])
        P.op("dve", lambda e: e.scalar_tensor_tensor(out=neglam[:], in0=lamv[:, 3:4], scalar=-LAM_INIT, in1=lamv[:, 2:3], op0=ALU.add, op1=ALU.subtract), reads=[rC], writes=[rC])
        P.op("dve", lambda e: e.tensor_scalar(out=ghead[:], in0=ghead[:], scalar1=1.0 - LAM_INIT, scalar2=None, op0=ALU.mult), reads=[rC], writes=[rC])

        pall = top.enter_context(nc.psum_tensor("pall", [128, 8, 512], F32))
        rP = [Res() for _ in range(8)]

        def rstd_from(ps_ap, dst_ap, n_feat, eng_reads, eng_writes):
            P.op("dve", lambda e: e.tensor_scalar(out=dst_ap, in0=ps_ap, scalar1=1.0 / n_feat, scalar2=EPS, op0=ALU.mult, op1=ALU.add), reads=eng_reads, writes=eng_writes)
            P.op("act", lambda e: e.activation(out=dst_ap, in_=dst_ap, func=AF.Ln), reads=eng_writes, writes=eng_writes)
            P.op("act", lambda e: e.activation(out=dst_ap, in_=dst_ap, func=AF.Exp, scale=-0.5), reads=eng_writes, writes=eng_writes)

        with contextlib.ExitStack() as s0:
            NSL = 3
            wf = [sb(f"wf{i}", [128, 2048], F32, s0) for i in range(NSL)]
            wb = [sb(f"wb{i}", [128, 2048], BF16, s0) for i in range(NSL)]
            rwf = [Res() for _ in range(NSL)]
            rwb = [Res() for _ in range(NSL)]
            cnt = 0
            for src, dst, nb, w in ((win_r, win_s, 64, 2048), (wgate_r, wgate_s, 32, 2048), (wpa_r, wpa_s, 16, 1024),
                                    (wpb_r, wpb_s, 16, 1024), (wout_r, wout_s, 16, 2048), (wup_r, wup_s, 64, 2048),
                                    (wdown_r, wdown_s, 16, 8192)):
                for b_ in range(nb):
                    for c0 in range(0, w, 2048):
                        cw = min(2048, w - c0)
                        s = cnt % NSL
                        P.dma("sp", f"wl{s}", lambda e, s=s, src=src, b_=b_, c0=c0, cw=cw: e.dma_start(out=wf[s][:, 0:cw], in_=src[b_, :, c0:c0 + cw]), writes=[rwf[s]])
                        ce = ("dve", "pool", "act")[cnt % 3]
                        if ce == "act":
                            P.op("act", lambda e, s=s, cw=cw: e.copy(out=wb[s][:, 0:cw], in_=wf[s][:, 0:cw]), reads=[rwf[s]], writes=[rwb[s]])
                        else:
                            P.op(ce, lambda e, s=s, cw=cw: e.tensor_copy(out=wb[s][:, 0:cw], in_=wf[s][:, 0:cw]), reads=[rwf[s]], writes=[rwb[s]])
                        P.dma("pool", f"ws{s}", lambda e, s=s, dst=dst, b_=b_, c0=c0, cw=cw: e.dma_start(out=dst[b_, :, c0:c0 + cw], in_=wb[s][:, 0:cw]), reads=[rwb[s]], writes=[])
                        cnt += 1
            wstore_keys = [f"ws{s}" for s in range(NSL)]

            NH = 256
            xs = [sb(f"xs{i}", [128, KC, NH], F32, s0) for i in range(2)]
            sq = sb("sq0", [128, KC, NH], F32, s0)
            hb = [sb(f"hb{i}", [128, KC, NH], BF16, s0) for i in range(2)]
            rs0 = sb("rs0", [128, NH], F32, s0)
            rxs = [Res(), Res()]
            rsq = Res()
            rhb = [Res(), Res()]
            rrs = Res()
            rHB = Res()
            rHO = Res()
            it = 0
            for src, dst, ntok, rdst in ((xT_b, hT_b, T, rHB), (xT_o, hT_o, TOT, rHO)):
                for c0 in range(0, ntok, NH):
                    n = min(NH, ntok - c0)
                    s = it % 2
                    P.dma("sp", f"xl{s}", lambda e, s=s, src=src, c0=c0, n=n: e.dma_start(out=xs[s][:, :, 0:n], in_=src[:, :, c0:c0 + n]), writes=[rxs[s]])
                    P.op("act", lambda e, s=s, n=n: e.activation(out=sq[:, :, 0:n], in_=xs[s][:, :, 0:n], func=AF.Square), reads=[rxs[s]], writes=[rsq])
                    for kc in range(KC):
                        P.op("pe", lambda e, kc=kc, n=n: e.matmul(out=pall[:, 0, 0:n], lhsT=onesf[:], rhs=sq[:, kc, 0:n], start=(kc == 0), stop=(kc == KC - 1)),
                             reads=[rsq, rC], writes=[rP[0]])
                    rstd_from(pall[:, 0, 0:n], rs0[:, 0:n], D, [rP[0]], [rrs])
                    for kc in range(KC):
                        P.op("dve", lambda e, s=s, kc=kc, n=n: e.scalar_tensor_tensor(out=hb[s][:, kc, 0:n], in0=xs[s][:, kc, 0:n], scalar=gains[:, 0, kc:kc + 1], in1=rs0[:, 0:n], op0=ALU.mult, op1=ALU.mult),
                             reads=[rxs[s], rrs, rC], writes=[rhb[s]])
                    P.dma("pool", f"hs{s}", lambda e, s=s, dst=dst, c0=c0, n=n: e.dma_start(out=dst[:, :, c0:c0 + n], in_=hb[s][:, :, 0:n]), reads=[rhb[s]], writes=[rdst])
                    it += 1
            hstore_keys = ["hs0", "hs1"]
            P.barrier()

        rOA = Res()
        rOB = Res()
        ostore_keys = []
        import os
        KSTOP = int(os.environ.get('KSTOP', '9'))
        KATT = int(os.environ.get('KATT', '1'))
        KP = int(os.environ.get('KP', '9'))
        KSAMP = int(os.environ.get('KSAMP', '1'))
        KTYP = [int(c) for c in os.environ.get('KTYP', '01')]
        with contextlib.ExitStack() as s1:
            NK = 256
            KT = sb("KT", [128, T], BF16, s1)
            V = sb("V", [128, T // 128, 128], BF16, s1)
            QT = sb("QT", [128, TOT], BF16, s1)
            wh = [sb(f"wh{i}", [128, KC, 128], BF16, s1) for i in range(5)]
            ht = [sb(f"ht{i}", [128, KC, NK], BF16, s1) for i in range(2)]
            tC = [sb(f"tC{i}", [128, NK], F32, s1) for i in range(2)]
            tS = [sb(f"tS{i}", [128, NK], F32, s1) for i in range(2)]
            mt = sb("mt", [128, 16, 512], BF16, s1)
            tA = sb("tA", [128, NK], F32, s1)
            tB = sb("tB", [128, NK], F32, s1)
            kst = [sb(f"kst{i}", [128, NK], F32, s1) for i in range(2)]
            vst = [sb(f"vst{i}", [128, NK], F32, s1) for i in range(2)]
            KTs = sb("KTs", [128, 256], BF16, s1)
            ostage = sb("ostage", [128, 256], BF16, s1)
            rost = Res()
            Vs = sb("Vs", [128, SBC, 128], BF16, s1)
            NSS = 4
            Pt = [sb(f"Pt{i}", [128, 2, 512], BF16, s1) for i in range(NSS)]
            Et = [sb(f"Et{i}", [128, 512], F32, s1) for i in range(NSS)]
            Lt = [sb(f"Lt{i}", [128, 512], BF16, s1) for i in range(NSS)]
            Tt = [sb(f"Tt{i}", [128, 512], F32, s1) for i in range(NSS)]
            At = [sb(f"At{i}", [128, 512], BF16, s1) for i in range(NSS)]
            carry = sb("carry", [128, 512], F32, s1)
            acc = sb("acc", [128, 2, 512], F32, s1)
            racc = Res()
            f1 = sb("f1", [128, 512], F32, s1)
            f2 = sb("f2", [128, 512], F32, s1)
            f3 = sb("f3", [128, 512], F32, s1)
            f4 = sb("f4", [128, 512], F32, s1)
            ob16 = [sb(f"ob16{i}", [128, 512], BF16, s1) for i in range(2)]
            ckf = sb("ckf", [128, PAST // 2], F32, s1)
            ckb = sb("ckb", [128, PAST], BF16, s1)
            cvf = sb("cvf", [128, NPB // 2, 128], F32, s1)
            cvb = sb("cvb", [128, NPB, 128], BF16, s1)

            rKT, rV, rQT, rmt, rtA, rtB, rKTs, rVs, rcarry = [Res() for _ in range(9)]
            rwh = [Res() for _ in range(5)]
            rht = [Res(), Res()]
            rtab = [Res(), Res()]
            rkst = [Res(), Res()]
            rvst = [Res(), Res()]
            rPt, rEt, rLt, rTt, rAt = [[Res() for _ in range(NSS)] for _ in range(5)]
            rPt2 = [[Res(), Res()] for _ in range(NSS)]
            rf = [Res() for _ in range(4)]
            rob16 = [Res(), Res()]
            rckf, rckb, rcvf, rcvb = [Res() for _ in range(4)]
            state = {"ht": 0, "st": 0, "sl": 0, "ob": 0, "s4": 0, "sp2": 0}
            P.op("pool", lambda e: e.memset(ostage[:], 0.0), writes=[rost])

            def project_tile(typ, head, src, c0, n, own):
                s = state["ht"] % 2
                state["ht"] += 1
                rsrc = rHO if own else rHB
                P.dma("sp", f"htl{s}", lambda e: e.dma_start(out=ht[s][:, :, 0:n], in_=src[:, :, c0:c0 + n]), reads=[rsrc], writes=[rht[s]])
                if typ == 0:
                    cs, ss_ = (ropeC_o, ropeS_o) if own else (ropeC_b, ropeS_b)
                    P.dma("sp", f"tbl{s}", lambda e: e.dma_start(out=tC[s][:, 0:n], in_=cs[:, c0:c0 + n]), writes=[rtab[s]])
                    P.dma("sp", f"tbl{s}", lambda e: e.dma_start(out=tS[s][:, 0:n], in_=ss_[:, c0:c0 + n]), writes=[rtab[s]])

                def chain(widx, bank):
                    for kc in range(KC):
                        P.op("pe", lambda e, kc=kc: e.matmul(out=pall[:, bank, 0:n], lhsT=wh[widx][:, kc, :], rhs=ht[s][:, kc, 0:n], start=(kc == 0), stop=(kc == KC - 1)),
                             reads=[rwh[widx], rht[s]], writes=[rP[bank]])

                def roped(w0, dst_fn, dst_res, extra=None):
                    chain(w0, 4)
                    chain(w0 + 1, 5)
                    P.op("dve", lambda e: e.tensor_tensor(out=tA[:, 0:n], in0=pall[:, 4, 0:n], in1=tC[s][:, 0:n], op=ALU.mult), reads=[rP[4], rtab[s]], writes=[rtA])
                    P.op("dve", lambda e: e.tensor_tensor(out=tB[:, 0:n], in0=pall[:, 5, 0:n], in1=tS[s][:, 0:n], op=ALU.mult), reads=[rP[5], rtab[s]], writes=[rtB])
                    P.op("pool", lambda e: e.tensor_tensor(out=dst_fn(), in0=tA[:, 0:n], in1=tB[:, 0:n], op=ALU.add), reads=[rtA, rtB], writes=[dst_res])
                    if extra is not None:
                        P.op("pool", lambda e: e.tensor_tensor(out=extra[0](), in0=tA[:, 0:n], in1=tB[:, 0:n], op=ALU.add), reads=[rtA, rtB], writes=[extra[1]])

                def plain(widx, dst_fn, dst_res, extra=None):
                    chain(widx, 4)
                    P.op("act", lambda e: e.copy(out=dst_fn(), in_=pall[:, 4, 0:n]), reads=[rP[4]], writes=[dst_res])
                    if extra is not None:
                        P.op("pool", lambda e: e.tensor_copy(out=extra[0](), in_=dst_fn()), reads=[dst_res], writes=[extra[1]])

                samp = own and c0 >= TO
                if KP < 1 or (own and KP < 4) or (samp and KP < 6):
                    return
                if typ == 0:
                    iq, ik, iv = 0, 2, 4
                else:
                    iq, ik, iv = 0, 1, 2
                so = state["st"] % 2
                if own:
                    state["st"] += 1
                    if typ == 0:
                        roped(iq, lambda: QT[:, c0:c0 + n], rQT)
                    else:
                        plain(iq, lambda: QT[:, c0:c0 + n], rQT)
                    ex = (lambda: KTs[:, 0:n], rKTs) if samp else None
                    if typ == 0:
                        roped(ik, lambda: kst[so][:, 0:n], rkst[so], ex)
                    else:
                        plain(ik, lambda: kst[so][:, 0:n], rkst[so], ex)
                    kout = kaT_o if typ == 0 else kbT_o
                    P.dma("pool", f"kso{so}", lambda e: e.dma_start(out=kout[head, :, c0:c0 + n], in_=kst[so][:, 0:n]), reads=[rkst[so]])
                    out_chans.add(f"kso{so}")
                else:
                    if typ == 0:
                        roped(ik, lambda: KT[:, c0:c0 + n], rKT)
                    else:
                        plain(ik, lambda: KT[:, c0:c0 + n], rKT)
                if KP < 2:
                    return
                chain(iv, 6)
                P.op("act", lambda e: e.copy(out=vst[so][:, 0:n], in_=pall[:, 6, 0:n]), reads=[rP[6]], writes=[rvst[so]])
                if own:
                    vout = vaT_o if typ == 0 else vbT_o
                    P.dma("pool", f"vso{so}", lambda e: e.dma_start(out=vout[head, :, c0:c0 + n], in_=vst[so][:, 0:n]), reads=[rvst[so]])
                    out_chans.add(f"vso{so}")
                    if samp and KP >= 7:
                        for sbi in range(SBC):
                            P.op("pe", lambda e, sbi=sbi: e.transpose(out=pall[0:ST, 7, sbi * 128:(sbi + 1) * 128], in_=vst[so][:, sbi * ST:(sbi + 1) * ST], identity=identf[:]),
                                 reads=[rvst[so], rC], writes=[rP[7]])
                            P.op("dve", lambda e, sbi=sbi: e.tensor_copy(out=Vs[0:ST, sbi, :], in_=pall[0:ST, 7, sbi * 128:(sbi + 1) * 128]), reads=[rP[7]], writes=[rVs])
                elif KP >= 3:
                    state["st"] += 1

                    def deferred():
                        nb_ = n // 128
                        for i in range(nb_):
                            P.op("pe", lambda e, i=i: e.transpose(out=pall[:, 7, i * 128:(i + 1) * 128], in_=vst[so][:, i * 128:(i + 1) * 128], identity=identf[:]),
                                 reads=[rvst[so], rC], writes=[rP[7]])
                        kb0 = c0 // 128
                        for i in range(nb_):
                            P.op("dve", lambda e, i=i: e.tensor_copy(out=V[:, kb0 + i, :], in_=pall[:, 7, i * 128:(i + 1) * 128]), reads=[rP[7]], writes=[rV])
                    return deferred
                return None

            def attn_diff(head, q0, n, blocks):
                nblk = len(blocks)

                P.op("pool", lambda e: e.memset(acc[:, :, 0:n], 0.0), writes=[racc])

                info = {}

                def stageA(bi, kT, vap, nk, mask, rds):
                    b0 = 2 + 2 * (state["sl"] % 3)
                    state["sl"] += 1
                    sl = state["s4"] % NSS
                    state["s4"] += 1
                    info[bi] = sl
                    for m in range(2):
                        P.op("pe", lambda e, m=m: e.matmul(out=pall[0:nk, b0 + m, 0:n], lhsT=kT(m), rhs=QT[m * 64:(m + 1) * 64, q0:q0 + n], start=True, stop=True),
                             reads=rds + [rQT], writes=[rP[b0 + m]])
                    P.op("act", lambda e: e.activation(out=Pt[sl][0:nk, :, 0:n], in_=pall[0:nk, b0:b0 + 2, 0:n], func=AF.Exp, scale=0.125),
                         reads=[rP[b0], rP[b0 + 1]], writes=[rPt2[sl][0], rPt2[sl][1]])
                    if mask is not None:
                        for m in range(2):
                            P.op(("pool", "dve")[m], lambda e, m=m: e.tensor_tensor(out=Pt[sl][0:nk, m, 0:n], in0=Pt[sl][0:nk, m, 0:n], in1=mask, op=ALU.mult),
                                 reads=[rPt2[sl][m], rmt], writes=[rPt2[sl][m]])

                def stageB(bi, kT, vap, nk, mask, rds):
                    sl = info[bi]
                    st_, sp_ = (bi == 0), (bi == nblk - 1)
                    for m in range(2):
                        P.op("pe", lambda e, m=m: e.matmul(out=pall[:, m, 0:n], lhsT=vap, rhs=Pt[sl][0:nk, m, 0:n], start=st_, stop=sp_),
                             reads=rds + [rPt2[sl][m]], writes=[rP[m]])
                    P.op("dve", lambda e: e.tensor_tensor(out=acc[0:nk, :, 0:n], in0=acc[0:nk, :, 0:n], in1=Pt[sl][0:nk, :, 0:n], op=ALU.add), reads=[racc, rPt2[sl][0], rPt2[sl][1]], writes=[racc])
                SK = 3
                for t in range(nblk + SK):
                    if t < nblk:
                        stageA(t, *blocks[t])
                    if t - SK >= 0:
                        stageB(t - SK, *blocks[t - SK])
                a, b, o, q_ = f1[:, 0:n], f2[:, 0:n], f3[:, 0:n], f4[:, 0:n]
                for m in range(2):
                    P.op("pe", lambda e, m=m: e.matmul(out=pall[:, 2 + m, 0:n], lhsT=onesf[:], rhs=acc[:, m, 0:n], start=True, stop=True), reads=[rC, racc], writes=[rP[2 + m]])
                P.op("dve", lambda e: e.reciprocal(out=q_, in_=pall[:, 2, 0:n]), reads=[rP[2]], writes=[rf[3]])
                P.op("dve", lambda e: e.tensor_tensor(out=a, in0=pall[:, 0, 0:n], in1=q_, op=ALU.mult), reads=[rP[0], rf[3]], writes=[rf[0]])
                P.op("dve", lambda e: e.reciprocal(out=q_, in_=pall[:, 3, 0:n]), reads=[rP[3]], writes=[rf[3]])
                P.op("dve", lambda e: e.tensor_tensor(out=b, in0=pall[:, 1, 0:n], in1=q_, op=ALU.mult), reads=[rP[1], rf[3]], writes=[rf[1]])
                P.op("dve", lambda e: e.scalar_tensor_tensor(out=o, in0=b, scalar=neglam[:, 0:1], in1=a, op0=ALU.mult, op1=ALU.add), reads=[rf[0], rf[1], rC], writes=[rf[2]])
                P.op("act", lambda e: e.activation(out=a, in_=o, func=AF.Square), reads=[rf[2]], writes=[rf[0]])
                P.op("pe", lambda e: e.matmul(out=pall[:, 4, 0:n], lhsT=onesf[:], rhs=a, start=True, stop=True), reads=[rC, rf[0]], writes=[rP[4]])
                rstd_from(pall[:, 4, 0:n], b, 128, [rP[4]], [rf[1]])
                so = state["ob"] % 2
                state["ob"] += 1
                P.op("dve", lambda e: e.scalar_tensor_tensor(out=ob16[so][:, 0:n], in0=o, scalar=ghead[:, 0:1], in1=b, op0=ALU.mult, op1=ALU.mult), reads=[rf[2], rf[1], rC], writes=[rob16[so]])
                if q0 >= TO:
                    P.op("pool", lambda e: e.tensor_copy(out=ostage[:, q0 - TO:q0 - TO + n], in_=ob16[so][:, 0:n]), reads=[rob16[so]], writes=[rost])
                    if q0 + n == TO + NS:
                        P.dma("pool", "osst", lambda e: e.dma_start(out=oaT_s[:, head, TO:TO + 256], in_=ostage[:]), reads=[rost], writes=[])
                else:
                    P.dma("pool", f"oas{so}", lambda e: e.dma_start(out=oaT_s[:, head, q0:q0 + n], in_=ob16[so][:, 0:n]), reads=[rob16[so]], writes=[])

            def attn_sb(head, q0, n, blocks):
                nblk = len(blocks)
                P.op("pool", lambda e: e.memset(carry[:, 0:n], 0.0), writes=[rcarry])
                sc = 128.0 ** -0.5

                info = {}

                def stageA(bi, kT, vap, nk, mask, rds):
                    bz = 1 + state["sl"] % 2
                    state["sl"] += 1
                    sl = state["s4"] % NSS
                    state["s4"] += 1
                    info[bi] = sl
                    P.op("pe", lambda e: e.matmul(out=pall[0:nk, bz, 0:n], lhsT=kT(0), rhs=QT[:, q0:q0 + n], start=True, stop=True),
                         reads=rds + [rQT], writes=[rP[bz]])
                    P.op("act", lambda e: e.activation(out=Et[sl][0:nk, 0:n], in_=pall[0:nk, bz, 0:n], func=AF.Exp, scale=sc), reads=[rP[bz]], writes=[rEt[sl]])

                def stageA2(bi, kT, vap, nk, mask, rds):
                    sl = info[bi]
                    P.op("act", lambda e: e.activation(out=Lt[sl][0:nk, 0:n], in_=Et[sl][0:nk, 0:n], func=AF.Ln, bias=1.0, scale=1.0), reads=[rEt[sl]], writes=[rLt[sl]])
                    if mask is not None:
                        P.op("pool", lambda e: e.tensor_tensor(out=Lt[sl][0:nk, 0:n], in0=Lt[sl][0:nk, 0:n], in1=mask, op=ALU.mult), reads=[rLt[sl], rmt], writes=[rLt[sl]])

                def stageB(bi, kT, vap, nk, mask, rds):
                    sl = info[bi]
                    slp = state["sp2"] % 2
                    state["sp2"] += 1
                    bc, bs_ = 3 + slp, 5 + slp
                    P.op("pe", lambda e: e.matmul(out=pall[0:nk, bc, 0:n], lhsT=tri[0:nk, 0:nk], rhs=Lt[sl][0:nk, 0:n], start=True, stop=True), reads=[rC, rLt[sl]], writes=[rP[bc]])
                    P.op("pe", lambda e: e.matmul(out=pall[:, bs_, 0:n], lhsT=onesb[0:nk, :], rhs=Lt[sl][0:nk, 0:n], start=True, stop=True), reads=[rC, rLt[sl]], writes=[rP[bs_]])
                    P.op("dve", lambda e: e.tensor_tensor(out=Tt[sl][0:nk, 0:n], in0=pall[0:nk, bc, 0:n], in1=carry[0:nk, 0:n], op=ALU.add), reads=[rP[bc], rcarry], writes=[rTt[sl]])
                    P.op("dve", lambda e: e.tensor_tensor(out=carry[:, 0:n], in0=pall[:, bs_, 0:n], in1=carry[:, 0:n], op=ALU.add), reads=[rP[bs_], rcarry], writes=[rcarry])
                    P.op("act", lambda e: e.activation(out=Tt[sl][0:nk, 0:n], in_=Tt[sl][0:nk, 0:n], func=AF.Exp, scale=-1.0), reads=[rTt[sl]], writes=[rTt[sl]])
                    P.op("pool", lambda e: e.tensor_tensor(out=At[sl][0:nk, 0:n], in0=Et[sl][0:nk, 0:n], in1=Tt[sl][0:nk, 0:n], op=ALU.mult), reads=[rEt[sl], rTt[sl]], writes=[rAt[sl]])
                    if mask is not None:
                        P.op("pool", lambda e: e.tensor_tensor(out=At[sl][0:nk, 0:n], in0=At[sl][0:nk, 0:n], in1=mask, op=ALU.mult), reads=[rAt[sl], rmt], writes=[rAt[sl]])

                def stageC(bi, kT, vap, nk, mask, rds):
                    sl = info[bi]
                    P.op("pe", lambda e: e.matmul(out=pall[:, 0, 0:n], lhsT=vap, rhs=At[sl][0:nk, 0:n], start=(bi == 0), stop=(bi == nblk - 1)),
                         reads=rds + [rAt[sl]], writes=[rP[0]])
                for t in range(nblk + 4):
                    if t < nblk:
                        stageA(t, *blocks[t])
                    if 0 <= t - 2 < nblk:
                        stageB(t - 2, *blocks[t - 2])
                    if t < nblk:
                        stageA2(t, *blocks[t])
                    if 0 <= t - 4 < nblk:
                        stageC(t - 4, *blocks[t - 4])
                so = state["ob"] % 2
                state["ob"] += 1
                P.op("act", lambda e: e.copy(out=ob16[so][:, 0:n], in_=pall[:, 0, 0:n]), reads=[rP[0]], writes=[rob16[so]])
                if q0 >= TO:
                    P.op("pool", lambda e: e.tensor_copy(out=ostage[:, q0 - TO:q0 - TO + n], in_=ob16[so][:, 0:n]), reads=[rob16[so]], writes=[rost])
                    if q0 + n == TO + NS:
                        P.dma("pool", "osst", lambda e: e.dma_start(out=obT_s[:, head, TO:TO + 256], in_=ostage[:]), reads=[rost], writes=[])
                else:
                    P.dma("pool", f"obs{so}", lambda e: e.dma_start(out=obT_s[:, head, q0:q0 + n], in_=ob16[so][:, 0:n]), reads=[rob16[so]], writes=[])

            for typ in (KTYP if KSTOP >= 2 else []):
                md = mdiff_d if typ == 0 else msb_d
                P.dma("sp", "mtl", lambda e, md=md: e.dma_start(out=mt[:], in_=md), writes=[rmt])
                for head in range(8):
                    nw = 5 if typ == 0 else 3
                    base = 5 * head if typ == 0 else 40 + 3 * head
                    for i in range(nw):
                        P.dma("sp", f"whl{i}", lambda e, i=i, base=base: e.dma_start(out=wh[i][:], in_=win_s[base + i, :, :].rearrange("p (a b) -> p a b", b=128)), writes=[rwh[i]])
                    pend = None
                    for c0 in range(0, T, NK):
                        d_ = project_tile(typ, head, hT_b, c0, NK, False)
                        if pend is not None:
                            pend()
                        pend = d_
                    if pend is not None:
                        pend()
                    for c0 in range(0, TOT, NK):
                        project_tile(typ, head, hT_o, c0, min(NK, TOT - c0), True)
                    afn = attn_diff if typ == 0 else attn_sb
                    for j in (range(NJ) if KATT else []):
                        blocks = []
                        for kb in range(16 * j + 16):
                            if typ == 0:
                                kT = (lambda m, kb=kb: KT[m * 64:(m + 1) * 64, kb * 128:(kb + 1) * 128])
                            else:
                                kT = (lambda m, kb=kb: KT[:, kb * 128:(kb + 1) * 128])
                            mask = mt[:, kb - 16 * j, :] if kb >= 16 * j else None
                            blocks.append((kT, V[:, kb, :], 128, mask, [rKT, rV]))
                        if typ == 1:
                            blocks = blocks[::-1]
                        afn(head, j * 512, 512, blocks)
                    ck_d, cv_d = (cdk, cdv) if typ == 0 else (csk, csv)
                    for sbi in (range(SBC) if KSAMP else []):
                        for hf in range(2):
                            P.dma("sp", "ckl", lambda e, sbi=sbi, ck_d=ck_d, head=head, hf=hf: e.dma_start(out=ckf[:], in_=ck_d[sbi, head, :, hf * (PAST // 2):(hf + 1) * (PAST // 2)]), writes=[rckf])
                            P.dma("sp", "cvl", lambda e, sbi=sbi, cv_d=cv_d, head=head, hf=hf: e.dma_start(out=cvf[:], in_=cv_d[sbi, head, :, hf * (NPB // 2):(hf + 1) * (NPB // 2), :]), writes=[rcvf])
                            P.op("dve", lambda e, hf=hf: e.tensor_copy(out=ckb[:, hf * (PAST // 2):(hf + 1) * (PAST // 2)], in_=ckf[:]), reads=[rckf], writes=[rckb])
                            P.op("pool", lambda e, hf=hf: e.tensor_copy(out=cvb[:, hf * (NPB // 2):(hf + 1) * (NPB // 2), :], in_=cvf[:]), reads=[rcvf], writes=[rcvb])
                        blocks = []
                        for kb in range(NPB):
                            if typ == 0:
                                kT = (lambda m, kb=kb: ckb[m * 64:(m + 1) * 64, kb * 128:(kb + 1) * 128])
                            else:
                                kT = (lambda m, kb=kb: ckb[:, kb * 128:(kb + 1) * 128])
                            blocks.append((kT, cvb[:, kb, :], 128, None, [rckb, rcvb]))
                        if typ == 0:
                            kT = (lambda m, sbi=sbi: KTs[m * 64:(m + 1) * 64, sbi * ST:(sbi + 1) * ST])
                            blocks.append((kT, Vs[0:ST, sbi, :], ST, None, [rKTs, rVs]))
                        else:
                            kT = (lambda m, sbi=sbi: KTs[:, sbi * ST:(sbi + 1) * ST])
                            blocks.append((kT, Vs[0:ST, sbi, :], ST, msamp[0:ST, 0:ST], [rKTs, rVs, rC]))
                            blocks = blocks[::-1]
                        afn(head, TO + sbi * ST, ST, blocks)
            P.barrier()

        with contextlib.ExitStack() as s3:
            NQ = 512
            A = sb("A3", [128, KC, NQ], F32, s3)
            B = sb("B3", [128, KC, NQ], F32, s3)
            C = sb("C3", [128, KC, NQ], BF16, s3)
            Fb = sb("F3", [128, 64, NQ], BF16, s3)
            NR = 8
            ring = [sb(f"ring{i}", [128, KC, 128], BF16, s3) for i in range(NR)]
            sga = sb("sga", [128, NQ], F32, s3)
            sgb = sb("sgb", [128, NQ], F32, s3)
            m1 = sb("m1", [128, NQ], F32, s3)
            m2 = sb("m2", [128, NQ], F32, s3)
            sqt = [sb(f"sqt{i}", [128, NQ], F32, s3) for i in range(2)]
            rsd = sb("rsd", [128, NQ], F32, s3)
            tmp3 = [sb(f"tmp3{i}", [128, NQ], F32, s3) for i in range(2)]
            rA, rB, rCC, rFlo, rFhi, rsga, rsgb, rm1, rm2, rrsd = [Res() for _ in range(10)]
            rring = [Res() for _ in range(NR)]
            rsqt = [Res(), Res()]
            rtmp3 = [Res(), Res()]
            st3 = {"r": 0, "q": 0, "t": 0, "pb": 0}

            def wload(src_ap, ncols):
                i = st3["r"] % NR
                st3["r"] += 1
                nch = ncols // 128
                P.dma("sp", f"rg{i}", lambda e: e.dma_start(out=ring[i][:, 0:nch, :], in_=src_ap.rearrange("p (a b) -> p a b", b=128)), writes=[rring[i]])
                return i

            def rF(f):
                return rFlo if f < 32 else rFhi

            def sumsq_accum(src_ap, src_res, n, f, nf):
                q = st3["q"] % 2
                st3["q"] += 1
                P.op("act", lambda e: e.activation(out=sqt[q][:, 0:n], in_=src_ap, func=AF.Square), reads=src_res, writes=[rsqt[q]])
                P.op("pe", lambda e: e.matmul(out=pall[:, 4, 0:n], lhsT=onesf[:], rhs=sqt[q][:, 0:n], start=(f == 0), stop=(f == nf - 1)), reads=[rC, rsqt[q]], writes=[rP[4]])

            def resid_add(n, gi):
                for f in range(KC):
                    t = st3["t"] % 2
                    st3["t"] += 1
                    P.op("dve", lambda e, f=f, t=t: e.scalar_tensor_tensor(out=tmp3[t][:, 0:n], in0=B[:, f, 0:n], scalar=gains[:, gi, f:f + 1], in1=rsd[:, 0:n], op0=ALU.mult, op1=ALU.mult),
                         reads=[rB, rrsd, rC], writes=[rtmp3[t]])
                    P.op("pool", lambda e, f=f, t=t: e.tensor_tensor(out=A[:, f, 0:n], in0=A[:, f, 0:n], in1=tmp3[t][:, 0:n], op=ALU.add), reads=[rA, rtmp3[t]], writes=[rA])

            def tile3(c0, n):
                P.dma("sp", "xa", lambda e, c0=c0, n=n: e.dma_start(out=A[:, :, 0:n], in_=xT_o[:, :, c0:c0 + n]), writes=[rA])
                P.dma("sp", "hc", lambda e, c0=c0, n=n: e.dma_start(out=C[:, :, 0:n], in_=hT_o[:, :, c0:c0 + n]), reads=[rHO], writes=[rCC])
                P.dma("sp", "oab", lambda e, c0=c0, n=n: e.dma_start(out=Fb[:, 0:8, 0:n], in_=oaT_s[:, :, c0:c0 + n]), reads=[rOA], writes=[rFlo])
                P.dma("sp", "oab", lambda e, c0=c0, n=n: e.dma_start(out=Fb[:, 8:16, 0:n], in_=obT_s[:, :, c0:c0 + n]), reads=[rOB], writes=[rFlo])
                for f in range(KC):
                    ia = wload(wgate_s[f, :, :], 2048)
                    ib = wload(wgate_s[16 + f, :, :], 2048)
                    ipa = wload(wpa_s[f, :, :], 1024)
                    ipb = wload(wpb_s[f, :, :], 1024)
                    for kc in range(KC):
                        P.op("pe", lambda e, kc=kc, ia=ia: e.matmul(out=pall[:, 0, 0:n], lhsT=ring[ia][:, kc, :], rhs=C[:, kc, 0:n], start=(kc == 0), stop=(kc == KC - 1)), reads=[rring[ia], rCC], writes=[rP[0]])
                    for kc in range(KC):
                        P.op("pe", lambda e, kc=kc, ib=ib: e.matmul(out=pall[:, 1, 0:n], lhsT=ring[ib][:, kc, :], rhs=C[:, kc, 0:n], start=(kc == 0), stop=(kc == KC - 1)), reads=[rring[ib], rCC], writes=[rP[1]])
                    for h in range(8):
                        P.op("pe", lambda e, h=h, ipa=ipa: e.matmul(out=pall[:, 2, 0:n], lhsT=ring[ipa][:, h, :], rhs=Fb[:, h, 0:n], start=(h == 0), stop=(h == 7)), reads=[rring[ipa], rFlo], writes=[rP[2]])
                    for h in range(8):
                        P.op("pe", lambda e, h=h, ipb=ipb: e.matmul(out=pall[:, 3, 0:n], lhsT=ring[ipb][:, h, :], rhs=Fb[:, 8 + h, 0:n], start=(h == 0), stop=(h == 7)), reads=[rring[ipb], rFlo], writes=[rP[3]])
                    P.op("act", lambda e: e.activation(out=sga[:, 0:n], in_=pall[:, 0, 0:n], func=AF.Sigmoid), reads=[rP[0]], writes=[rsga])
                    P.op("act", lambda e: e.activation(out=sgb[:, 0:n], in_=pall[:, 1, 0:n], func=AF.Sigmoid), reads=[rP[1]], writes=[rsgb])
                    P.op("dve", lambda e: e.tensor_tensor(out=m1[:, 0:n], in0=pall[:, 2, 0:n], in1=sga[:, 0:n], op=ALU.mult), reads=[rP[2], rsga], writes=[rm1])
                    P.op("dve", lambda e: e.tensor_tensor(out=m2[:, 0:n], in0=pall[:, 3, 0:n], in1=sgb[:, 0:n], op=ALU.mult), reads=[rP[3], rsgb], writes=[rm2])
                    P.op("pool", lambda e, f=f: e.tensor_tensor(out=Fb[:, 16 + f, 0:n], in0=m1[:, 0:n], in1=m2[:, 0:n], op=ALU.add), reads=[rm1, rm2], writes=[rFlo])
                for f in range(KC):
                    iw = wload(wout_s[f, :, :], 2048)
                    pb = 5 + st3["pb"] % 2
                    st3["pb"] += 1
                    for kc in range(KC):
                        P.op("pe", lambda e, kc=kc, iw=iw, pb=pb: e.matmul(out=pall[:, pb, 0:n], lhsT=ring[iw][:, kc, :], rhs=Fb[:, 16 + kc, 0:n], start=(kc == 0), stop=(kc == KC - 1)), reads=[rring[iw], rFlo], writes=[rP[pb]])
                    P.op("dve", lambda e, f=f, pb=pb: e.tensor_copy(out=B[:, f, 0:n], in_=pall[:, pb, 0:n]), reads=[rP[pb]], writes=[rB])
                    sumsq_accum(B[:, f, 0:n], [rB], n, f, KC)
                rstd_from(pall[:, 4, 0:n], rsd[:, 0:n], D, [rP[4]], [rrsd])
                resid_add(n, 1)
                for f in range(KC):
                    sumsq_accum(A[:, f, 0:n], [rA], n, f, KC)
                rstd_from(pall[:, 4, 0:n], rsd[:, 0:n], D, [rP[4]], [rrsd])
                for f in range(KC):
                    P.op("dve", lambda e, f=f: e.scalar_tensor_tensor(out=C[:, f, 0:n], in0=A[:, f, 0:n], scalar=gains[:, 2, f:f + 1], in1=rsd[:, 0:n], op0=ALU.mult, op1=ALU.mult), reads=[rA, rrsd, rC], writes=[rCC])
                for f in range(64):
                    iw = wload(wup_s[f, :, :], 2048)
                    pb = 5 + st3["pb"] % 2
                    st3["pb"] += 1
                    for kc in range(KC):
                        P.op("pe", lambda e, kc=kc, iw=iw, pb=pb: e.matmul(out=pall[:, pb, 0:n], lhsT=ring[iw][:, kc, :], rhs=C[:, kc, 0:n], start=(kc == 0), stop=(kc == KC - 1)), reads=[rring[iw], rCC], writes=[rP[pb]])
                    t = st3["t"] % 2
                    st3["t"] += 1
                    P.op("act", lambda e, pb=pb, t=t: e.activation(out=tmp3[t][:, 0:n], in_=pall[:, pb, 0:n], func=AF.Relu), reads=[rP[pb]], writes=[rtmp3[t]])
                    P.op("pool", lambda e, f=f, t=t: e.tensor_tensor(out=Fb[:, f, 0:n], in0=tmp3[t][:, 0:n], in1=tmp3[t][:, 0:n], op=ALU.mult), reads=[rtmp3[t]], writes=[rF(f)])
                for f in range(KC):
                    pb = 5 + st3["pb"] % 2
                    st3["pb"] += 1
                    for part in range(4):
                        iw = wload(wdown_s[f, :, part * 2048:(part + 1) * 2048], 2048)
                        for kc in range(KC):
                            kk = part * 16 + kc
                            P.op("pe", lambda e, kc=kc, kk=kk, iw=iw, pb=pb: e.matmul(out=pall[:, pb, 0:n], lhsT=ring[iw][:, kc, :], rhs=Fb[:, kk, 0:n], start=(kk == 0), stop=(kk == 63)), reads=[rring[iw], rF(kk)], writes=[rP[pb]])
                    P.op("dve", lambda e, f=f, pb=pb: e.tensor_copy(out=B[:, f, 0:n], in_=pall[:, pb, 0:n]), reads=[rP[pb]], writes=[rB])
                    sumsq_accum(B[:, f, 0:n], [rB], n, f, KC)
                rstd_from(pall[:, 4, 0:n], rsd[:, 0:n], D, [rP[4]], [rrsd])
                resid_add(n, 3)
                P.dma("pool", "yst", lambda e, c0=c0, n=n: e.dma_start(out=yT[:, :, c0:c0 + n], in_=A[:, :, 0:n]), reads=[rA])
                out_chans.add("yst")
            for c0 in (range(0, TOT, NQ) if KSTOP >= 3 else []):
                tile3(c0, min(NQ, TOT - c0))
        P.wait_all("sp", sorted(out_chans))
        P.barrier()
        P.emit()
    return nc


def fblocks(W):
    K, N = W.shape
    return np.ascontiguousarray(W.reshape(K // 128, 128, N // 128, 128).transpose(2, 1, 0, 3).reshape(N // 128, 128, K))


def featmajor(x2d):
    n = x2d.shape[0]
    return np.ascontiguousarray(x2d.T.reshape(KC, 128, n).transpose(1, 0, 2))


def rope_tables(pos):
    inv = (np.float32(THETA) ** (-np.arange(32, dtype=np.float32) / np.float32(32))).astype(np.float32)
    ang = pos.astype(np.float32)[:, None] * inv[None, :]
    cos = np.cos(ang).astype(np.float32)
    sin = np.sin(ang).astype(np.float32)
    p = np.arange(128)
    Cc = cos[:, p % 32].T
    Ss = sin[:, p % 32].T * np.where((p % 64) < 32, -1.0, 1.0).astype(np.float32)[:, None]
    return np.ascontiguousarray(Cc, dtype=np.float32), np.ascontiguousarray(Ss, dtype=np.float32)


_CACHE = {}


def kernel(x_prompt, x_sample, cache_diff_k, cache_diff_v, cache_sb_k, cache_sb_v,
           g_pre_mix, w_in, lambda_q1, lambda_k1, lambda_q2, lambda_k2, g_diff_head,
           w_gate, w_proj_a, w_proj_b, w_out, g_post_mix, g_pre_mlp, w_up, w_down, g_post_mlp):
    f32 = np.float32
    x_prompt = np.asarray(x_prompt, f32)
    x_sample = np.asarray(x_sample, f32)
    B_, T, _ = x_prompt.shape
    SBT, ST, _ = x_sample.shape
    PAST = cache_diff_k.shape[2]
    assert B_ == 2 and SBT == 16
    cfg = Cfg(T=T, PAST=PAST, ST=ST, SBC=2)
    key = (T, PAST, ST)
    if key not in _CACHE:
        _CACHE[key] = build(cfg)
    nc = _CACHE[key]
    NJ, TO, TOT, NS, NPB = cfg.NJ, cfg.TO, cfg.TOT, cfg.NS, cfg.NPB

    w_in0 = np.asarray(w_in, f32)[0]
    perm = (np.arange(64) + 32) % 64
    slices = []
    for h in range(8):
        q = np.concatenate([h * 64 + np.arange(64), (8 + h) * 64 + np.arange(64)])
        qp = np.concatenate([h * 64 + perm, (8 + h) * 64 + perm])
        slices += [q, qp, 1024 + q, 1024 + qp, 2048 + h * 128 + np.arange(128)]
    for h in range(8):
        r = h * 128 + np.arange(128)
        slices += [3072 + r, 4096 + r, 5120 + r]
    win_r = np.stack([fblocks(w_in0[:, c])[0] for c in slices])
    wgate_r = fblocks(np.asarray(w_gate, f32)[0])
    wpa_r = fblocks(np.asarray(w_proj_a, f32)[0])
    wpb_r = fblocks(np.asarray(w_proj_b, f32)[0])
    wout_r = fblocks(np.asarray(w_out, f32)[0])
    wup_r = fblocks(np.asarray(w_up, f32)[0])
    wdown_r = fblocks(np.asarray(w_down, f32)[0])

    def gvec(g):
        return np.asarray(g, f32)[0].reshape(KC, 128).T
    gains = np.ascontiguousarray(np.stack([gvec(g_pre_mix), gvec(g_post_mix), gvec(g_pre_mlp), gvec(g_post_mlp)], axis=1))
    ghead = np.ascontiguousarray(np.asarray(g_diff_head, f32)[0].reshape(128, 1))
    lams = np.ascontiguousarray(np.broadcast_to(np.stack([np.asarray(v, f32)[0] for v in (lambda_q1, lambda_k1, lambda_q2, lambda_k2)])[None], (128, 4, 64)))
    tri = (np.arange(128)[:, None] >= np.arange(128)[None, :]).astype(NPBF)
    onesb = np.ones((128, 128), NPBF)
    onesf = np.ones((128, 128), f32)
    identf = np.eye(128, dtype=f32)
    msamp = np.zeros((128, 256), NPBF)
    msamp[:ST, :ST] = (np.arange(ST)[:, None] < np.arange(ST)[None, :]).astype(NPBF)
    ropeC_b, ropeS_b = rope_tables(np.arange(T))
    xTb = [featmajor(x_prompt[b]) for b in range(2)]
    cdk_all = np.asarray(cache_diff_k, f32)[0]
    cdv_all = np.asarray(cache_diff_v, f32)[0]
    csk_all = np.asarray(cache_sb_k, f32)[0]
    csv_all = np.asarray(cache_sb_v, f32)[0]

    in_maps = []
    own_tok = []
    for c in range(8):
        b, qtr = c // 4, c % 4
        toks = np.concatenate([np.arange((4 * j + qtr) * 512, (4 * j + qtr + 1) * 512) for j in range(NJ)])
        own_tok.append(toks)
        xo = np.concatenate([x_prompt[b][toks], x_sample[2 * c], x_sample[2 * c + 1], np.zeros((cfg.NSP - NS, D), f32)], axis=0)
        pos_o = np.concatenate([toks, PAST + np.arange(ST), PAST + np.arange(ST), np.zeros(cfg.NSP - NS, np.int64)])
        rC_o, rS_o = rope_tables(pos_o)
        kpos = (np.arange(4)[:, None, None] * 512 + np.arange(4)[None, :, None] * 128 + np.arange(128)[None, None, :])
        qpos = qtr * 512 + np.arange(512)
        md = (kpos[..., None] < ((qpos // 64 + 1) * 64)[None, None, None, :])
        ms = (kpos[..., None] < qpos[None, None, None, :])
        mdiff = np.ascontiguousarray(md.reshape(16, 128, 512).transpose(1, 0, 2)).astype(NPBF)
        msb = np.ascontiguousarray(ms.reshape(16, 128, 512).transpose(1, 0, 2)).astype(NPBF)
        sb_ids = [2 * c, 2 * c + 1]
        dk = cdk_all[sb_ids]
        cdk_c = np.concatenate([dk[:, :, 0:8, :], dk[:, :, 8:16, :]], axis=-1)
        cdk_c = np.ascontiguousarray(cdk_c.transpose(0, 2, 3, 1))
        csk_c = np.ascontiguousarray(csk_all[sb_ids].transpose(0, 2, 3, 1))
        def vlay(v):
            return np.ascontiguousarray(v.reshape(2, NPB, 128, 8, 128).transpose(0, 3, 2, 1, 4))
        in_maps.append(dict(
            xT_b=xTb[b], xT_o=featmajor(xo), ropeC_b=ropeC_b, ropeS_b=ropeS_b, ropeC_o=rC_o, ropeS_o=rS_o,
            mdiff=mdiff, msb=msb, msamp=msamp, tri=tri, onesb=onesb, onesf=onesf, identf=identf,
            gains=gains, ghead=ghead, lams=lams, win_r=win_r, wgate_r=wgate_r, wpa_r=wpa_r, wpb_r=wpb_r,
            wout_r=wout_r, wup_r=wup_r, wdown_r=wdown_r,
            cdk=cdk_c, cdv=vlay(cdv_all[sb_ids]), csk=csk_c, csv=vlay(csv_all[sb_ids])))

    res = run_bass_kernel_spmd(nc, in_maps, core_ids=list(range(8)))

    y_p = np.zeros((2, T, D), f32)
    y_s = np.zeros((16, ST, D), f32)
    dk_p = np.zeros((1, 2, T, 16, 64), f32)
    dv_p = np.zeros((1, 2, T, 8, 128), f32)
    sk_p = np.zeros((1, 2, T, 8, 128), f32)
    sv_p = np.zeros((1, 2, T, 8, 128), f32)
    dk_s = np.zeros((1, 16, ST, 16, 64), f32)
    dv_s = np.zeros((1, 16, ST, 8, 128), f32)
    sk_s = np.zeros((1, 16, ST, 8, 128), f32)
    sv_s = np.zeros((1, 16, ST, 8, 128), f32)
    for c in range(8):
        r = res.results[c]
        b = c // 4
        toks = own_tok[c]
        y = np.asarray(r["yT"]).transpose(1, 0, 2).reshape(D, TOT).T
        y_p[b, toks] = y[:TO]
        ka = np.asarray(r["kaT_o"]).transpose(2, 0, 1)
        va = np.asarray(r["vaT_o"]).transpose(2, 0, 1)
        kb = np.asarray(r["kbT_o"]).transpose(2, 0, 1)
        vb = np.asarray(r["vbT_o"]).transpose(2, 0, 1)
        ka16 = np.concatenate([ka[:, :, 0:64], ka[:, :, 64:128]], axis=1)
        dk_p[0, b, toks] = ka16[:TO]
        dv_p[0, b, toks] = va[:TO]
        sk_p[0, b, toks] = kb[:TO]
        sv_p[0, b, toks] = vb[:TO]
        for i in range(2):
            sl = slice(TO + i * ST, TO + (i + 1) * ST)
            y_s[2 * c + i] = y[sl]
            dk_s[0, 2 * c + i] = ka16[sl]
            dv_s[0, 2 * c + i] = va[sl]
            sk_s[0, 2 * c + i] = kb[sl]
            sv_s[0, 2 * c + i] = vb[sl]
    return (y_p, y_s, dk_p, dv_p, sk_p, sv_p, dk_s, dv_s, sk_s, sv_s)
```
